# Optimizing a Trainium2 kernel written in Bass

```python
import math
import jax, jax.numpy as jnp
from jax import lax
import numpy as np

D_MODEL = 2048
BATCH = 4
SEQ = 2048
DEPTH = 1
DEC_BATCH = 8
DEC_SEQ = 1
PAST_LEN = 16384
PAGE_SIZE = 128

NSA_HEADS = 16
NSA_KV_HEADS = 4
NSA_HPG = NSA_HEADS // NSA_KV_HEADS
NSA_HEAD_DIM = 64
NSA_Q = NSA_HEADS * NSA_HEAD_DIM
NSA_KV = NSA_KV_HEADS * NSA_HEAD_DIM
CMP_STRIDE = 16
CMP_BLOCK = 32
CMP_HIDDEN = 128
SLC_BLOCK = 64
N_SELECT = 16
WINDOW = 512
NSA_Q_BLOCK = 64
GLA_HEADS = 4
GLA_DK = (D_MODEL // 4) // GLA_HEADS
GLA_DV = (D_MODEL // 2) // GLA_HEADS
GLA_RANK = 16
GLA_TAU = 16.0
GLA_CHUNK = 16
D_FF = 4 * D_MODEL
EPS = 1e-6
NEG = -1e30
FORCE = 1e4
SPLITS = (NSA_Q, 6 * NSA_KV, 3 * NSA_HEADS,
          GLA_HEADS * GLA_DK, GLA_HEADS * GLA_DK, GLA_HEADS * GLA_DV, GLA_HEADS * GLA_DV,
          GLA_RANK, 2 * D_MODEL)
D_IN = sum(SPLITS)

kernel_name = 'nsa_gla_hybrid_decode_step'


def rmsnorm(x, g):
    xf = x.astype(jnp.float32)
    y = xf * lax.rsqrt(jnp.mean(xf * xf, axis=-1, keepdims=True) + EPS)
    return (y * g.astype(jnp.float32)).astype(x.dtype)


def softmax_masked(s, mask):
    p = jax.nn.softmax(jnp.where(mask, s, NEG), axis=-1)
    return jnp.where(mask, p, 0.0)


def compress(k, pe, w1, w2):
    B, L, G, dh = k.shape
    m = CMP_BLOCK // CMP_STRIDE
    n_chunk = L // CMP_STRIDE
    n_cmp = n_chunk - m + 1
    ch = k[:, :n_chunk * CMP_STRIDE].reshape(B, n_chunk, CMP_STRIDE, G, dh)
    blk = jnp.concatenate([ch[:, j:j + n_cmp] for j in range(m)], axis=2)
    blk = blk + pe[None, None, :, None, :].astype(k.dtype)
    blk = blk.transpose(0, 1, 3, 2, 4).reshape(B, n_cmp, G, CMP_BLOCK * dh)
    return jax.nn.gelu(blk @ w1) @ w2


def slc_importance(p_cmp, n_slc):
    r = SLC_BLOCK // CMP_STRIDE
    m = CMP_BLOCK // CMP_STRIDE
    n_cmp = p_cmp.shape[-1]
    pad = [(0, 0)] * (p_cmp.ndim - 1) + [(m - 1, n_slc * r + r - n_cmp)]
    pp = jnp.pad(p_cmp, pad)
    return sum(pp[..., o:o + n_slc * r:r] for o in range(r + m - 1))


def nsa_attend(q, gates, kc, vc, ks, vs, kw, vw, start, w):
    B, T, G, HPG, dh = q.shape
    L = kc.shape[1]
    scale = dh ** -0.5
    kcmp = compress(kc, w['cmp_pe_k'], w['cmp_k_w1'], w['cmp_k_w2'])
    vcmp = compress(vc, w['cmp_pe_v'], w['cmp_v_w1'], w['cmp_v_w2'])
    n_cmp = kcmp.shape[1]
    cmp_end = jnp.arange(n_cmp) * CMP_STRIDE + CMP_BLOCK - 1
    n_slc = -(-L // SLC_BLOCK)
    n_sel = min(N_SELECT, n_slc)
    pad_l = n_slc * SLC_BLOCK - L

    def blocks(a):
        a = jnp.pad(a, ((0, 0), (0, pad_l), (0, 0), (0, 0)))
        return a.reshape(B, n_slc, SLC_BLOCK, G, dh).transpose(0, 3, 1, 2, 4)

    ks_b, vs_b = blocks(ks), blocks(vs)
    qb = min(NSA_Q_BLOCK, T)
    n_qb = -(-T // qb)
    Tp = n_qb * qb
    qpad = ((0, 0), (0, Tp - T), (0, 0), (0, 0), (0, 0))
    q = jnp.pad(q, qpad)
    gates = jnp.pad(gates, qpad)
    kvpad = ((0, 0), (0, Tp - T), (0, 0), (0, 0))
    kw = jnp.pad(kw, kvpad)
    vw = jnp.pad(vw, kvpad)
    b_ix = jnp.arange(B)[:, None, None, None]
    g_ix = jnp.arange(G)[None, None, :, None]
    slc_ids = jnp.arange(n_slc)
    f32 = jnp.float32

    def body(i):
        q_b = lax.dynamic_slice_in_dim(q, i * qb, qb, axis=1)
        g_b = lax.dynamic_slice_in_dim(gates, i * qb, qb, axis=1)
        pos = start + i * qb + jnp.arange(qb)
        s_c = jnp.einsum('bqghd,bngd->bqghn', q_b, kcmp, preferred_element_type=f32) * scale
        m_c = (cmp_end[None, :] <= pos[:, None])[None, :, None, None, :]
        p_c = softmax_masked(s_c, m_c)
        o_c = jnp.einsum('bqghn,bngd->bqghd', p_c.astype(vcmp.dtype), vcmp)
        imp = slc_importance(p_c.sum(axis=3), n_slc)
        valid = (slc_ids[None, :] * SLC_BLOCK <= pos[:, None])
        forced = (slc_ids[None, :] == 0) | (slc_ids[None, :] == pos[:, None] // SLC_BLOCK)
        score = jnp.where(valid[None, :, None, :],
                          imp + jnp.where(forced, FORCE, 0.0)[None, :, None, :], NEG)
        _, idx = lax.top_k(score, n_sel)
        k_sel = ks_b[b_ix, g_ix, idx]
        v_sel = vs_b[b_ix, g_ix, idx]
        kpos = idx[..., None] * SLC_BLOCK + jnp.arange(SLC_BLOCK)
        m_s = (kpos <= pos[None, :, None, None, None])[:, :, :, None]
        s_s = jnp.einsum('bqghd,bqgnkd->bqghnk', q_b, k_sel, preferred_element_type=f32) * scale
        p_s = softmax_masked(s_s.reshape(B, qb, G, HPG, n_sel * SLC_BLOCK),
                             m_s.reshape(B, qb, G, 1, n_sel * SLC_BLOCK)).reshape(s_s.shape)
        o_s = jnp.einsum('bqghnk,bqgnkd->bqghd', p_s.astype(v_sel.dtype), v_sel)
        k_w = lax.dynamic_slice_in_dim(kw, i * qb, WINDOW + qb, axis=1)
        v_w = lax.dynamic_slice_in_dim(vw, i * qb, WINDOW + qb, axis=1)
        wpos = start - WINDOW + i * qb + jnp.arange(WINDOW + qb)
        m_w = ((wpos[None, :] <= pos[:, None]) & (wpos[None, :] >= pos[:, None] - WINDOW)
               & (wpos[None, :] >= 0))[None, :, None, None, :]
        s_w = jnp.einsum('bqghd,bkgd->bqghk', q_b, k_w, preferred_element_type=f32) * scale
        p_w = softmax_masked(s_w, m_w)
        o_w = jnp.einsum('bqghk,bkgd->bqghd', p_w.astype(v_w.dtype), v_w)
        return g_b[..., 0:1] * o_c + g_b[..., 1:2] * o_s + g_b[..., 2:3] * o_w

    out = lax.map(body, jnp.arange(n_qb))
    return out.transpose(1, 0, 2, 3, 4, 5).reshape(B, Tp, G, HPG, dh)[:, :T]


def gla_chunked(q, k, v, g, s0):
    B, T, H, dk = q.shape
    C = min(GLA_CHUNK, T)
    n = -(-T // C)
    Tp = n * C

    def chunks(a):
        a = jnp.pad(a, ((0, 0), (0, Tp - T), (0, 0), (0, 0)))
        return a.reshape(B, n, C, H, a.shape[-1]).transpose(1, 0, 3, 2, 4)

    qc, kc, vc, gc = chunks(q), chunks(k), chunks(v), chunks(g)
    causal = jnp.tril(jnp.ones((C, C), dtype=bool))

    def step(S, inp):
        qi, ki, vi, gi = inp
        b = jnp.cumsum(gi, axis=2)
        b_last = b[:, :, -1:, :]
        q_t = qi * jnp.exp(b)
        k_t = ki * jnp.exp(-b)
        a = jnp.where(causal, jnp.einsum('bhcd,bhsd->bhcs', q_t, k_t), 0.0)
        o = jnp.einsum('bhcd,bhde->bhce', q_t, S) + jnp.einsum('bhcs,bhse->bhce', a, vi)
        S = (jnp.exp(b_last)[:, :, 0, :, None] * S
             + jnp.einsum('bhcd,bhce->bhde', ki * jnp.exp(b_last - b), vi))
        return S, o

    S, o = lax.scan(step, s0, (qc, kc, vc, gc))
    o = o.transpose(1, 0, 3, 2, 4).reshape(B, Tp, H, v.shape[-1])[:, :T]
    return o, S


def layer(x, past, win, s0, w):
    B, T, _ = x.shape
    G, HPG, dh = NSA_KV_HEADS, NSA_HPG, NSA_HEAD_DIM
    f32 = jnp.float32
    start = past[0].shape[1]
    h = rmsnorm(x, w['norm1_g'])
    proj = h @ w['w_in']
    pts = np.cumsum(SPLITS)[:-1].tolist()
    q_n, kv_n, g_n, q_l, k_l, v_l, r_l, lr_l, br = jnp.split(proj, pts, axis=-1)
    kv = kv_n.reshape(B, T, 6, G, dh)
    rows = [kv[:, :, j] for j in range(6)]
    full = [jnp.concatenate([p.astype(x.dtype), r], axis=1) for p, r in zip(past, rows[:4])]
    wext = [jnp.concatenate([jnp.zeros((B, WINDOW - b.shape[1], G, dh), x.dtype), b.astype(x.dtype), r], axis=1)
            for b, r in zip(win, rows[4:])]
    n_keep = min(WINDOW, win[0].shape[1] + T)
    new_win = [a[:, a.shape[1] - n_keep:] for a in wext]
    q = q_n.reshape(B, T, G, HPG, dh)
    gates = jax.nn.sigmoid(g_n + w['b_nsa_gate']).reshape(B, T, G, HPG, 3)
    o_a = nsa_attend(q, gates, full[0], full[1], full[2], full[3], wext[0], wext[1], start, w)
    o_a = o_a.reshape(B, T, NSA_Q)
    gq = q_l.reshape(B, T, GLA_HEADS, GLA_DK).astype(f32) * (GLA_DK ** -0.5)
    gk = k_l.reshape(B, T, GLA_HEADS, GLA_DK).astype(f32)
    gv = v_l.reshape(B, T, GLA_HEADS, GLA_DV).astype(f32)
    log_a = jax.nn.log_sigmoid((lr_l @ w['gla_w_a2'] + w['gla_b_a']).astype(f32)) / GLA_TAU
    log_a = log_a.reshape(B, T, GLA_HEADS, GLA_DK)
    o_g, s_new = gla_chunked(gq, gk, gv, log_a, s0.astype(f32))
    o_g = rmsnorm(o_g, w['gla_norm_g']) * jax.nn.silu(r_l.reshape(B, T, GLA_HEADS, GLA_DV).astype(f32))
    o_b = o_g.reshape(B, T, GLA_HEADS * GLA_DV).astype(x.dtype)
    ga, gb = jnp.split(br, 2, axis=-1)
    mix = (jax.nn.sigmoid(ga) * (o_a @ w['w_br_a']) + jax.nn.sigmoid(gb) * (o_b @ w['w_br_b'])) @ w['w_o']
    x = x + mix
    hf = jax.nn.relu(rmsnorm(x, w['norm2_g']) @ w['w_up'])
    x = x + (hf * hf) @ w['w_down']
    return x, rows[:4], new_win, s_new


def setup_inputs(seed: int = 0) -> dict:
    key = jax.random.key(seed)
    keys = iter(jax.random.split(key, 48))

    def nrm(shape, scale):
        return jax.random.normal(next(keys), shape, jnp.float32) * scale

    n_pages = PAST_LEN // PAGE_SIZE
    n_pool = (DEC_BATCH * n_pages * 5) // 4
    win_buf = min(WINDOW, PAST_LEN)
    kv_shape = (DEPTH, n_pool, PAGE_SIZE, NSA_KV_HEADS, NSA_HEAD_DIM)
    win_shape = (DEPTH, DEC_BATCH, win_buf, NSA_KV_HEADS, NSA_HEAD_DIM)
    x_prompt = nrm((BATCH, SEQ, D_MODEL), 1.0)
    x_sample = nrm((DEC_BATCH, DEC_SEQ, D_MODEL), 1.0)
    cache_cmp_k = nrm(kv_shape, 1.0)
    cache_cmp_v = nrm(kv_shape, 1.0)
    cache_slc_k = nrm(kv_shape, 1.0)
    cache_slc_v = nrm(kv_shape, 1.0)
    cache_win_k = nrm(win_shape, 1.0)
    cache_win_v = nrm(win_shape, 1.0)
    state_gla = nrm((DEPTH, DEC_BATCH, GLA_HEADS, GLA_DK, GLA_DV), 1.0)
    page_table = jax.random.permutation(next(keys), n_pool)[:DEC_BATCH * n_pages]
    page_table = page_table.reshape(DEC_BATCH, n_pages).astype(jnp.int32)
    return {
        'x_prompt': x_prompt,
        'x_sample': x_sample,
        'cache_cmp_k': cache_cmp_k,
        'cache_cmp_v': cache_cmp_v,
        'cache_slc_k': cache_slc_k,
        'cache_slc_v': cache_slc_v,
        'cache_win_k': cache_win_k,
        'cache_win_v': cache_win_v,
        'state_gla': state_gla,
        'page_table': page_table,
        'norm1_g': 1.0 + nrm((DEPTH, D_MODEL), 0.05),
        'w_in': nrm((DEPTH, D_MODEL, D_IN), D_MODEL ** -0.5),
        'b_nsa_gate': nrm((DEPTH, 3 * NSA_HEADS), 0.01),
        'cmp_pe_k': nrm((DEPTH, CMP_BLOCK, NSA_HEAD_DIM), 0.1),
        'cmp_pe_v': nrm((DEPTH, CMP_BLOCK, NSA_HEAD_DIM), 0.1),
        'cmp_k_w1': nrm((DEPTH, CMP_BLOCK * NSA_HEAD_DIM, CMP_HIDDEN), (CMP_BLOCK * NSA_HEAD_DIM) ** -0.5),
        'cmp_k_w2': nrm((DEPTH, CMP_HIDDEN, NSA_HEAD_DIM), CMP_HIDDEN ** -0.5),
        'cmp_v_w1': nrm((DEPTH, CMP_BLOCK * NSA_HEAD_DIM, CMP_HIDDEN), (CMP_BLOCK * NSA_HEAD_DIM) ** -0.5),
        'cmp_v_w2': nrm((DEPTH, CMP_HIDDEN, NSA_HEAD_DIM), CMP_HIDDEN ** -0.5),
        'gla_w_a2': nrm((DEPTH, GLA_RANK, GLA_HEADS * GLA_DK), GLA_RANK ** -0.5),
        'gla_b_a': nrm((DEPTH, GLA_HEADS * GLA_DK), 0.1),
        'gla_norm_g': 1.0 + nrm((DEPTH, GLA_DV), 0.05),
        'w_br_a': nrm((DEPTH, NSA_Q, D_MODEL), NSA_Q ** -0.5),
        'w_br_b': nrm((DEPTH, GLA_HEADS * GLA_DV, D_MODEL), (GLA_HEADS * GLA_DV) ** -0.5),
        'w_o': nrm((DEPTH, D_MODEL, D_MODEL), D_MODEL ** -0.5),
        'norm2_g': 1.0 + nrm((DEPTH, D_MODEL), 0.05),
        'w_up': nrm((DEPTH, D_MODEL, D_FF), D_MODEL ** -0.5),
        'w_down': nrm((DEPTH, D_FF, D_MODEL), D_FF ** -0.5),
        'norm_f': 1.0 + nrm((D_MODEL,), 0.05),
    }


def reference(x_prompt, x_sample, cache_cmp_k, cache_cmp_v, cache_slc_k, cache_slc_v, cache_win_k,
              cache_win_v, state_gla, page_table, norm1_g, w_in, b_nsa_gate, cmp_pe_k, cmp_pe_v,
              cmp_k_w1, cmp_k_w2, cmp_v_w1, cmp_v_w2, gla_w_a2, gla_b_a, gla_norm_g, w_br_a, w_br_b,
              w_o, norm2_g, w_up, w_down, norm_f):
    G, dh = NSA_KV_HEADS, NSA_HEAD_DIM
    xp, xs = x_prompt, x_sample
    bp, db = xp.shape[0], xs.shape[0]
    outs_p = [[] for _ in range(7)]
    outs_s = [[] for _ in range(7)]
    for l in range(DEPTH):
        w = dict(norm1_g=norm1_g[l], w_in=w_in[l], b_nsa_gate=b_nsa_gate[l], cmp_pe_k=cmp_pe_k[l],
                 cmp_pe_v=cmp_pe_v[l], cmp_k_w1=cmp_k_w1[l], cmp_k_w2=cmp_k_w2[l], cmp_v_w1=cmp_v_w1[l],
                 cmp_v_w2=cmp_v_w2[l], gla_w_a2=gla_w_a2[l], gla_b_a=gla_b_a[l], gla_norm_g=gla_norm_g[l],
                 w_br_a=w_br_a[l], w_br_b=w_br_b[l], w_o=w_o[l], norm2_g=norm2_g[l], w_up=w_up[l],
                 w_down=w_down[l])
        empty = jnp.zeros((bp, 0, G, dh), xp.dtype)
        s0_p = jnp.zeros((bp, GLA_HEADS, GLA_DK, GLA_DV), jnp.float32)
        xp, rows_p, win_p, s_p = layer(xp, [empty, empty, empty, empty], [empty, empty], s0_p, w)
        past_s = [c[l][page_table].reshape(db, -1, G, dh)
                  for c in (cache_cmp_k, cache_cmp_v, cache_slc_k, cache_slc_v)]
        xs, rows_s, win_s, s_s = layer(xs, past_s, [cache_win_k[l], cache_win_v[l]], state_gla[l], w)
        for lst, a in zip(outs_p, rows_p + win_p + [s_p]):
            lst.append(a)
        for lst, a in zip(outs_s, rows_s + win_s + [s_s]):
            lst.append(a)
    y_prompt = rmsnorm(xp, norm_f)
    y_sample = rmsnorm(xs, norm_f)
    p_cmp_k, p_cmp_v, p_slc_k, p_slc_v, p_win_k, p_win_v, p_gla = [jnp.stack(a) for a in outs_p]
    s_cmp_k, s_cmp_v, s_slc_k, s_slc_v, s_win_k, s_win_v, s_gla = [jnp.stack(a) for a in outs_s]
    return (y_prompt, y_sample, p_cmp_k, p_cmp_v, p_slc_k, p_slc_v, p_win_k, p_win_v, p_gla,
            s_cmp_k, s_cmp_v, s_slc_k, s_slc_v, s_win_k, s_win_v, s_gla)
```

```python
import numpy as np
from contextlib import ExitStack
import concourse.bass as bass
import concourse.mybir as mybir
from concourse.bass_utils import run_bass_kernel_spmd
from concourse.alu_op_type import AluOpType as ALU

F32 = mybir.dt.float32
BF16 = mybir.dt.bfloat16
I32 = mybir.dt.int32
AF = mybir.ActivationFunctionType
AX = mybir.AxisListType

D = 2048
NPF = 2048
TS = 2048
NPG = 8
SW = 32
NT = NPF + SW
EPS = 1e-6
NEGB = -30000.0

ENGS = ('pe', 'dve', 'act', 'pool', 'sp')
ENG_ATTR = {'pe': 'tensor', 'dve': 'vector', 'act': 'scalar', 'pool': 'gpsimd', 'sp': 'sync'}


class Buf:
    __slots__ = ('name', 'last_w', 'readers')

    def __init__(self, name):
        self.name = name
        self.last_w = None
        self.readers = []


class Op:
    __slots__ = ('eng', 'fn', 'reads', 'writes', 'deps', 'needs_inc', 'sem', 'semval', 'is_dma', 'idx',
                 'extra_waits', 'barrier')

    def __init__(self, eng, fn, reads, writes, is_dma):
        self.eng = eng
        self.fn = fn
        self.reads = reads
        self.writes = writes
        self.deps = []
        self.needs_inc = False
        self.sem = None
        self.semval = 0
        self.is_dma = is_dma
        self.extra_waits = []
        self.barrier = False


class Prog:
    def __init__(self, nc, dma_pool=16):
        self.nc = nc
        self.ops = []
        self.dma_pool = dma_pool

    def buf(self, name):
        return Buf(name)

    def bufs(self, name, n):
        return [Buf(f"{name}{i}") for i in range(n)]

    mute = False

    def op(self, eng, fn, reads=(), writes=()):
        if self.mute:
            return None
        o = Op(eng, fn, tuple(reads), tuple(writes), False)
        self.ops.append(o)
        return o

    def dma(self, eng, out, in_, reads=(), writes=(), **kw):
        if self.mute:
            return None

        def fn(e, out=out, in_=in_, kw=kw):
            return e.dma_start(out=out, in_=in_, **kw)
        o = Op(eng, fn, tuple(reads), tuple(writes), True)
        self.ops.append(o)
        return o

    def dma_fn(self, eng, fn, reads=(), writes=()):
        o = Op(eng, fn, tuple(reads), tuple(writes), True)
        self.ops.append(o)
        return o

    def barrier(self):
        for e in ENGS:
            o = Op(e, lambda eng: eng.nop(), (), (), False)
            o.barrier = True
            self.ops.append(o)

    def finish(self, stack):
        nc = self.nc
        ops = self.ops
        last_on = {e: None for e in ENGS}
        dma_since = []
        for i, o in enumerate(ops):
            o.idx = i
            deps = {}
            if o.barrier:
                for e in ENGS:
                    if last_on[e] is not None:
                        deps[last_on[e].idx] = last_on[e]
                for d in dma_since:
                    deps[d.idx] = d
            for b in o.reads:
                if b.last_w is not None:
                    deps[b.last_w.idx] = b.last_w
            for b in o.writes:
                if b.last_w is not None:
                    deps[b.last_w.idx] = b.last_w
                for r in b.readers:
                    deps[r.idx] = r
            for b in o.reads:
                b.readers.append(o)
            for b in o.writes:
                b.last_w = o
                b.readers = []
            deps.pop(i, None)
            for d in deps.values():
                if d.eng == 'pe' and o.eng == 'pe' and not d.is_dma and not o.is_dma and not o.barrier:
                    continue
                o.deps.append(d)
                d.needs_inc = True
            if o.is_dma:
                o.needs_inc = True
                dma_since.append(o)
            else:
                last_on[o.eng] = o
            if o.barrier and o.eng == ENGS[-1]:
                dma_since = []
        eng_sem = {e: stack.enter_context(nc.semaphore(f"s_{e}")) for e in ENGS}
        eng_cnt = {e: 0 for e in ENGS}
        pools = {e: None for e in ENGS}
        pool_state = {}
        for o in ops:
            if not o.needs_inc:
                continue
            if o.is_dma:
                if pools[o.eng] is None:
                    pools[o.eng] = [stack.enter_context(nc.semaphore(f"d_{o.eng}{k}"))
                                    for k in range(self.dma_pool)]
                    pool_state[o.eng] = [0, [0] * self.dma_pool]
                st = pool_state[o.eng]
                k = st[0] % self.dma_pool
                st[0] += 1
                sem = pools[o.eng][k]
                prev = st[1][k]
                if prev > 0:
                    o.extra_waits.append((sem, prev))
                st[1][k] = prev + 16
                o.sem = sem
                o.semval = prev + 16
            else:
                eng_cnt[o.eng] += 1
                o.sem = eng_sem[o.eng]
                o.semval = eng_cnt[o.eng]
        self.n_waits = 0
        per_eng = {e: [o for o in ops if o.eng == e] for e in ENGS}
        block = stack.enter_context(nc.Block())

        def emit(e, lst):
            waited = {}

            def body(eng):
                for o in lst:
                    ws = [(d.sem, d.semval) for d in o.deps] + o.extra_waits
                    for sem, val in ws:
                        key = id(sem)
                        if waited.get(key, 0) >= val:
                            continue
                        waited[key] = val
                        eng.wait_ge(sem, val)
                        self.n_waits += 1
                    ins = o.fn(eng)
                    if o.needs_inc:
                        ins.then_inc(o.sem, 16 if o.is_dma else 1)
            return body

        for e in ENGS:
            if per_eng[e]:
                getattr(block, ENG_ATTR[e])(emit(e, per_eng[e]))
        return {e: len(per_eng[e]) for e in ENGS}


class K:
    pass


def _consts(hf):
    c = {}
    c['ident_bf'] = np.eye(128, dtype=np.float32)
    c['ident_f'] = np.eye(128, dtype=np.float32)
    s = np.arange(128)
    U01 = (s[:, None] <= s[None, :]).astype(np.float32)
    c['uneg'] = (-U01 / 16.0).astype(np.float32)
    c['u01x4'] = np.tile(U01[:, None, :], (1, 4, 1)).reshape(128, 512).astype(np.float32)
    c['ones_f'] = np.ones((128, 128), np.float32)
    esel = np.zeros((128, 16, 128), np.float32)
    for row in list(range(32)) + list(range(64, 96)):
        j = row % 64
        for kt in range(16):
            key = np.arange(128)
            esel[row, kt, :] = ((kt * 128 + key) // 64 == j)
    c['esel'] = esel.reshape(128, 2048)
    kk = np.arange(128)[:, None]
    qq = np.arange(128)[None, :]
    le = (kk <= qq).astype(np.float32)
    ge = (kk >= qq).astype(np.float32)
    c['tri'] = np.concatenate([le, le, ge, ge], 1).astype(np.float32)
    ii = np.arange(128)[:, None, None]
    qt = np.arange(8)[None, :, None]
    q = np.arange(128)[None, None, :]
    qpos = 1024 + qt * 128 + q
    c['cmk'] = ((16 * ii + 31) <= qpos).astype(np.float32).reshape(128, 1024)
    i2 = np.arange(128)[:, None]
    j2 = np.arange(32)[None, :]
    c['amat'] = ((i2 >= 4 * j2 - 1) & (i2 <= 4 * j2 + 3)).astype(np.float32)
    qpos2 = 1024 + np.arange(8)[None, :, None] * 128 + np.arange(128)[:, None, None]
    jj = np.arange(32)[None, None, :]
    valid = (jj * 64 <= qpos2) & ((hf == 1) | (jj >= 16))
    forced = (jj == (0 if hf == 1 else 16)) | (jj == qpos2 // 64)
    fb = np.where(valid, np.where(forced, 1e4, 0.0), -1e30).astype(np.float32)
    c['fbias'] = fb.reshape(128, 256)
    pkc = np.zeros((128, 4), np.float32)
    if hf == 0:
        pkc[0:64, 0] = -3750.0
        for row in list(range(16)) + list(range(64, 80)):
            pkc[row, 1] = -30000.0
    pkc[:, 2] = 1.0 if hf == 1 else 0.0
    c['pkc'] = pkc
    As = np.zeros((128, 8, 257), np.float32)
    jv = np.arange(257)[None, :]
    for tl in range(8):
        for r in range(128):
            if tl == 3 and r == 127:
                continue
            i = tl * 128 + r if tl < 4 else 511 + (tl - 4) * 128 + r
            As[r, tl, :] = ((i >= 4 * jv - 1) & (i <= 4 * jv + 3))[0]
    c['As'] = As.reshape(128, 8 * 257)
    fbs = np.zeros((128, 257), np.float32)
    fbs[:, 0] = 1e4
    fbs[:, 256] = 1e4
    c['fbs'] = fbs
    rsel = np.zeros((128, 2, 128), np.float32)
    rsel[0, 0, 0:64] = 1.0
    rsel[0, 1, 64:128] = 1.0
    c['rsel'] = rsel.reshape(128, 256)
    gsel = np.zeros((128, 4, 128), np.float32)
    for sl in range(4):
        gsel[0, sl, sl * 32] = 1.0
    c['gsel'] = gsel.reshape(128, 512)
    sel0 = np.zeros((128, 32), np.float32)
    sel0[0::32, 0] = 1.0
    c['sel0'] = sel0
    c['piota'] = np.tile(np.arange(128, dtype=np.float32)[:, None], (1, 128))
    smc = np.zeros((128, 2), np.float32)
    smc[0, 0] = 1.0
    c['smc'] = smc
    return c


def build(dbg=None):
    nc = bass.Bass("TRN2", target_bir_lowering=False)
    k = K()
    k.nc = nc
    k.dbg = dbg
    import os
    k.dbg2 = os.environ.get('DBG2', '')
    k.cut = int(os.environ.get('CUT', '99'))
    k.skip = os.environ.get('SKIP', '').split(',')
    k.nqt = int(os.environ['NQT']) if 'NQT' in os.environ else None
    k.npass = int(os.environ['NPASS']) if 'NPASS' in os.environ else None
    k.nseg = int(os.environ['NSEG']) if 'NSEG' in os.environ else None
    k.npg = int(os.environ['NPG']) if 'NPG' in os.environ else None

    def din(name, shape, dt=F32):
        return nc.dram_tensor(name, list(shape), dt, kind="ExternalInput").ap()

    def dout(name, shape, dt=F32):
        return nc.dram_tensor(name, list(shape), dt, kind="ExternalOutput").ap()

    def dscr(name, shape, dt):
        return nc.dram_tensor(name, list(shape), dt, kind="ExternalOutput").ap()

    xa = din("xa", [NPF, D])
    xs = din("xs", [1, D])
    g1col = din("g1col", [128, 16])
    w_gla_fm = din("w_gla_fm", [D, 1024 + 16])
    w_gla_tm = din("w_gla_tm", [D, 512 + 1024 + 1024])
    w_a2 = din("w_a2", [16, 512])
    b_a = din("b_a", [1, 512])
    gnorm = din("gnorm", [128, 256])
    sg = din("sg", [4, 128, 256])
    ident_bf_d = din("ident_bf", [128, 128])
    uneg_d = din("uneg", [128, 128])
    u01x4_d = din("u01x4", [128, 512])
    ones_d = din("ones_f", [128, 128])
    w_q = din("w_q", [D, 1024])
    w_kT = din("w_kT", [D, 2048])
    w_kv = din("w_kv", [D, 1536])
    w_g = din("w_g", [D, 48])
    bgate_d = din("bgate", [1, 48])
    w1k_d = din("w1k", [128, 4096])
    w1v_d = din("w1v", [128, 4096])
    w2kd_d = din("w2kd", [128, 128])
    w2v_d = din("w2v", [128, 64])
    pek_d = din("pek", [128, 1024])
    pev_d = din("pev", [128, 1024])
    esel_d = din("esel", [128, 2048])
    tri_d = din("tri", [128, 512])
    cmk_d = din("cmk", [128, 1024])
    amat_d = din("amat", [128, 32])
    fbias_d = din("fbias", [128, 256])
    pkc_d = din("pkc", [128, 4])
    w_br = din("w_br", [D, 4096])
    w_bra = din("w_bra", [1024, D])
    w_brb = din("w_brb", [1024, D])
    w_o = din("w_o", [D, D])
    w_up = din("w_up", [D, 8192])
    w_down = din("w_down", [8192, D])
    g2col = din("g2col", [128, 16])
    gfcol = din("gfcol", [128, 16])
    identf_d = din("ident_f", [128, 128])
    mT_d = dscr("mT_d", [16, 128, NT - 1024], BF16)
    o_y = dout("o_y", [1024, D])
    o_ys = dout("o_ys", [SW, D])
    ccK = din("ccK", [163840, 256])
    ccV = din("ccV", [163840, 256])
    csK = din("csK", [163840, 256])
    csV = din("csV", [163840, 256])
    pt_d = din("pt_rep", [128, 128], I32)
    As_d = din("As", [128, 8 * 257])
    fbs_d = din("fbs", [128, 257])
    rsel_d = din("rsel", [128, 256])
    gsel_d = din("gsel", [128, 512])
    sel0_d = din("sel0", [128, 32])
    piota_d = din("piota", [128, 128])
    smc_d = din("smc", [128, 2])
    cwk = din("cwk", [512, 256])
    cwv = din("cwv", [512, 256])
    o_swk = dout("o_swk", [512, 256])
    o_swv = dout("o_swv", [512, 256])
    o_rows = dout("o_rows", [1024, 1536])
    o_rows_s = dout("o_rows_s", [SW, 1536])
    oaT_d = dscr("oaT_d", [8, 128, NT - 1024], BF16)
    o_pgla = dout("o_pgla", [4, 128, 256])
    o_sgla = dout("o_sgla", [4, 128, 256])
    obT_d = dscr("obT_d", [8, 128, NT - 1024], BF16)

    with ExitStack() as top:
        P = Prog(nc)
        k.P = P
        sb = lambda st, name, shape, dt: st.enter_context(nc.sbuf_tensor(name, list(shape), dt))
        ps = [top.enter_context(nc.psum_tensor(f"ps{i}", [128, 512], F32)) for i in range(8)]
        b_ps = P.bufs('ps', 8)
        k.ps_rr = 0

        def nps(lo=0, hi=8):
            i = lo + (k.ps_rr % (hi - lo))
            k.ps_rr += 1
            return ps[i], b_ps[i]
        k.psb_rr = 0

        def npsb():
            i = k.psb_rr % 2
            k.psb_rr += 1
            return psb[:, i * 512:(i + 1) * 512], b_psb[i]

        ident = sb(top, "ident", [128, 128], BF16)
        uneg = sb(top, "uneg_t", [128, 128], F32)
        u01x4 = sb(top, "u01x4_t", [128, 512], F32)
        ones_bf = sb(top, "ones_bf", [128, 128], BF16)
        ones_f = sb(top, "ones_ft", [128, 128], F32)
        g1c = sb(top, "g1c", [128, 16], F32)
        epsc = sb(top, "epsc", [128, 1], F32)
        b_const = P.buf('const')
        P.dma('pool', ident[:], ident_bf_d[:, :], writes=[b_const])
        P.dma('pool', ones_bf[:], ones_d[:, :], writes=[b_const])
        P.dma('sp', ones_f[:], ones_d[:, :], writes=[b_const])
        P.dma('sp', uneg[:], uneg_d[:, :], writes=[b_const])
        P.dma('sp', u01x4[:], u01x4_d[:, :], writes=[b_const])
        P.dma('sp', g1c[:], g1col[:, :], writes=[b_const])
        P.op('pool', lambda e: e.memset(epsc[:], EPS), writes=[b_const])

        NWB = 2
        k.w_rr = 0
        k.wbuf = None
        k.b_wbuf = None

        def alloc_w(st, tag, nelem, nb=2):
            k.nwb = nb
            k.w_rr = 0
            k.wbuf = [sb(st, f"wbuf{tag}{i}", [128, nelem], BF16) for i in range(nb)]
            k.b_wbuf = P.bufs('wbuf' + tag, nb)

        def load_w(src, c0, ncols, nkc=16, row0=0):
            i = k.w_rr % k.nwb
            k.w_rr += 1
            wbuf, b_wbuf = k.wbuf, k.b_wbuf
            wt = wbuf[i][:, 0:nkc * ncols].rearrange("p (k n) -> p k n", n=ncols)
            sv = src[row0:row0 + nkc * 128, c0:c0 + ncols].rearrange("(k p) n -> p k n", p=128)
            P.dma('pool', wt[:, :, :], sv[:, :, :], writes=[b_wbuf[i]])
            return wt, b_wbuf[i]

        with ExitStack() as sB:
            hT = sb(sB, "hT", [128, 16, NT], BF16)
            b_hT = P.bufs('hT', 17)
            tiles = [(t * 128, 128) for t in range(16)] + [(TS, SW)]

            with ExitStack() as s0:
                xt = [sb(s0, f"xt{i}", [128, D], F32) for i in range(3)]
                b_xt = P.bufs('xt', 3)
                junk = sb(s0, "junk", [128, D], BF16)
                b_junk = P.buf('junk')
                xn = [sb(s0, f"xn{i}", [128, D], BF16) for i in range(3)]
                b_xn = P.bufs('xn', 3)
                st_ = sb(s0, "nstat", [128, 17, 2], F32)
                b_st = P.bufs('nst', 17)
                for ti, (t0, m) in enumerate(tiles):
                    if (dbg == 'p0' or 's' in k.dbg2) and ti == 16:
                        continue
                    if 'c' in k.dbg2:
                        continue
                    i = ti % 3
                    if ti < 16:
                        P.dma('sp', xt[i][0:m, :], xa[t0:t0 + m, :], writes=[b_xt[i]])
                    else:
                        P.op('pool', lambda e, i=i, m=m: e.memset(xt[i][0:m, :], 0.0), writes=[b_xt[i]])
                        P.dma('sp', xt[i][0:1, :], xs[0:1, :], writes=[b_xt[i]])
                    if 'b' in k.dbg2:
                        continue
                    P.op('act', lambda e, i=i, m=m, ti=ti: e.activation(
                        out=junk[0:m, :], in_=xt[i][0:m, :], func=AF.Square, accum_out=st_[0:m, ti, 0:1]),
                        reads=[b_xt[i]], writes=[b_junk, b_st[ti]])
                    if 'e' in k.dbg2:
                        continue
                    P.op('act', lambda e, m=m, ti=ti: e.activation(
                        out=st_[0:m, ti, 1:2], in_=st_[0:m, ti, 0:1], func=AF.Sqrt, scale=1.0 / D,
                        bias=epsc[0:m, 0:1]), reads=[b_st[ti], b_const], writes=[b_st[ti]])
                    P.op('dve', lambda e, m=m, ti=ti: e.reciprocal(
                        out=st_[0:m, ti, 1:2], in_=st_[0:m, ti, 1:2]), reads=[b_st[ti]], writes=[b_st[ti]])
                    if 'd' in k.dbg2:
                        continue
                    P.op('dve', lambda e, i=i, m=m, ti=ti: e.tensor_scalar(
                        out=xn[i][0:m, :], in0=xt[i][0:m, :], scalar1=st_[0:m, ti, 1:2], scalar2=None, op0=ALU.mult),
                        reads=[b_xt[i], b_st[ti]], writes=[b_xn[i]])
                    for q4 in range(4 if 'a' not in k.dbg2 else 0):
                        pt_, bpt = nps()
                        for j in range(4):
                            kc = q4 * 4 + j
                            P.op('pe', lambda e, i=i, m=m, kc=kc, j=j, pt_=pt_: e.matmul(
                                pt_[:, j * 128:j * 128 + m], lhsT=xn[i][0:m, kc * 128:(kc + 1) * 128],
                                rhs=ident[0:m, 0:m], start=True, stop=True), reads=[b_xn[i], b_const], writes=[bpt])
                        for j in range(4):
                            kc = q4 * 4 + j
                            eng = 'dve' if j % 2 == 0 else 'act'
                            if eng == 'dve':
                                P.op('dve', lambda e, m=m, kc=kc, j=j, pt_=pt_, t0=t0: e.tensor_scalar(
                                    out=hT[:, kc, t0:t0 + m], in0=pt_[:, j * 128:j * 128 + m],
                                    scalar1=g1c[:, kc:kc + 1], scalar2=None, op0=ALU.mult),
                                    reads=[bpt, b_const], writes=[b_hT[ti]])
                            else:
                                P.op('act', lambda e, m=m, kc=kc, j=j, pt_=pt_, t0=t0: e.activation(
                                    out=hT[:, kc, t0:t0 + m], in_=pt_[:, j * 128:j * 128 + m], func=AF.Copy,
                                    scale=g1c[:, kc:kc + 1]), reads=[bpt, b_const], writes=[b_hT[ti]])
            P.barrier()

            with ExitStack() as s1:
              if dbg != 'p0':
                  alloc_w(s1, 'g', 4096)
                  qT = sb(s1, "g_qT", [128, 4, 1024 + SW], BF16)
                  kT = sb(s1, "g_kT", [128, 4, 1024 + SW], BF16)
                  lrT = sb(s1, "g_lrT", [16, NT], BF16)
                  ktok = sb(s1, "g_ktok", [128, 17, 512], BF16)
                  vtok = sb(s1, "g_vtok", [128, 17, 1024], BF16)
                  rtok = sb(s1, "g_rtok", [128, 9, 1024], BF16)
                  b_qT, b_kT, b_lrT = P.buf('gqT'), P.buf('gkT'), P.buf('glrT')
                  b_ktok, b_vtok, b_rtok = P.bufs('gktok', 17), P.bufs('gvtok', 17), P.bufs('grtok', 9)
                  wa2 = sb(s1, "wa2", [16, 512], BF16)
                  ba = sb(s1, "ba", [1, 512], BF16)
                  gn = sb(s1, "gn", [128, 256], F32)
                  P.dma('pool', wa2[:], w_a2[:, :], writes=[b_const])
                  P.dma('pool', ba[:], b_a[:, :], writes=[b_const])
                  P.dma('sp', gn[:], gnorm[:, :], writes=[b_const])
                  evac_rr = [0]

                  def evac(out_ap, in_ap, reads, writes, mul=None):
                      evac_rr[0] += 1
                      if mul is not None:
                          P.op('act', lambda e: e.activation(out=out_ap, in_=in_ap, func=AF.Copy, scale=mul),
                               reads=reads, writes=writes)
                      elif evac_rr[0] % 2:
                          P.op('act', lambda e: e.activation(out=out_ap, in_=in_ap, func=AF.Copy),
                               reads=reads, writes=writes)
                      else:
                          P.op('dve', lambda e: e.tensor_copy(out=out_ap, in_=in_ap), reads=reads, writes=writes)

                  fm_tok = [(1024, 512), (1536, 512), (TS, SW)]
                  if 's' in k.dbg2:
                      fm_tok = fm_tok[:2]
                  for hq in range(4):
                      wt, bw = load_w(w_gla_fm, hq * 256, 256)
                      dst, bdst = (qT, b_qT) if hq < 2 else (kT, b_kT)
                      for h2 in range(2):
                          h = (hq % 2) * 2 + h2
                          for (t0, n) in fm_tok:
                              pp, bpp = nps()
                              for kc in range(16):
                                  P.op('pe', lambda e, pp=pp, wt=wt, kc=kc, h2=h2, t0=t0, n=n: e.matmul(
                                      pp[:, 0:n], lhsT=wt[:, kc, h2 * 128:(h2 + 1) * 128], rhs=hT[:, kc, t0:t0 + n],
                                      start=(kc == 0), stop=(kc == 15)),
                                      reads=[bw] + b_hT, writes=[bpp])
                              evac(dst[:, h, t0 - 1024:t0 - 1024 + n], pp[:, 0:n], [bpp], [bdst],
                                   mul=(128.0 ** -0.5 if hq < 2 else None))
                  wt, bw = load_w(w_gla_fm, 1024, 16)
                  for (t0, n) in [(0, 512), (512, 512), (1024, 512), (1536, 512), (TS, SW)][:4 if 's' in k.dbg2 else 5]:
                      pp, bpp = nps()
                      for kc in range(16):
                          P.op('pe', lambda e, pp=pp, wt=wt, kc=kc, t0=t0, n=n: e.matmul(
                              pp[0:16, 0:n], lhsT=wt[:, kc, 0:16], rhs=hT[:, kc, t0:t0 + n],
                              start=(kc == 0), stop=(kc == 15)), reads=[bw] + b_hT, writes=[bpp])
                      evac(lrT[:, t0:t0 + n], pp[0:16, 0:n], [bpp], [b_lrT])
                  for ci in range(10):
                      wt, bw = load_w(w_gla_tm, ci * 256, 256)
                      for ti, (t0, m) in enumerate(tiles):
                          if ci >= 6 and ti < 8:
                              continue
                          if 's' in k.dbg2 and ti == 16:
                              continue
                          pp, bpp = nps()
                          for kc in range(16):
                              P.op('pe', lambda e, pp=pp, wt=wt, kc=kc, t0=t0, m=m: e.matmul(
                                  pp[0:m, 0:256], lhsT=hT[:, kc, t0:t0 + m], rhs=wt[:, kc, :],
                                  start=(kc == 0), stop=(kc == 15)), reads=[bw, b_hT[ti]], writes=[bpp])
                          if ci < 2:
                              evac(ktok[0:m, ti, ci * 256:(ci + 1) * 256], pp[0:m, 0:256], [bpp], [b_ktok[ti]])
                          elif ci < 6:
                              evac(vtok[0:m, ti, (ci - 2) * 256:(ci - 1) * 256], pp[0:m, 0:256], [bpp], [b_vtok[ti]])
                          else:
                              evac(rtok[0:m, ti - 8, (ci - 6) * 256:(ci - 5) * 256], pp[0:m, 0:256], [bpp], [b_rtok[ti - 8]])

                  S = sb(s1, "g_S", [128, 4, 256], F32)
                  Sb = sb(s1, "g_Sb", [128, 4, 256], BF16)
                  b_S, b_Sb = P.buf('S'), P.buf('Sb')
                  P.dma('sp', S[:], sg.rearrange("h d e -> d h e"), writes=[b_S])
                  P.op('act', lambda e: e.activation(out=Sb[:], in_=S[:], func=AF.Copy), reads=[b_S], writes=[b_Sb])
                  gpos = sb(s1, "g_gpos", [128, 512], F32)
                  ebn = sb(s1, "g_ebn", [128, 512], F32)
                  ktl = sb(s1, "g_ktl", [128, 512], BF16)
                  ebT = sb(s1, "g_ebT", [128, 512], F32)
                  enbT = sb(s1, "g_enbT", [128, 512], F32)
                  qtl = sb(s1, "g_qtl", [128, 512], BF16)
                  ktlT = sb(s1, "g_ktlT", [128, 512], BF16)
                  aT = sb(s1, "g_aT", [128, 512], BF16)
                  osb = sb(s1, "g_osb", [128, 1024], F32)
                  sr = sb(s1, "g_sr", [128, 1024], BF16)
                  ob = sb(s1, "g_ob", [128, 1024], BF16)
                  gst = sb(s1, "g_st", [128, 8], F32)
                  obT = sb(s1, "g_obT", [128, 8, 128], BF16)
                  one1 = sb(s1, "g_one1", [128, 1], F32)
                  P.op('pool', lambda e: e.memset(one1[:], 1.0), writes=[b_const])
                  (b_gpos, b_ebn, b_ktl, b_ebT, b_enbT, b_qtl, b_ktlT, b_aT, b_osb, b_sr, b_ob, b_gst, b_obT,
                   ) = P.bufs('gl', 13)
                  b_obTd = P.buf('obTd')
                  b_out_gla = P.buf('out_gla')

                  chunk_list = [(16, tiles[16])] + list(enumerate(tiles[:16]))
                  if dbg == 'p1a':
                      chunk_list = []
                  elif dbg == 'p1b':
                      chunk_list = chunk_list[:1]
                  elif dbg == 'p1c':
                      chunk_list = chunk_list[1:2]
                  elif dbg == 'p1d':
                      chunk_list = chunk_list[9:10]
                  for ti, (t0, m) in chunk_list:
                      own = ti >= 8
                      samp = ti == 16
                      cS, cSb, bS, bSb = (S, Sb, b_S, b_Sb)
                      pz, bpz = nps()
                      P.op('pe', lambda e, pz=pz, t0=t0, m=m: e.matmul(
                          pz[0:m, :], lhsT=lrT[0:16, t0:t0 + m], rhs=wa2[0:16, :], start=True, stop=False),
                          reads=[b_lrT, b_const], writes=[bpz])
                      P.op('pe', lambda e, pz=pz, m=m: e.matmul(
                          pz[0:m, :], lhsT=ones_bf[0:1, 0:m], rhs=ba[0:1, :], start=False, stop=True),
                          reads=[b_const], writes=[bpz])
                      P.op('act', lambda e, pz=pz, m=m: e.activation(out=gpos[0:m, :], in_=pz[0:m, :], func=AF.Exp,
                                                                      scale=-1.0), reads=[bpz], writes=[b_gpos])
                      P.op('act', lambda e, m=m: e.activation(out=gpos[0:m, :], in_=gpos[0:m, :], func=AF.Ln,
                                                               bias=one1[0:m, 0:1]), reads=[b_gpos, b_const],
                           writes=[b_gpos])
                      pb_, bpb = nps()
                      P.op('pe', lambda e, pb_=pb_, m=m: e.matmul(pb_[0:m, :], lhsT=uneg[0:m, 0:m], rhs=gpos[0:m, :],
                                                                    start=True, stop=True),
                           reads=[b_gpos, b_const], writes=[bpb])
                      pbT, bpbT = nps()
                      for h in range(4):
                          P.op('pe', lambda e, pbT=pbT, m=m, h=h: e.matmul(
                              pbT[:, h * 128:h * 128 + m], lhsT=gpos[0:m, h * 128:(h + 1) * 128], rhs=uneg[0:m, 0:m],
                              start=(h == 0), stop=(h == 3), skip_group_check=True),
                              reads=[b_gpos, b_const], writes=[bpbT])
                      P.op('act', lambda e, pb_=pb_, m=m: e.activation(out=ebn[0:m, :], in_=pb_[0:m, :], func=AF.Exp,
                                                                        scale=-1.0), reads=[bpb], writes=[b_ebn])
                      P.op('dve', lambda e, m=m, ti=ti: e.tensor_tensor(out=ktl[0:m, :], in0=ktok[0:m, ti, :],
                                                                         in1=ebn[0:m, :], op=ALU.mult),
                           reads=[b_ebn, b_ktok[ti]], writes=[b_ktl])
                      pT3 = pbT[:, :].rearrange("p (h c) -> p h c", c=128)
                      ebT3 = ebT[:, :].rearrange("p (h c) -> p h c", c=128)
                      enbT3 = enbT[:, :].rearrange("p (h c) -> p h c", c=128)
                      P.op('act', lambda e, m=m, pT3=pT3, ebT3=ebT3: e.activation(
                          out=ebT3[:, :, 0:m], in_=pT3[:, :, 0:m], func=AF.Exp), reads=[bpbT], writes=[b_ebT])
                      if own:
                          c0 = t0 - 1024
                          P.mute = k.cut < 1
                          P.op('act', lambda e, m=m, pT3=pT3, enbT3=enbT3: e.activation(
                              out=enbT3[:, :, 0:m], in_=pT3[:, :, 0:m], func=AF.Exp, scale=-1.0),
                              reads=[bpbT], writes=[b_enbT])
                          qtl3 = qtl[:, :].rearrange("p (h c) -> p h c", c=128)
                          ktlT3 = ktlT[:, :].rearrange("p (h c) -> p h c", c=128)
                          P.op('dve', lambda e, m=m, c0=c0, qtl3=qtl3, ebT3=ebT3: e.tensor_tensor(
                              out=qtl3[:, :, 0:m], in0=qT[:, :, c0:c0 + m], in1=ebT3[:, :, 0:m], op=ALU.mult),
                              reads=[b_qT, b_ebT], writes=[b_qtl])
                          P.op('dve', lambda e, m=m, c0=c0, ktlT3=ktlT3, enbT3=enbT3: e.tensor_tensor(
                              out=ktlT3[:, :, 0:m], in0=kT[:, :, c0:c0 + m], in1=enbT3[:, :, 0:m], op=ALU.mult),
                              reads=[b_kT, b_enbT], writes=[b_ktlT])
                          P.mute = k.cut < 2
                          pa, bpa = nps()
                          for h in range(4):
                              P.op('pe', lambda e, pa=pa, m=m, h=h: e.matmul(
                                  pa[0:m, h * 128:h * 128 + m], lhsT=ktlT[:, h * 128:h * 128 + m],
                                  rhs=qtl[:, h * 128:h * 128 + m], start=(h == 0), stop=(h == 3),
                                  skip_group_check=True), reads=[b_qtl, b_ktlT], writes=[bpa])
                          pa3 = pa[:, :].rearrange("p (h c) -> p h c", c=128)
                          aT3 = aT[:, :].rearrange("p (h c) -> p h c", c=128)
                          u3 = u01x4[:, :].rearrange("p (h c) -> p h c", c=128)
                          P.op('dve', lambda e, m=m, pa3=pa3, aT3=aT3, u3=u3: e.tensor_tensor(
                              out=aT3[0:m, :, 0:m], in0=pa3[0:m, :, 0:m], in1=u3[0:m, :, 0:m], op=ALU.mult),
                              reads=[bpa, b_const], writes=[b_aT])
                          P.mute = k.cut < 3
                          po = [nps(), nps()]
                          for h in range(4):
                              pp, bpp = po[h // 2]
                              cc = (h % 2) * 256
                              P.op('pe', lambda e, pp=pp, m=m, h=h, cc=cc, ti=ti: e.matmul(
                                  pp[0:m, cc:cc + 256], lhsT=aT[0:m, h * 128:h * 128 + m],
                                  rhs=vtok[0:m, ti, h * 256:(h + 1) * 256], start=(h % 2 == 0), stop=False,
                                  skip_group_check=True), reads=[b_aT, b_vtok[ti]], writes=[bpp])
                              P.op('pe', lambda e, pp=pp, m=m, h=h, cc=cc, cSb=cSb: e.matmul(
                                  pp[0:m, cc:cc + 256], lhsT=qtl[:, h * 128:h * 128 + m], rhs=cSb[:, h, :],
                                  start=False, stop=True, skip_group_check=True), reads=[b_qtl, bSb], writes=[bpp])
                          P.mute = k.cut < 4
                          for h in range(4):
                              pp, bpp = po[h // 2]
                              cc = (h % 2) * 256
                              P.op('dve', lambda e, pp=pp, m=m, h=h, cc=cc: e.tensor_copy(
                                  out=osb[0:m, h * 256:(h + 1) * 256], in_=pp[0:m, cc:cc + 256]),
                                  reads=[bpp], writes=[b_osb])
                              P.op('act', lambda e, pp=pp, m=m, h=h, cc=cc: e.activation(
                                  out=ob[0:m, h * 256:(h + 1) * 256], in_=osb[0:m, h * 256:(h + 1) * 256], func=AF.Square,
                                  accum_out=gst[0:m, h:h + 1]), reads=[b_osb], writes=[b_ob, b_gst])
                          for h in range(4):
                              P.op('act', lambda e, m=m, h=h: e.activation(
                                  out=gst[0:m, 4 + h:5 + h], in_=gst[0:m, h:h + 1], func=AF.Sqrt, scale=1.0 / 256,
                                  bias=epsc[0:m, 0:1]), reads=[b_gst, b_const], writes=[b_gst])
                          P.op('dve', lambda e, m=m: e.reciprocal(out=gst[0:m, 4:8], in_=gst[0:m, 4:8]),
                               reads=[b_gst], writes=[b_gst])
                          P.mute = k.cut < 5
                          P.op('act', lambda e, m=m, ti=ti: e.activation(out=sr[0:m, :], in_=rtok[0:m, ti - 8, :],
                                                                           func=AF.Silu),
                               reads=[b_rtok[ti - 8], b_sr], writes=[b_sr])
                          P.mute = k.cut < 6
                          for h in range(4):
                              P.op('act', lambda e, m=m, h=h: e.activation(
                                  out=osb[0:m, h * 256:(h + 1) * 256], in_=osb[0:m, h * 256:(h + 1) * 256],
                                  func=AF.Copy, scale=gst[0:m, 4 + h:5 + h]), reads=[b_osb, b_gst], writes=[b_osb])
                              P.op('dve', lambda e, m=m, h=h: e.tensor_tensor(
                                  out=osb[0:m, h * 256:(h + 1) * 256], in0=osb[0:m, h * 256:(h + 1) * 256],
                                  in1=gn[0:m, :], op=ALU.mult), reads=[b_osb, b_const], writes=[b_osb])
                          P.op('dve', lambda e, m=m: e.tensor_tensor(out=ob[0:m, :], in0=osb[0:m, :], in1=sr[0:m, :],
                                                                      op=ALU.mult),
                               reads=[b_osb, b_sr], writes=[b_ob])
                          P.mute = k.cut < 7
                          for q4 in range(2):
                              pt_, bpt = nps()
                              for j in range(4):
                                  fc = q4 * 4 + j
                                  P.op('pe', lambda e, m=m, fc=fc, j=j, pt_=pt_: e.matmul(
                                      pt_[:, j * 128:j * 128 + m], lhsT=ob[0:m, fc * 128:(fc + 1) * 128],
                                      rhs=ident[0:m, 0:m], start=True, stop=True),
                                      reads=[b_ob, b_const], writes=[bpt])
                              ptv = pt_[:, :].rearrange("p (j c) -> p j c", c=128)
                              P.op('act', lambda e, m=m, q4=q4, ptv=ptv: e.activation(
                                  out=obT[:, q4 * 4:(q4 + 1) * 4, 0:m], in_=ptv[:, :, 0:m], func=AF.Copy),
                                  reads=[bpt], writes=[b_obT])
                          P.mute = k.cut < 8
                          P.dma('sp', obT_d[:, :, c0:c0 + m].rearrange("f p c -> p f c"), obT[:, :, 0:m],
                                reads=[b_obT], writes=[b_obTd], allow_slow_non_contiguous=(m < 128))
                      P.mute = k.cut < 0
                      for h in range(4):
                          pk, bpk = nps()
                          P.op('pe', lambda e, pk=pk, m=m, h=h, ti=ti: e.matmul(
                              pk[:, 0:256], lhsT=ktl[0:m, h * 128:(h + 1) * 128],
                              rhs=vtok[0:m, ti, h * 256:(h + 1) * 256], start=True, stop=True),
                              reads=[b_ktl, b_vtok[ti]], writes=[bpk])
                          P.op('dve', lambda e, pk=pk, h=h, cS=cS: e.tensor_tensor(
                              out=cS[:, h, :], in0=pk[:, 0:256], in1=cS[:, h, :], op=ALU.add),
                              reads=[bpk, bS], writes=[bS])
                          li = h * 128 + (0 if samp else m - 1)
                          P.op('dve', lambda e, h=h, cS=cS, li=li: e.tensor_scalar(
                              out=cS[:, h, :], in0=cS[:, h, :], scalar1=ebT[:, li:li + 1],
                              scalar2=None, op0=ALU.mult), reads=[b_ebT, bS], writes=[bS])
                      if samp:
                          P.dma('sp', o_sgla.rearrange("h d e -> d h e"), S[:], reads=[b_S], writes=[b_out_gla])
                          P.op('dve', lambda e: e.memset(S[:], 0.0), reads=[b_S], writes=[b_S])
                          P.op('pool', lambda e: e.memset(Sb[:], 0.0), reads=[b_Sb], writes=[b_Sb])
                      else:
                          P.op('act', lambda e, cS=cS, cSb=cSb: e.activation(out=cSb[:], in_=cS[:], func=AF.Copy),
                               reads=[bS], writes=[bSb])
                  P.dma('sp', o_pgla.rearrange("h d e -> d h e"), S[:], reads=[b_S], writes=[b_out_gla])
            P.barrier()
            def MM(out, lhsT, rhs, start, stop, reads, writes):
                P.op('pe', lambda e: e.matmul(out, lhsT=lhsT, rhs=rhs, start=start, stop=stop,
                                              skip_group_check=True), reads=reads, writes=writes)

            def ACT(out, in_, func, reads, writes, **kw):
                P.op('act', lambda e: e.activation(out=out, in_=in_, func=func, **kw), reads=reads, writes=writes)

            def TT(out, a, b, op, reads, writes):
                P.op('dve', lambda e: e.tensor_tensor(out=out, in0=a, in1=b, op=op), reads=reads, writes=writes)

            def TSC(out, a, s1, s2, op0, op1, reads, writes):
                if op1 is None:
                    P.op('dve', lambda e: e.tensor_scalar(out=out, in0=a, scalar1=s1, scalar2=None, op0=op0),
                         reads=reads, writes=writes)
                else:
                    P.op('dve', lambda e: e.tensor_scalar(out=out, in0=a, scalar1=s1, scalar2=s2, op0=op0, op1=op1),
                         reads=reads, writes=writes)

            def CP(out, in_, reads, writes):
                P.op('dve', lambda e: e.tensor_copy(out=out, in_=in_), reads=reads, writes=writes)

            def RCP(out, in_, reads, writes):
                P.op('dve', lambda e: e.reciprocal(out=out, in_=in_), reads=reads, writes=writes)
            ev2 = [0]

            def EV(out, in_, reads, writes):
                ev2[0] += 1
                if ev2[0] % 2:
                    ACT(out, in_, AF.Copy, reads, writes)
                else:
                    CP(out, in_, reads, writes)

            def transpose_store(src, bsrc, m, nfc, dstT, bdstT, dram, c0, bdram):
                for q4 in range(nfc // 4):
                    pt_, bpt = nps(0, 4)
                    for j in range(4):
                        fc = q4 * 4 + j
                        MM(pt_[:, j * 128:j * 128 + m], src[0:m, fc * 128:(fc + 1) * 128], ident[0:m, 0:m], True, True,
                           [bsrc, b_const], [bpt])
                    ptv = pt_[:, :].rearrange("p (j c) -> p j c", c=128)
                    ACT(dstT[:, q4 * 4:(q4 + 1) * 4, 0:m], ptv[:, :, 0:m], AF.Copy, [bpt], [bdstT])
                P.dma('sp', dram[:, :, c0:c0 + m].rearrange("f p c -> p f c"), dstT[:, :, 0:m],
                      reads=[bdstT], writes=[bdram], allow_slow_non_contiguous=(m < 128))

            b_oaTd = P.buf('oaTd')
            b_rows = P.buf('rows_out')
            with ExitStack() as s2:
                w1 = [sb(s2, f"n_w1{t}", [128, 32, 128], BF16) for t in range(2)]
                w2kd = sb(s2, "n_w2kd", [128, 128], BF16)
                w2v = sb(s2, "n_w2v", [128, 64], BF16)
                cvec = sb(s2, "n_cvec", [128, 2], F32)
                bg = sb(s2, "n_bg", [128, 48], BF16)
                e0 = sb(s2, "n_e0", [128, 128], BF16)
                nqTs = sb(s2, "n_qTs", [128, 8, SW], BF16)
                knew = sb(s2, "n_knew", [128, 8, SW], BF16)
                vnew = sb(s2, "n_vnew", [128, 2, 4, 80], BF16)
                gats = sb(s2, "n_gats", [128, 48], F32)
                smc = sb(s2, "n_smc", [128, 2], F32)
                P.dma('sp', smc[:], smc_d[:, :], writes=[b_const])
                b_samp = P.buf('sampbits')
                s2big = ExitStack()
                alloc_w(s2big, 'n', 4096)
                nqT = sb(s2big, "n_qT", [128, 8, 1024 + SW], BF16)
                ksT = sb(s2big, "n_ksT", [128, 4, NT], BF16)
                kwT = sb(s2big, "n_kwT", [128, 4, NT], BF16)
                vS = sb(s2big, "n_vS", [128, 17, 4, 80], BF16)
                vW = sb(s2big, "n_vW", [128, 17, 4, 80], BF16)
                gat = sb(s2big, "n_gat", [128, 9, 48], F32)
                kcT = sb(s2big, "n_kcT", [128, 4, 128], BF16)
                vc = sb(s2big, "n_vc", [128, 4, 80], BF16)
                b_nqT, b_gat, b_kcT, b_vc, b_cvec = P.bufs('nq', 5)
                b_ksT, b_kwT = P.bufs('nksT', 5), P.bufs('nkwT', 5)
                b_vS, b_vW = P.bufs('nvS', 17), P.bufs('nvW', 17)
                P.dma('pool', w1[0][:], w1k_d.rearrange("p (j m) -> p j m", m=128), writes=[b_const])
                P.dma('pool', w1[1][:], w1v_d.rearrange("p (j m) -> p j m", m=128), writes=[b_const])
                P.dma('pool', w2kd[:], w2kd_d[:, :], writes=[b_const])
                P.dma('pool', w2v[:], w2v_d[:, :], writes=[b_const])
                P.op('pool', lambda e: e.memset(bg[:], 0.0), writes=[b_const])
                P.op('pool', lambda e: e.memset(e0[:], 0.0), writes=[b_const])
                P.dma('pool', bg[0:1, :], bgate_d[:, :], writes=[b_const])
                P.dma('pool', e0[0:1, :], ones_d[0:1, :], writes=[b_const])
                P.op('pool', lambda e: e.memset(vS[:], 1.0), writes=b_vS)
                P.op('pool', lambda e: e.memset(vW[:], 1.0), writes=b_vW)
                P.op('pool', lambda e: e.memset(vc[:], 1.0), writes=[b_vc])
                fm5 = [(0, 512), (512, 512), (1024, 512), (1536, 512), (TS, SW)]
                P.mute = 'q' in k.skip
                for L in range(4):
                    wt, bw = load_w(w_q, L * 256, 256)
                    for c2 in range(2):
                        ch = L * 2 + c2
                        for (t0, n) in fm5[2:]:
                            pp, bpp = nps(0, 4)
                            for kc in range(16):
                                MM(pp[:, 0:n], wt[:, kc, c2 * 128:(c2 + 1) * 128], hT[:, kc, t0:t0 + n], kc == 0, kc == 15,
                                   [bw] + b_hT, [bpp])
                            EV(nqT[:, ch, t0 - 1024:t0 - 1024 + n], pp[:, 0:n], [bpp], [b_nqT])
                P.mute = 'kT' in k.skip
                with ExitStack() as s2c:
                    cT = [sb(s2c, f"n_cT{i}", [128, 16, 128], BF16) for i in range(2)]
                    b_cT = P.bufs('ncT', 2)
                    xg = sb(s2c, "n_xg", [128, 128], F32)
                    x2 = sb(s2c, "n_x2", [128, 128], F32)
                    gel = sb(s2c, "n_gel", [128, 128], BF16)
                    b_xg, b_x2, b_gel = P.bufs('ngel', 3)
                    peT = [sb(s2c, f"n_peT{t}", [128, 32, 32], BF16) for t in range(2)]
                    P.dma('pool', peT[0][:], pek_d.rearrange("p (j r) -> p j r", r=32), writes=[b_const])
                    P.dma('pool', peT[1][:], pev_d.rearrange("p (j r) -> p j r", r=32), writes=[b_const])
                    for t in range(2):
                        pp, bpp = nps(0, 4)
                        for j in range(32):
                            MM(pp[:, 0:32], w1[t][0:64, j, :], peT[t][0:64, j, :], j == 0, j == 31, [b_const], [bpp])
                        CP(cvec[:, t:t + 1], pp[:, 0:1], [bpp], [b_cvec])
                    for L in range(8):
                        wt, bw = load_w(w_kT, L * 256, 256)
                        for c2 in range(2):
                            ch = L * 2 + c2
                            typ, g = ch // 4, ch % 4
                            ci = ch % 2
                            for fi, (t0, n) in enumerate(fm5):
                                pp, bpp = nps(0, 4)
                                for kc in range(16):
                                    MM(pp[:, 0:n], wt[:, kc, c2 * 128:(c2 + 1) * 128], hT[:, kc, t0:t0 + n], kc == 0,
                                       kc == 15, [bw] + b_hT, [bpp])
                                if typ < 2:
                                    if fi < 4:
                                        EV(cT[ci][:, :, t0 // 16:(t0 + n) // 16].rearrange("p s c -> p c s"),
                                           pp[:, 0:n].rearrange("p (c s) -> p c s", s=16), [bpp], [b_cT[ci]])
                                elif typ == 2:
                                    EV(ksT[:, g, t0:t0 + n], pp[:, 0:n], [bpp], [b_ksT[fi]])
                                else:
                                    EV(kwT[:, g, t0:t0 + n], pp[:, 0:n], [bpp], [b_kwT[fi]])
                            if typ < 2 and 'cmp' not in k.skip:
                                pp, bpp = nps(0, 4)
                                for j in range(32):
                                    MM(pp[:, 0:127], w1[typ][0:64, j, :], cT[ci][0:64, j % 16, (j // 16):(j // 16) + 127],
                                       j == 0, j == 31, [b_const, b_cT[ci]], [bpp])
                                TSC(xg[:, 0:127], pp[:, 0:127], cvec[:, typ:typ + 1], None, ALU.add, None, [bpp, b_cvec], [b_xg])
                                TT(x2[:, 0:127], xg[:, 0:127], xg[:, 0:127], ALU.mult, [b_xg], [b_x2])
                                TSC(x2[:, 0:127], x2[:, 0:127], 0.044715, 1.0, ALU.mult, ALU.add, [b_x2], [b_x2])
                                TT(x2[:, 0:127], x2[:, 0:127], xg[:, 0:127], ALU.mult, [b_x2, b_xg], [b_x2])
                                ACT(x2[:, 0:127], x2[:, 0:127], AF.Exp, [b_x2], [b_x2], scale=-1.5957691216057308)
                                TSC(x2[:, 0:127], x2[:, 0:127], 1.0, None, ALU.add, None, [b_x2], [b_x2])
                                RCP(x2[:, 0:127], x2[:, 0:127], [b_x2], [b_x2])
                                TT(gel[:, 0:127], x2[:, 0:127], xg[:, 0:127], ALU.mult, [b_x2, b_xg], [b_gel])
                                pp, bpp = nps(0, 4)
                                if typ == 0:
                                    MM(pp[:, 0:127], w2kd[:, :], gel[:, 0:127], True, True, [b_const, b_gel], [bpp])
                                    EV(kcT[:, g, 0:127], pp[:, 0:127], [bpp], [b_kcT])
                                else:
                                    MM(pp[0:127, 0:64], gel[:, 0:127], w2v[:, :], True, True, [b_const, b_gel], [bpp])
                                    EV(vc[0:127, g, 0:64], pp[0:127, 0:64], [bpp], [b_vc])
                P.mute = False
                P.barrier()
                P.mute = 'rows' in k.skip
                with ExitStack() as s2r:
                    rows = [sb(s2r, f"n_rows{i}", [128, 256], F32) for i in range(3)]
                    b_rw = P.bufs('nrows', 3)
                    cwt = [sb(s2r, f"n_cwt{i}", [128, 4, 256], F32) for i in range(2)]
                    b_cwt = P.bufs('ncwt', 2)
                    for i_, (src_, dst_) in enumerate(((cwk, o_swk), (cwv, o_swv))):
                        P.dma('sp', cwt[i_][:, :, :], src_.rearrange("(t p) n -> p t n", p=128), writes=[b_cwt[i_]])
                        P.dma('sp', dst_[0:127, :], cwt[i_][1:128, 0, :], reads=[b_cwt[i_]], writes=[b_rows])
                        for t_ in range(1, 4):
                            P.dma('sp', dst_[t_ * 128 - 1:t_ * 128 + 127, :], cwt[i_][:, t_, :], reads=[b_cwt[i_]],
                                  writes=[b_rows])
                    rr = 0
                    for L in range(6):
                        wt, bw = load_w(w_kv, L * 256, 256)
                        for ti, (t0, m) in enumerate(tiles):
                            if ti < 8 and L not in (3, 5):
                                continue
                            pp, bpp = nps(0, 4)
                            for kc in range(16):
                                MM(pp[0:m, 0:256], hT[:, kc, t0:t0 + m], wt[:, kc, :], kc == 0, kc == 15, [bw, b_hT[ti]], [bpp])
                            if L in (3, 5):
                                dstv, bdv = (vS, b_vS) if L == 3 else (vW, b_vW)
                                for g_ in range(4):
                                    EV(dstv[0:m, ti, g_, 0:64], pp[0:m, g_ * 64:(g_ + 1) * 64], [bpp], [bdv[ti]])
                            if ti >= 8:
                                i = rr % 3
                                rr += 1
                                CP(rows[i][0:m, :], pp[0:m, 0:256], [bpp], [b_rw[i]])
                                if ti < 16:
                                    P.dma('sp', o_rows[t0 - 1024:t0 - 1024 + m, L * 256:(L + 1) * 256], rows[i][0:m, :],
                                          reads=[b_rw[i]], writes=[b_rows])
                                else:
                                    P.dma('sp', o_rows_s[0:m, L * 256:(L + 1) * 256], rows[i][0:m, :],
                                          reads=[b_rw[i]], writes=[b_rows])
                                    if L >= 4:
                                        P.dma('sp', (o_swk if L == 4 else o_swv)[511:512, :], rows[i][0:1, :],
                                              reads=[b_rw[i]], writes=[b_rows])
                P.mute = False
                P.barrier()
                P.mute = 'gates' in k.skip
                wt, bw = load_w(w_g, 0, 48)
                for ti, (t0, m) in enumerate(tiles):
                    if ti < 8:
                        continue
                    pp, bpp = nps(0, 4)
                    for kc in range(16):
                        MM(pp[0:m, 0:48], hT[:, kc, t0:t0 + m], wt[:, kc, :], kc == 0, False, [bw, b_hT[ti]], [bpp])
                    MM(pp[0:m, 0:48], e0[:, 0:m], bg[:, :], False, True, [b_const], [bpp])
                    ACT(gat[0:m, ti - 8, :], pp[0:m, 0:48], AF.Sigmoid, [bpp], [b_gat])

                P.mute = False
                oacc = sb(s2big, "n_oacc", [128, 1024], F32)
                otmp = sb(s2big, "n_otmp", [128, 64], F32)
                oab = sb(s2big, "n_oab", [128, 1024], BF16)
                oaT = sb(s2big, "n_oaT", [128, 8, 128], BF16)
                dn = sb(s2big, "n_dn", [128, 8], F32)
                pT = [[sb(s2big, f"n_pT{i}{h}", [128, 256], BF16) for h in range(2)] for i in range(2)]
                b_pT = [P.bufs(f'npT{i}', 2) for i in range(2)]
                b_oacc, b_otmp, b_oab, b_oaT, b_dn = P.bufs('noa', 5)
                pt_rr = [0]
                gat3 = gat[:, :, :].rearrange("p t (h b) -> p t h b", b=3)

                def finalize(accs, nq, g, tix, br, first):
                    for s_ in range(4):
                        pa_, bpa_ = accs[s_]
                        TSC(dn[0:nq, s_:s_ + 1], pa_[0:nq, 64:65], 1e-30, None, ALU.max, None, [bpa_], [b_dn])
                    RCP(dn[0:nq, 0:4], dn[0:nq, 0:4], [b_dn], [b_dn])
                    TT(dn[0:nq, 4:8], dn[0:nq, 0:4], gat3[0:nq, tix, 4 * g:4 * g + 4, br], ALU.mult, [b_dn, b_gat], [b_dn])
                    for s_ in range(4):
                        pa_, bpa_ = accs[s_]
                        col = (4 * g + s_) * 64
                        if first:
                            ACT(oacc[0:nq, col:col + 64], pa_[0:nq, 0:64], AF.Copy, [bpa_, b_dn], [b_oacc],
                                scale=dn[0:nq, 4 + s_:5 + s_])
                        else:
                            ACT(otmp[0:nq, :], pa_[0:nq, 0:64], AF.Copy, [bpa_, b_dn], [b_otmp],
                                scale=dn[0:nq, 4 + s_:5 + s_])
                            TT(oacc[0:nq, col:col + 64], oacc[0:nq, col:col + 64], otmp[0:nq, :], ALU.add,
                               [b_oacc, b_otmp], [b_oacc])

                def pv_tile(accs_, outs, V, bV, kt, g, first, last):
                    for s_ in range(4):
                        _, _, pt2, bpt2 = outs[s_ // 2]
                        MM(accs_[s_][0][0:128, 0:65], pt2[:, (s_ % 2) * 128:(s_ % 2 + 1) * 128], V[:, kt, g, 0:65],
                           first, last, [bpt2, bV[kt]], [accs_[s_][1]])

                def qk_tile(g, qc0, nq, KT, nk, bKT, bias=None):
                    i = pt_rr[0] % 2
                    pt_rr[0] += 1
                    outs = []
                    for h in range(2):
                        pp, bpp = ps[h * 2 + i], b_ps[h * 2 + i]
                        rhs = nqT[h * 64:(h + 1) * 64, 2 * g:2 * g + 2, qc0:qc0 + nq]
                        if bias is not None:
                            bl, br_, bb = bias
                            MM(pp[0:nk, 0:2 * nq], bl[h * 64:(h + 1) * 64, :], br_[h * 64:(h + 1) * 64, :], True, False, bb, [bpp])
                        MM(pp[0:nk, 0:2 * nq], KT[h * 64:(h + 1) * 64, :], rhs, bias is None, True, [b_nqT] + bKT, [bpp])
                        outs.append((pp, bpp, pT[i][h], b_pT[i][h]))
                    return outs

                if 'nsap' not in k.skip:
                  with ExitStack() as s2p:
                    esel = sb(s2p, "n_esel", [128, 16, 128], BF16)
                    tri = sb(s2p, "n_tri", [128, 2, 256], BF16)
                    cmk = sb(s2p, "n_cmk", [128, 8, 128], F32)
                    amat = sb(s2p, "n_amat", [128, 32], F32)
                    fbias = sb(s2p, "n_fbias", [128, 8, 32], F32)
                    pkc = sb(s2p, "n_pkc", [128, 4], F32)
                    ef = sb(s2p, "n_ef", [128, 512], F32)
                    eb = sb(s2p, "n_eb", [128, 512], BF16)
                    rden = sb(s2p, "n_rden", [128, 512], F32)
                    sc = sb(s2p, "n_sc", [128, 32], F32)
                    sc2 = sb(s2p, "n_sc2", [128, 32], F32)
                    m8 = sb(s2p, "n_m8", [128, 16], F32)
                    selbb = sb(s2p, "n_selbb", [128, 128], BF16)
                    selbT = sb(s2p, "n_selbT", [128, 256], BF16)
                    b_ef, b_eb, b_rden, b_sc, b_sc2, b_m8, b_selbb, b_selbT = P.bufs('npa', 8)
                    P.dma('pool', esel[:], esel_d.rearrange("p (t k) -> p t k", k=128), writes=[b_const])
                    P.dma('pool', tri[:], tri_d.rearrange("p (t k) -> p t k", k=256), writes=[b_const])
                    P.dma('sp', cmk[:], cmk_d.rearrange("p (t k) -> p t k", k=128), writes=[b_const])
                    P.dma('sp', amat[:], amat_d[:, :], writes=[b_const])
                    P.dma('sp', fbias[:], fbias_d.rearrange("p (t k) -> p t k", k=32), writes=[b_const])
                    P.dma('sp', pkc[:], pkc_d[:, :], writes=[b_const])
                    P.op('pool', lambda e: e.memset(selbb[:], 0.0), writes=[b_selbb])
                    accs = [(ps[4 + s_], b_ps[4 + s_]) for s_ in range(4)]
                    for qt in range(8 if k.nqt is None else k.nqt):
                        qc0 = qt * 128
                        for g in range(4):
                            outs = qk_tile(g, qc0, 128, kcT[:, g, 0:127], 127, [b_kcT])
                            for h, (pp, bpp, _, _) in enumerate(outs):
                                ACT(ef[0:127, h * 256:(h + 1) * 256], pp[0:127, 0:256], AF.Exp, [bpp, b_const], [b_ef],
                                    scale=0.125, bias=pkc[0:127, 0:1])
                            ef3_ = ef[0:127, :].rearrange("p (s q) -> p s q", q=128)
                            TT(ef3_, ef3_, cmk[0:127, qt:qt + 1, :].broadcast_to([127, 4, 128]), ALU.mult, [b_ef, b_const], [b_ef])
                            CP(eb[0:127, :], ef[0:127, :], [b_ef], [b_eb])
                            pd, bpd = nps(0, 4)
                            MM(pd[:, 0:512], ones_bf[0:127, 0:128], eb[0:127, :], True, True, [b_const, b_eb], [bpd])
                            TSC(rden[0:127, :], pd[0:127, :], 1e-30, None, ALU.max, None, [bpd], [b_rden])
                            RCP(rden[0:127, :], rden[0:127, :], [b_rden], [b_rden])
                            TT(ef[0:127, :], ef[0:127, :], rden[0:127, :], ALU.mult, [b_ef, b_rden], [b_ef])
                            pi_, bpi = nps(0, 4)
                            for s_ in range(4):
                                MM(pi_[:, 0:32], ef[0:127, s_ * 128:(s_ + 1) * 128], amat[0:127, :], s_ == 0, s_ == 3,
                                   [b_ef, b_const], [bpi])
                            TT(sc[:, :], pi_[:, 0:32], fbias[:, qt, :], ALU.add, [bpi, b_const], [b_sc])
                            P.op('dve', lambda e: e.max(out=m8[:, 0:8], in_=sc[:, :]), reads=[b_sc], writes=[b_m8])
                            P.op('dve', lambda e: e.match_replace(out=sc2[:, :], in_to_replace=m8[:, 0:8], in_values=sc[:, :],
                                                                  imm_value=-3e38), reads=[b_sc, b_m8], writes=[b_sc2])
                            P.op('dve', lambda e: e.max(out=m8[:, 8:16], in_=sc2[:, :]), reads=[b_sc2], writes=[b_m8])
                            TSC(sc2[:, :], sc[:, :], m8[:, 15:16], None, ALU.is_ge, None, [b_sc, b_m8], [b_sc2])
                            TSC(selbb[:, 0:32], sc2[:, :], -1.0, 30000.0, ALU.add, ALU.mult, [b_sc2], [b_selbb])
                            TSC(selbb[:, 64:96], sc2[:, :], -1.0, 30000.0, ALU.add, ALU.mult, [b_sc2], [b_selbb])
                            pt_, bpt = nps(0, 4)
                            MM(pt_[:, 0:128], selbb[:, :], ident[:, :], True, True, [b_selbb, b_const], [bpt])
                            for r_ in range(2):
                                TSC(selbT[:, r_ * 128:(r_ + 1) * 128], pt_[:, 0:128], pkc[:, 1:2], None, ALU.add, None,
                                   [bpt, b_const], [b_selbT])
                            for s_ in range(4):
                                MM(accs[s_][0][0:128, 0:65], eb[0:127, s_ * 128:(s_ + 1) * 128], vc[0:127, g, 0:65], True, True,
                                   [b_eb, b_vc], [accs[s_][1]])
                            finalize(accs, 128, g, qt, 0, True)
                            pend = None
                            for ix, kt in enumerate(range(qt + 4, qt + 9)):
                                outs = qk_tile(g, qc0, 128, kwT[:, g, kt * 128:(kt + 1) * 128], 128, [b_kwT[kt // 4]])
                                for h, (pp, bpp, pt2, bpt2) in enumerate(outs):
                                    ACT(pt2[:, :], pp[:, 0:256], AF.Exp, [bpp], [bpt2], scale=0.125)
                                    if ix == 0:
                                        TT(pt2[:, :], pt2[:, :], tri[:, 1, :], ALU.mult, [bpt2, b_const], [bpt2])
                                    if ix == 4:
                                        TT(pt2[:, :], pt2[:, :], tri[:, 0, :], ALU.mult, [bpt2, b_const], [bpt2])
                                    if kt < 8:
                                        TSC(pt2[:, :], pt2[:, :], pkc[:, 2:3], None, ALU.mult, None, [bpt2, b_const], [bpt2])
                                if pend is not None:
                                    pv_tile(accs, pend[0], vW, b_vW, pend[1], g, pend[2] == 0, False)
                                pend = (outs, kt, ix)
                            pv_tile(accs, pend[0], vW, b_vW, pend[1], g, False, True)
                            finalize(accs, 128, g, qt, 2, False)
                            nkt = 9 + qt
                            pend = None
                            for kt in range(nkt):
                                outs = qk_tile(g, qc0, 128, ksT[:, g, kt * 128:(kt + 1) * 128], 128, [b_ksT[kt // 4]],
                                               bias=(esel[:, kt, :], selbT[:, :], [b_const, b_selbT]))
                                for h, (pp, bpp, pt2, bpt2) in enumerate(outs):
                                    ACT(pt2[:, :], pp[:, 0:256], AF.Exp, [bpp], [bpt2], scale=0.125)
                                    if kt == nkt - 1:
                                        TT(pt2[:, :], pt2[:, :], tri[:, 0, :], ALU.mult, [bpt2, b_const], [bpt2])
                                if pend is not None:
                                    pv_tile(accs, pend[0], vS, b_vS, pend[1], g, pend[1] == 0, False)
                                pend = (outs, kt)
                            pv_tile(accs, pend[0], vS, b_vS, pend[1], g, nkt == 1, True)
                            finalize(accs, 128, g, qt, 1, False)
                        CP(oab[:, :], oacc[:, :], [b_oacc], [b_oab])
                        transpose_store(oab, b_oab, 128, 8, oaT, b_oaT, oaT_d, qc0, b_oaTd)
                P.mute = False
                CP(nqTs[:, :, :], nqT[:, :, 1024:1024 + SW], [b_nqT], [b_samp])
                for g_ in range(4):
                    CP(knew[:, g_, :], ksT[:, g_, TS:TS + SW], [b_ksT[4]], [b_samp])
                    CP(knew[:, 4 + g_, :], kwT[:, g_, TS:TS + SW], [b_kwT[4]], [b_samp])
                    TSC(vnew[0:SW, 0, g_, 0:65], vS[0:SW, 16, g_, 0:65], smc[0:SW, 0:1], None, ALU.mult, None,
                        [b_vS[16], b_const], [b_samp])
                    TSC(vnew[0:SW, 1, g_, 0:65], vW[0:SW, 16, g_, 0:65], smc[0:SW, 0:1], None, ALU.mult, None,
                        [b_vW[16], b_const], [b_samp])
                CP(gats[0:SW, :], gat[0:SW, 8, :], [b_gat], [b_samp])
                P.barrier()
                s2big.close()
                if 'nsas' not in k.skip:
                  with ExitStack() as s2s:
                    As = sb(s2s, "s_As", [128, 8, 257], F32)
                    fbs = sb(s2s, "s_fbs", [128, 257], F32)
                    rsel = sb(s2s, "s_rsel", [128, 2, 128], F32)
                    gsel = sb(s2s, "s_gsel", [128, 4, 128], F32)
                    sel0 = sb(s2s, "s_sel0", [128, 32], F32)
                    piota = sb(s2s, "s_piota", [128, 128], F32)
                    pti = sb(s2s, "s_pti", [128, 128], I32)
                    ptf = sb(s2s, "s_ptf", [128, 128], F32)
                    idx = sb(s2s, "s_idx", [128, 128], I32)
                    b_idx = P.buf('sidx')
                    P.dma('sp', As[:], As_d.rearrange("p (s j) -> p s j", j=257), writes=[b_const])
                    P.dma('sp', fbs[:], fbs_d[:, :], writes=[b_const])
                    P.dma('sp', rsel[:], rsel_d.rearrange("p (r k) -> p r k", k=128), writes=[b_const])
                    P.dma('sp', gsel[:], gsel_d.rearrange("p (r k) -> p r k", k=128), writes=[b_const])
                    P.dma('sp', sel0[:], sel0_d[:, :], writes=[b_const])
                    P.dma('sp', piota[:], piota_d[:, :], writes=[b_const])
                    P.dma('sp', pti[:], pt_d[:, :], writes=[b_idx])
                    CP(ptf[:, :], pti[:, :], [b_idx], [b_idx])
                    TSC(ptf[:, :], ptf[:, :], 128.0, None, ALU.mult, None, [b_idx], [b_idx])
                    TT(ptf[:, :], ptf[:, :], piota[:, :], ALU.add, [b_idx, b_const], [b_idx])
                    CP(idx[:, :], ptf[:, :], [b_idx], [b_idx])
                    segb = sb(s2s, "s_seg", [128, 2, 16, 513], BF16)
                    b_seg = P.buf('sseg')
                    halo = [sb(s2s, f"s_halo{t}", [128, 2, 16, 1], BF16) for t in range(2)]
                    b_halo = P.bufs('shalo', 2)
                    pg = [sb(s2s, f"s_pg{i}", [128, 256], F32) for i in range(NPG)]
                    b_pg = P.bufs('spg', NPG)
                    pgb = [sb(s2s, f"s_pgb{i}", [128, 256], BF16) for i in range(4)]
                    b_pgb = P.bufs('spgb', 4)
                    pgb_rr = [0]
                    kcTs = sb(s2s, "s_kcTs", [128, 4, 1024], BF16)
                    vcs = sb(s2s, "s_vcs", [128, 8, 4, 80], BF16)
                    b_kcTs, b_vcs = P.buf('skcTs'), P.buf('svcs')
                    sxg = sb(s2s, "s_xg", [128, 512], F32)
                    sx2 = sb(s2s, "s_x2", [128, 512], F32)
                    sgel = sb(s2s, "s_gel", [128, 512], BF16)
                    b_sxg, b_sx2, b_sgel = P.bufs('sgel', 3)
                    P.op('pool', lambda e: e.memset(vcs[:], 1.0), writes=[b_vcs])
                    pg_rr = [0]

                    def gather(cache, page):
                        i = pg_rr[0] % NPG
                        pg_rr[0] += 1

                        def fn(e, i=i, page=page, cache=cache):
                            return e.indirect_dma_start(out=pg[i][:, :], out_offset=None, in_=cache[:, :],
                                                        in_offset=bass.IndirectOffsetOnAxis(ap=idx[:, page:page + 1], axis=0))
                        P.dma_fn('pool', fn, reads=[b_idx], writes=[b_pg[i]])
                        return pg[i], b_pg[i]

                    nseg = 8
                    tile_nk = [128, 128, 128, 127, 128, 128, 128, 128]
                    for sg_ in range(2):
                        b0, nb = (1, 511) if sg_ == 0 else (0, 512)
                        for t in range(2):
                            if sg_ > 0:
                                CP(segb[:, :, :, 0:1], halo[t][:, :, :, :], [b_halo[t]], [b_seg])
                            for pgi in range(64):
                                page = sg_ * 64 + pgi
                                src, bsrc = gather(ccK if t == 0 else ccV, page)
                                j = pgb_rr[0] % 4
                                pgb_rr[0] += 1
                                EV(pgb[j][:, :], src[:, :], [bsrc], [b_pgb[j]])
                                pt_, bpt = nps(0, 4)
                                for pr in range(2):
                                    MM(pt_[:, pr * 128:(pr + 1) * 128], pgb[j][:, pr * 128:(pr + 1) * 128], ident[:, :], True, True,
                                       [b_pgb[j], b_const], [bpt])
                                EV(segb[:, :, :, 1 + pgi * 8:9 + pgi * 8].rearrange("p r s c -> p r c s"),
                                   pt_[:, 0:256].rearrange("p (r c s) -> p r c s", r=2, s=16), [bpt], [b_seg])
                            if sg_ == 0:
                                CP(halo[t][:, :, :, :], segb[:, :, :, 512:513], [b_seg], [b_halo[t]])
                            for g in range(4):
                                pr, hh = g // 2, g % 2
                                pp, bpp = nps(0, 4)
                                for j in range(32):
                                    MM(pp[:, 0:nb], w1[t][hh * 64:(hh + 1) * 64, j, :],
                                       segb[hh * 64:(hh + 1) * 64, pr, j % 16, b0 + j // 16:b0 + j // 16 + nb], j == 0, j == 31,
                                       [b_const, b_seg], [bpp])
                                xg_, x2_, gl_ = sxg, sx2, sgel
                                bxg_, bx2_, bgl_ = b_sxg, b_sx2, b_sgel
                                TSC(xg_[:, 0:nb], pp[:, 0:nb], cvec[:, t:t + 1], None, ALU.add, None, [bpp, b_cvec], [bxg_])
                                TT(x2_[:, 0:nb], xg_[:, 0:nb], xg_[:, 0:nb], ALU.mult, [bxg_], [bx2_])
                                TSC(x2_[:, 0:nb], x2_[:, 0:nb], 0.044715, 1.0, ALU.mult, ALU.add, [bx2_], [bx2_])
                                TT(x2_[:, 0:nb], x2_[:, 0:nb], xg_[:, 0:nb], ALU.mult, [bx2_, bxg_], [bx2_])
                                ACT(x2_[:, 0:nb], x2_[:, 0:nb], AF.Exp, [bx2_], [bx2_], scale=-1.5957691216057308)
                                TSC(x2_[:, 0:nb], x2_[:, 0:nb], 1.0, None, ALU.add, None, [bx2_], [bx2_])
                                RCP(x2_[:, 0:nb], x2_[:, 0:nb], [bx2_], [bx2_])
                                TT(gl_[:, 0:nb], x2_[:, 0:nb], xg_[:, 0:nb], ALU.mult, [bx2_, bxg_], [bgl_])
                                if t == 0:
                                    pp, bpp = nps(0, 4)
                                    MM(pp[:, 0:nb], w2kd[:, :], gl_[:, 0:nb], True, True, [b_const, bgl_], [bpp])
                                    EV(kcTs[:, g, sg_ * 512:sg_ * 512 + nb], pp[:, 0:nb], [bpp], [b_kcTs])
                                else:
                                    for c_ in range(4):
                                        nk_ = tile_nk[sg_ * 4 + c_]
                                        pp, bpp = nps(0, 4)
                                        MM(pp[0:nk_, 0:64], gl_[:, c_ * 128:c_ * 128 + nk_], w2v[:, :], True, True,
                                           [b_const, bgl_], [bpp])
                                        EV(vcs[0:nk_, sg_ * 4 + c_, g, 0:64], pp[0:nk_, 0:64], [bpp], [b_vcs])

                    pTa = [sb(s2s, f"s_pTa{i}", [128, 4, 128], BF16) for i in range(2)]
                    b_pTa = P.bufs('spTa', 2)
                    efa = sb(s2s, "s_efa", [128, 8, 4, 128], F32)
                    b_efa = P.buf('sefa')
                    accb = [sb(s2s, f"s_acc{i}", [128, 260], F32) for i in range(3)]
                    b_accb = P.bufs('sacc', 3)
                    st_rr = [0]

                    s_pend = [None]

                    def s_flush():
                        if s_pend[0] is not None:
                            s_pend[0]()
                            s_pend[0] = None

                    def s_tile(KT_of_g, nk, bKT, V_of_g, bV, acc_i, first, capture=None):
                        i = st_rr[0] % 2
                        st_rr[0] += 1
                        for h in range(2):
                            pp, bpp = ps[h * 2 + i], b_ps[h * 2 + i]
                            for g in range(4):
                                MM(pp[0:nk, g * 64:(g + 1) * 64], KT_of_g(g)[h * 64:(h + 1) * 64, :],
                                   nqTs[h * 64:(h + 1) * 64, 2 * g:2 * g + 2, :], True, True, [b_samp] + bKT, [bpp])
                            ppv = pp[0:nk, 0:256].rearrange("p (g c) -> p g c", c=64)
                            if capture is None:
                                ACT(pTa[i][0:nk, :, h * 64:(h + 1) * 64], ppv, AF.Exp, [bpp], [b_pTa[i]], scale=0.125)
                            else:
                                ACT(efa[0:nk, capture, :, h * 64:(h + 1) * 64], ppv, AF.Exp, [bpp], [b_efa], scale=0.125)
                        if capture is not None:
                            CP(pTa[i][0:nk, :, :], efa[0:nk, capture, :, :], [b_efa], [b_pTa[i]])
                        s_flush()

                        def pv_part(i=i, nk=nk, V_of_g=V_of_g, bV=bV, acc_i=acc_i, first=first):
                            pv, bpv = ps[4 + i], b_ps[4 + i]
                            for g in range(4):
                                MM(pv[:, g * 65:(g + 1) * 65], pTa[i][0:nk, g, :], V_of_g(g), True, True, [b_pTa[i]] + bV, [bpv])
                            if first:
                                CP(accb[acc_i][:, :], pv[:, 0:260], [bpv], [b_accb[acc_i]])
                            else:
                                TT(accb[acc_i][:, :], accb[acc_i][:, :], pv[:, 0:260], ALU.add, [bpv, b_accb[acc_i]],
                                   [b_accb[acc_i]])
                        s_pend[0] = pv_part

                    gT = sb(s2s, "s_gT", [128, 12], F32)
                    oaccs = sb(s2s, "s_oacc", [128, 256], F32)
                    otmps = sb(s2s, "s_otmp", [128, 64], F32)
                    dnc = sb(s2s, "s_dnc", [128, 8], F32)
                    b_gT, b_oaccs, b_otmps, b_dnc = P.bufs('sfin', 4)
                    gats3 = gats[:, :].rearrange("p (h b) -> p h b", b=3)
                    pg_, bpg_ = nps(0, 4)
                    for slot in range(4):
                        MM(pg_[:, 0:12], gsel[0:32, slot, :], gats3[0:32, slot:16:4, :], slot == 0, slot == 3, [b_const, b_samp], [bpg_])
                    CP(gT[:, :], pg_[:, 0:12], [bpg_], [b_gT])
                    gT3 = gT[:, :].rearrange("p (g b) -> p g b", b=3)

                    def fin_s(acc_i, br, first):
                        for g in range(4):
                            TSC(dnc[:, g:g + 1], accb[acc_i][:, g * 65 + 64:g * 65 + 65], 1e-30, None, ALU.max, None,
                                [b_accb[acc_i]], [b_dnc])
                        RCP(dnc[:, 0:4], dnc[:, 0:4], [b_dnc], [b_dnc])
                        TT(dnc[:, 4:8], dnc[:, 0:4], gT3[:, :, br], ALU.mult, [b_dnc, b_gT], [b_dnc])
                        for g in range(4):
                            if first:
                                ACT(oaccs[:, g * 64:(g + 1) * 64], accb[acc_i][:, g * 65:g * 65 + 64], AF.Copy,
                                    [b_accb[acc_i], b_dnc], [b_oaccs], scale=dnc[:, 4 + g:5 + g])
                            else:
                                ACT(otmps[:, :], accb[acc_i][:, g * 65:g * 65 + 64], AF.Copy, [b_accb[acc_i], b_dnc], [b_otmps],
                                    scale=dnc[:, 4 + g:5 + g])
                                TT(oaccs[:, g * 64:(g + 1) * 64], oaccs[:, g * 64:(g + 1) * 64], otmps[:, :], ALU.add,
                                   [b_oaccs, b_otmps], [b_oaccs])

                    for s_ in range(nseg):
                        nk = tile_nk[s_]
                        s_tile(lambda g, s_=s_, nk=nk: kcTs[:, g, s_ * 128:s_ * 128 + nk], nk, [b_kcTs],
                               lambda g, s_=s_, nk=nk: vcs[0:nk, s_, g, 0:65], [b_vcs], 0, s_ == 0, capture=s_)
                    s_flush()
                    fin_s(0, 0, True)
                    usb = sb(s2s, "s_usb", [128, 257], F32)
                    lhsg = sb(s2s, "s_lhsg", [128, 32], F32)
                    scs = sb(s2s, "s_scs", [128, 257], F32)
                    scs2 = sb(s2s, "s_scs2", [128, 257], F32)
                    m8s = sb(s2s, "s_m8s", [128, 16], F32)
                    sels = sb(s2s, "s_sels", [128, 257], F32)
                    m01 = sb(s2s, "s_m01", [128, 4, 128], F32)
                    b_usb, b_lhsg, b_scs, b_scs2, b_m8s, b_sels, b_m01 = P.bufs('simp', 7)
                    for g in range(4):
                        pu, bpu = ps[6], b_ps[6]
                        for s_ in range(nseg):
                            nk = tile_nk[s_]
                            MM(pu[:, 0:257], efa[0:nk, s_, g, :], As[0:nk, s_, :], s_ == 0, s_ == nseg - 1, [b_efa, b_const], [bpu])
                        CP(usb[:, :], pu[:, 0:257], [bpu], [b_usb])
                        TSC(dnc[:, 0:1], accb[0][:, g * 65 + 64:g * 65 + 65], 1e-30, None, ALU.max, None, [b_accb[0]], [b_dnc])
                        RCP(dnc[:, 0:1], dnc[:, 0:1], [b_dnc], [b_dnc])
                        TSC(lhsg[:, :], sel0[:, :], dnc[:, 0:1], None, ALU.mult, None, [b_dnc, b_const], [b_lhsg])
                        pim, bpim = ps[7], b_ps[7]
                        MM(pim[0:32, 0:257], lhsg[:, :], usb[:, :], True, True, [b_lhsg, b_usb], [bpim])
                        TT(scs[0:32, :], pim[0:32, 0:257], fbs[0:32, :], ALU.add, [bpim, b_const], [b_scs])
                        P.op('dve', lambda e: e.max(out=m8s[0:32, 0:8], in_=scs[0:32, :]), reads=[b_scs], writes=[b_m8s])
                        P.op('dve', lambda e: e.match_replace(out=scs2[0:32, :], in_to_replace=m8s[0:32, 0:8],
                                                              in_values=scs[0:32, :], imm_value=-3e38),
                             reads=[b_scs, b_m8s], writes=[b_scs2])
                        P.op('dve', lambda e: e.max(out=m8s[0:32, 8:16], in_=scs2[0:32, :]), reads=[b_scs2], writes=[b_m8s])
                        TSC(sels[0:32, :], scs[0:32, :], m8s[0:32, 15:16], None, ALU.is_ge, None, [b_scs, b_m8s], [b_sels])
                        pmk, bpmk = ps[6], b_ps[6]
                        MM(pmk[:, 0:128], rsel[0:32, 0, :], sels[0:32, 0:256:2], True, False, [b_const, b_sels], [bpmk])
                        MM(pmk[:, 0:128], rsel[0:32, 1, :], sels[0:32, 1:256:2], False, True, [b_const, b_sels], [bpmk])
                        CP(m01[:, g, :], pmk[:, 0:128], [bpmk], [b_m01])

                    kpd = [sb(s2s, f"s_kpd{i}", [128, 4, 128], BF16) for i in range(3)]
                    kTs = [sb(s2s, f"s_kTs{i}", [128, 4, 128], BF16) for i in range(3)]
                    vpg = [sb(s2s, f"s_vpg{i}", [128, 4, 80], BF16) for i in range(3)]
                    b_kpd, b_kTs, b_vpg = P.bufs('skpd', 3), P.bufs('skTs', 3), P.bufs('svpg', 3)
                    for i in range(3):
                        P.op('pool', lambda e, i=i: e.memset(vpg[i][:], 1.0), writes=[b_vpg[i]])
                    pt_rr2 = [0]

                    def page_prep(Ksrc, bK, Vsrc, bV, page):
                        i = pt_rr2[0] % 3
                        pt_rr2[0] += 1
                        k3 = Ksrc[:, :].rearrange("p (g d) -> p g d", d=64)
                        EV(kpd[i][:, :, :].rearrange("p g (r d) -> p g r d", d=64),
                           k3.unsqueeze(2).broadcast_to([128, 4, 2, 64]), [bK], [b_kpd[i]])
                        pt_, bpt = ps[6], b_ps[6]
                        for g in range(4):
                            MM(pt_[:, g * 128:(g + 1) * 128], kpd[i][:, g, :], ident[:, :], True, True, [b_kpd[i], b_const], [bpt])
                        EV(kTs[i][:, :, :], pt_[:, :].rearrange("p (g c) -> p g c", c=128), [bpt], [b_kTs[i]])
                        if page is None:
                            EV(vpg[i][:, :, 0:64], Vsrc[:, :].rearrange("p (g d) -> p g d", d=64), [bV], [b_vpg[i]])
                        else:
                            TT(vpg[i][:, :, 0:64], Vsrc[:, :].rearrange("p (g d) -> p g d", d=64),
                               m01[:, :, page:page + 1].broadcast_to([128, 4, 64]), ALU.mult, [bV, b_m01], [b_vpg[i]])
                            CP(vpg[i][:, :, 64], m01[:, :, page], [b_m01], [b_vpg[i]])
                        return i

                    def page_attend(i, acc_i, first):
                        s_tile(lambda g, i=i: kTs[i][:, g, :], 128, [b_kTs[i]], lambda g, i=i: vpg[i][:, g, 0:65], [b_vpg[i]],
                               acc_i, first)

                    def win_prep(t_):
                        i1 = pg_rr[0] % NPG
                        pg_rr[0] += 1
                        i2 = pg_rr[0] % NPG
                        pg_rr[0] += 1
                        P.dma('sp', pg[i1][:, :], cwk[t_ * 128:(t_ + 1) * 128, :], writes=[b_pg[i1]])
                        P.dma('sp', pg[i2][:, :], cwv[t_ * 128:(t_ + 1) * 128, :], writes=[b_pg[i2]])
                        return page_prep(pg[i1], b_pg[i1], pg[i2], b_pg[i2], None)
                    nxt = win_prep(0)
                    for t_ in range(4):
                        cur = nxt
                        if t_ + 1 < 4:
                            nxt = win_prep(t_ + 1)
                        page_attend(cur, 2, t_ == 0)
                    s_tile(lambda g: knew[:, 4 + g, :], SW, [b_samp], lambda g: vnew[0:SW, 1, g, 0:65], [b_samp], 2, False)
                    s_flush()
                    fin_s(2, 2, False)
                    npg = 128 if k.npg is None else k.npg
                    def slc_prep(page):
                        srcK, bK = gather(csK, page)
                        srcV, bV = gather(csV, page)
                        return page_prep(srcK, bK, srcV, bV, page)
                    nxt = slc_prep(0)
                    for page in range(npg):
                        cur = nxt
                        if page + 1 < npg:
                            nxt = slc_prep(page + 1)
                        page_attend(cur, 1, page == 0)
                    s_tile(lambda g: knew[:, g, :], SW, [b_samp], lambda g: vnew[0:SW, 0, g, 0:65], [b_samp], 1, False)
                    s_flush()
                    fin_s(1, 1, False)
                    oabs = sb(s2s, "s_oabs", [128, 256], BF16)
                    oTs = sb(s2s, "s_oTs", [128, 2, 128], BF16)
                    b_oabs, b_oTs = P.bufs('sout', 2)
                    CP(oabs[:, :], oaccs[:, :], [b_oaccs], [b_oabs])
                    pt_, bpt = nps(0, 4)
                    for c_ in range(2):
                        MM(pt_[:, c_ * 128:(c_ + 1) * 128], oabs[:, c_ * 128:(c_ + 1) * 128], ident[:, :], True, True,
                           [b_oabs, b_const], [bpt])
                    EV(oTs[:, :, :], pt_[:, 0:256].rearrange("p (c q) -> p c q", q=128), [bpt], [b_oTs])
                    for g in range(4):
                        for slot in range(4):
                            P.dma('sp', oaT_d[2 * g + slot // 2, (slot % 2) * 64:(slot % 2) * 64 + 64, 1024:1024 + SW],
                                  oTs[(g % 2) * 64:(g % 2) * 64 + 64, g // 2, slot * 32:slot * 32 + SW],
                                  reads=[b_oTs], writes=[b_oaTd])

            P.barrier()
            b_mTd = P.buf('mTd')
            if 'merge' not in k.skip:
              with ExitStack() as s3:
                alloc_w(s3, 'm', 4096, nb=3)
                oaTs = sb(s3, "m_oaTs", [128, 8, 1024 + SW], BF16)
                obTs = sb(s3, "m_obTs", [128, 8, 1024 + SW], BF16)
                sga = sb(s3, "m_sga", [128, 2, 1024 + SW], BF16)
                sgb = sb(s3, "m_sgb", [128, 2, 1024 + SW], BF16)
                t1 = sb(s3, "m_t1", [128, 2, 1024 + SW], F32)
                t2 = sb(s3, "m_t2", [128, 512], F32)
                mst = sb(s3, "m_mst", [128, 2, 1024 + SW], BF16)
                b_oaTs, b_obTs, b_sga, b_sgb, b_t1, b_t2, b_mst = P.bufs('mrg', 7)
                for f in range(8):
                    P.dma('sp', oaTs[:, f, :], oaT_d[f, :, :], reads=[b_oaTd], writes=[b_oaTs])
                    P.dma('sp', obTs[:, f, :], obT_d[f, :, :], reads=[b_obTd], writes=[b_obTs])
                own3 = [(0, 512), (512, 512), (1024, SW)]
                for L in range(8):
                    for which in range(4):
                        if which < 2:
                            wt, bw = load_w(w_br, which * 2048 + L * 256, 256)
                        else:
                            wt, bw = load_w(w_bra if which == 2 else w_brb, L * 256, 256, nkc=8)
                        for c2 in range(2):
                            for (t0, n) in own3:
                                pp, bpp = nps(0, 8)
                                nk_ = 16 if which < 2 else 8
                                for kc in range(nk_):
                                    if which < 2:
                                        rhs = hT[:, kc, 1024 + t0:1024 + t0 + n]
                                        rd = [bw] + b_hT
                                    elif which == 2:
                                        rhs = oaTs[:, kc, t0:t0 + n]
                                        rd = [bw, b_oaTs]
                                    else:
                                        rhs = obTs[:, kc, t0:t0 + n]
                                        rd = [bw, b_obTs]
                                    MM(pp[:, 0:n], wt[:, kc, c2 * 128:(c2 + 1) * 128], rhs, kc == 0, kc == nk_ - 1, rd, [bpp])
                                if which == 0:
                                    ACT(sga[:, c2, t0:t0 + n], pp[:, 0:n], AF.Sigmoid, [bpp], [b_sga])
                                elif which == 1:
                                    ACT(sgb[:, c2, t0:t0 + n], pp[:, 0:n], AF.Sigmoid, [bpp], [b_sgb])
                                elif which == 2:
                                    TT(t1[:, c2, t0:t0 + n], pp[:, 0:n], sga[:, c2, t0:t0 + n], ALU.mult, [bpp, b_sga], [b_t1])
                                else:
                                    TT(t2[:, 0:n], pp[:, 0:n], sgb[:, c2, t0:t0 + n], ALU.mult, [bpp, b_sgb], [b_t2])
                                    TT(mst[:, c2, t0:t0 + n], t2[:, 0:n], t1[:, c2, t0:t0 + n], ALU.add, [b_t2, b_t1], [b_mst])
                    for c2 in range(2):
                        P.dma('sp', mT_d[L * 2 + c2, :, :], mst[:, c2, :], reads=[b_mst], writes=[b_mTd])
        P.barrier()
        b_yout = P.buf('yout')
        if 'tail' not in k.skip:
          with ExitStack() as s4:
            x1T = sb(s4, "t_x1T", [128, 16, 1024 + SW], F32)
            b_x1T = P.bufs('x1T', 3)
            identf = sb(s4, "t_identf", [128, 128], F32)
            g2c = sb(s4, "t_g2c", [128, 16], F32)
            gfc = sb(s4, "t_gfc", [128, 16], F32)
            P.dma('sp', identf[:], identf_d[:, :], writes=[b_const])
            P.dma('sp', g2c[:], g2col[:, :], writes=[b_const])
            P.dma('sp', gfc[:], gfcol[:, :], writes=[b_const])
            own3 = [(0, 512), (512, 512), (1024, SW)]
            with ExitStack() as s4a:
                alloc_w(s4a, 'o', 4096, nb=3)
                mTs = sb(s4a, "t_mTs", [128, 16, 1024 + SW], BF16)
                b_mTs = P.buf('mTs')
                for f in range(16):
                    P.dma('sp', mTs[:, f, :], mT_d[f, :, :], reads=[b_mTd], writes=[b_mTs])
                xq = [sb(s4a, f"t_xq{i}", [128, D], F32) for i in range(2)]
                b_xq = P.bufs('xq', 2)
                for ti in range(9):
                    i = ti % 2
                    t0, m = (ti * 128, 128) if ti < 8 else (1024, SW)
                    ri = 0 if t0 < 512 else (1 if t0 < 1024 else 2)
                    if ti < 8:
                        P.dma('sp', xq[i][0:m, :], xa[1024 + t0:1024 + t0 + m, :], writes=[b_xq[i]])
                    else:
                        P.op('pool', lambda e, i=i, m=m: e.memset(xq[i][0:m, :], 0.0), writes=[b_xq[i]])
                        P.dma('sp', xq[i][0:1, :], xs[0:1, :], writes=[b_xq[i]])
                    for q4 in range(4):
                        pt_, bpt = nps(0, 8)
                        for j in range(4):
                            kc = q4 * 4 + j
                            MM(pt_[:, j * 128:j * 128 + m], xq[i][0:m, kc * 128:(kc + 1) * 128], identf[0:m, 0:m], True, True,
                               [b_xq[i], b_const], [bpt])
                        ptv = pt_[:, :].rearrange("p (j c) -> p j c", c=128)
                        EV(x1T[:, q4 * 4:(q4 + 1) * 4, t0:t0 + m], ptv[:, :, 0:m], [bpt], [b_x1T[ri]])
                for L in range(8):
                    wt, bw = load_w(w_o, L * 256, 256)
                    for c2 in range(2):
                        ch = L * 2 + c2
                        for ri, (t0, n) in enumerate(own3):
                            pp, bpp = nps(0, 8)
                            for kc in range(16):
                                MM(pp[:, 0:n], wt[:, kc, c2 * 128:(c2 + 1) * 128], mTs[:, kc, t0:t0 + n], kc == 0, kc == 15,
                                   [bw, b_mTs], [bpp])
                            TT(x1T[:, ch, t0:t0 + n], x1T[:, ch, t0:t0 + n], pp[:, 0:n], ALU.add, [bpp, b_x1T[ri]], [b_x1T[ri]])
            P.barrier()
            sqt = [sb(s4, f"t_sq{i}", [128, 512], F32) for i in range(2)]
            b_sqt = P.bufs('sqt', 2)
            rs = sb(s4, "t_rs", [128, 512], F32)
            tmpn = [sb(s4, f"t_tmpn{i}", [128, 512], F32) for i in range(2)]
            b_rs = P.buf('rs')
            b_tmpn = P.bufs('tmpn', 2)
            nrm_rr = [0]

            def fm_rstd(ri, t0, n):
                pss, bpss = nps(0, 8)
                for kc in range(16):
                    i = nrm_rr[0] % 2
                    nrm_rr[0] += 1
                    ACT(sqt[i][:, 0:n], x1T[:, kc, t0:t0 + n], AF.Square, [b_x1T[ri]], [b_sqt[i]])
                    MM(pss[:, 0:n], ones_f[:, :], sqt[i][:, 0:n], kc == 0, kc == 15, [b_const, b_sqt[i]], [bpss])
                CP(rs[:, 0:n], pss[:, 0:n], [bpss], [b_rs])
                ACT(rs[:, 0:n], rs[:, 0:n], AF.Sqrt, [b_rs, b_const], [b_rs], scale=1.0 / D, bias=epsc[:, 0:1])
                RCP(rs[:, 0:n], rs[:, 0:n], [b_rs], [b_rs])

            with ExitStack() as s4b:
                alloc_w(s4b, 'f', 4096, nb=6)
                h2T = sb(s4b, "t_h2T", [128, 16, 1024 + SW], BF16)
                hfc = [sb(s4b, f"t_hfc{i}", [128, 2, 1024 + SW], BF16) for i in range(2)]
                rl = [sb(s4b, f"t_rl{i}", [128, 512], F32) for i in range(2)]
                b_h2T = P.bufs('h2T', 3)
                b_hfc = P.bufs('hfc', 2)
                b_rl = P.bufs('rl', 2)
                rl_rr = [0]
                for ri, (t0, n) in enumerate(own3):
                    fm_rstd(ri, t0, n)
                    for kc in range(16):
                        i = kc % 2
                        TT(tmpn[i][:, 0:n], x1T[:, kc, t0:t0 + n], rs[:, 0:n], ALU.mult, [b_x1T[ri], b_rs], [b_tmpn[i]])
                        ACT(h2T[:, kc, t0:t0 + n], tmpn[i][:, 0:n], AF.Copy, [b_tmpn[i], b_const], [b_h2T[ri]],
                            scale=g2c[:, kc:kc + 1])
                nchunk = 32 if k.npass is None else k.npass
                for c in range(nchunk):
                    wu, bwu = load_w(w_up, c * 256, 256)
                    wd, bwd = load_w(w_down, 0, 2048, nkc=2, row0=c * 256)
                    hi = c % 2
                    for c2 in range(2):
                        for ri, (t0, n) in enumerate(own3):
                            pp, bpp = nps(0, 8)
                            for kc in range(16):
                                MM(pp[:, 0:n], wu[:, kc, c2 * 128:(c2 + 1) * 128], h2T[:, kc, t0:t0 + n], kc == 0, kc == 15,
                                   [bwu, b_h2T[ri]], [bpp])
                            i = rl_rr[0] % 2
                            rl_rr[0] += 1
                            ACT(rl[i][:, 0:n], pp[:, 0:n], AF.Relu, [bpp], [b_rl[i]])
                            TT(hfc[hi][:, c2, t0:t0 + n], rl[i][:, 0:n], rl[i][:, 0:n], ALU.mult, [b_rl[i]], [b_hfc[hi]])
                    for oc in range(16):
                        for ri, (t0, n) in enumerate(own3):
                            pp, bpp = nps(0, 8)
                            for kc2 in range(2):
                                MM(pp[:, 0:n], wd[:, kc2, oc * 128:(oc + 1) * 128], hfc[hi][:, kc2, t0:t0 + n], kc2 == 0, kc2 == 1,
                                   [bwd, b_hfc[hi]], [bpp])
                            TT(x1T[:, oc, t0:t0 + n], x1T[:, oc, t0:t0 + n], pp[:, 0:n], ALU.add, [bpp, b_x1T[ri]], [b_x1T[ri]])
            P.barrier()
            with ExitStack() as s4c:
                ytok = [sb(s4c, f"t_ytok{i}", [128, D], F32) for i in range(4)]
                b_ytok = P.bufs('ytok', 4)
                yk = [sb(s4c, f"t_yk{i}", [128, 512], F32) for i in range(2)]
                b_yk = P.bufs('yk', 2)
                for ri, (t0, n) in enumerate(own3):
                    fm_rstd(ri, t0, n)
                    nsub = (n + 127) // 128
                    for q4 in range(4):
                        pts = [(ps[4 + sbi], b_ps[4 + sbi]) for sbi in range(nsub)]
                        for j in range(4):
                            kc = q4 * 4 + j
                            i = kc % 2
                            TT(tmpn[i][:, 0:n], x1T[:, kc, t0:t0 + n], rs[:, 0:n], ALU.mult, [b_x1T[ri], b_rs], [b_tmpn[i]])
                            ACT(yk[i][:, 0:n], tmpn[i][:, 0:n], AF.Copy, [b_tmpn[i], b_const], [b_yk[i]], scale=gfc[:, kc:kc + 1])
                            for sbi in range(nsub):
                                m = min(128, n - sbi * 128)
                                MM(pts[sbi][0][0:m, j * 128:(j + 1) * 128], yk[i][:, sbi * 128:sbi * 128 + m], identf[:, :], True, True,
                                   [b_yk[i], b_const], [pts[sbi][1]])
                        for sbi in range(nsub):
                            m = min(128, n - sbi * 128)
                            EV(ytok[sbi][0:m, q4 * 512:(q4 + 1) * 512], pts[sbi][0][0:m, :], [pts[sbi][1]], [b_ytok[sbi]])
                    for sbi in range(nsub):
                        m = min(128, n - sbi * 128)
                        if ri < 2:
                            P.dma('sp', o_y[t0 + sbi * 128:t0 + sbi * 128 + m, :], ytok[sbi][0:m, :], reads=[b_ytok[sbi]], writes=[b_yout])
                        else:
                            P.dma('sp', o_ys[0:m, :], ytok[sbi][0:m, :], reads=[b_ytok[sbi]], writes=[b_yout])
        if dbg != 'p0':
            P.op('sp', lambda e: e.nop(), reads=[b_out_gla, b_obTd, b_oaTd, b_rows, b_mTd, b_yout])
        k.stats = P.finish(top)
        k.n_waits = P.n_waits
    return nc, k


def _prep_core(c, inp):
    b, hf = c // 2, c % 2
    m = {}
    xp = inp['x_prompt']
    own = xp[b, hf * 1024:(hf + 1) * 1024]
    pre = xp[b, 0:1024] if hf == 1 else np.zeros((1024, D), np.float32)
    m['xa'] = np.ascontiguousarray(np.concatenate([pre, own], 0))
    m['xs'] = np.ascontiguousarray(inp['x_sample'][c])
    m['sg'] = np.ascontiguousarray(inp['state_gla'][0, c])
    m['pt_rep'] = np.ascontiguousarray(np.tile(inp['page_table'][c][None, :], (128, 1)).astype(np.int32))
    m['cwk'] = np.ascontiguousarray(inp['cache_win_k'][0, c].reshape(512, 256))
    m['cwv'] = np.ascontiguousarray(inp['cache_win_v'][0, c].reshape(512, 256))
    m.update(_consts(hf))
    return m


def _prep_shared(inp):
    s = {}
    w_in = inp['w_in'][0]
    o = 1024 + 1536 + 48
    q_l = w_in[:, o:o + 512]
    k_l = w_in[:, o + 512:o + 1024]
    v_l = w_in[:, o + 1024:o + 2048]
    r_l = w_in[:, o + 2048:o + 3072]
    lr = w_in[:, o + 3072:o + 3088]
    s['w_gla_fm'] = np.ascontiguousarray(np.concatenate([q_l, k_l, lr], 1))
    s['w_gla_tm'] = np.ascontiguousarray(np.concatenate([k_l, v_l, r_l], 1))
    s['g1col'] = np.ascontiguousarray(inp['norm1_g'][0].reshape(16, 128).T)
    s['w_a2'] = np.ascontiguousarray(inp['gla_w_a2'][0])
    s['b_a'] = np.ascontiguousarray(inp['gla_b_a'][0][None, :])
    s['gnorm'] = np.ascontiguousarray(np.tile(inp['gla_norm_g'][0][None, :], (128, 1)))
    qcols = []
    for g in range(4):
        for r in range(2):
            for hd in (4 * g + r, 4 * g + r + 2):
                qcols.append(w_in[:, hd * 64:(hd + 1) * 64])
    s['w_q'] = np.ascontiguousarray(np.concatenate(qcols, 1))
    kv0 = 1024
    kcols = []
    for typ in (0, 1, 2, 4):
        for g in range(4):
            blk = w_in[:, kv0 + typ * 256 + g * 64: kv0 + typ * 256 + (g + 1) * 64]
            kcols += [blk, blk]
    s['w_kT'] = np.ascontiguousarray(np.concatenate(kcols, 1))
    s['w_kv'] = np.ascontiguousarray(w_in[:, kv0:kv0 + 1536])
    s['w_g'] = np.ascontiguousarray(w_in[:, 2560:2608])
    s['ccK'] = inp['cache_cmp_k'][0].reshape(163840, 256)
    s['ccV'] = inp['cache_cmp_v'][0].reshape(163840, 256)
    s['csK'] = inp['cache_slc_k'][0].reshape(163840, 256)
    s['csV'] = inp['cache_slc_v'][0].reshape(163840, 256)
    s['w_br'] = np.ascontiguousarray(w_in[:, 5696:9792])
    s['w_bra'] = np.ascontiguousarray(inp['w_br_a'][0])
    s['w_brb'] = np.ascontiguousarray(inp['w_br_b'][0])
    s['w_o'] = np.ascontiguousarray(inp['w_o'][0])
    s['w_up'] = np.ascontiguousarray(inp['w_up'][0])
    s['w_down'] = np.ascontiguousarray(inp['w_down'][0])
    s['g2col'] = np.ascontiguousarray(inp['norm2_g'][0].reshape(16, 128).T)
    s['gfcol'] = np.ascontiguousarray(inp['norm_f'].reshape(16, 128).T)
    s['bgate'] = np.ascontiguousarray(inp['b_nsa_gate'][0][None, :])
    for nm, key in (('w1k', 'cmp_k_w1'), ('w1v', 'cmp_v_w1')):
        w1 = inp[key][0].reshape(32, 64, 128).transpose(1, 0, 2)
        s[nm] = np.ascontiguousarray(np.concatenate([w1, w1], 0).reshape(128, 4096))
    w2k = inp['cmp_k_w2'][0]
    s['w2kd'] = np.ascontiguousarray(np.concatenate([w2k, w2k], 1))
    s['w2v'] = np.ascontiguousarray(inp['cmp_v_w2'][0])
    for nm, key in (('pek', 'cmp_pe_k'), ('pev', 'cmp_pe_v')):
        pe = inp[key][0].T
        pe = np.concatenate([pe, pe], 0)
        s[nm] = np.ascontiguousarray(np.repeat(pe[:, :, None], 32, axis=2).reshape(128, 1024))
    return s


_CACHE = {}


def kernel(**inp):
    inp = {k_: np.asarray(v) for k_, v in inp.items()}
    if 'nc' not in _CACHE:
        _CACHE['nc'] = build()
    nc, kk = _CACHE['nc']
    shared = _prep_shared(inp)
    in_maps = []
    for c in range(8):
        m = dict(shared)
        m.update(_prep_core(c, inp))
        in_maps.append(m)
    res = run_bass_kernel_spmd(nc, in_maps, core_ids=list(range(8)))
    R = res.results
    p_gla = np.stack([R[2 * b + 1]['o_pgla'] for b in range(4)])[None].astype(np.float32)
    s_gla = np.stack([R[c]['o_sgla'] for c in range(8)])[None].astype(np.float32)
    y_prompt = np.zeros((4, 2048, D), np.float32)
    y_sample = np.zeros((8, 1, D), np.float32)
    prow = [np.zeros((1, 4, 2048, 4, 64), np.float32) for _ in range(4)]
    pwin = [np.zeros((1, 4, 512, 4, 64), np.float32) for _ in range(2)]
    srow = [np.zeros((1, 8, 1, 4, 64), np.float32) for _ in range(4)]
    swin = [np.zeros((1, 8, 512, 4, 64), np.float32) for _ in range(2)]
    for c in range(8):
        b, hf = c // 2, c % 2
        y_prompt[b, hf * 1024:(hf + 1) * 1024] = R[c]['o_y']
        y_sample[c, 0] = R[c]['o_ys'][0]
        rows = R[c]['o_rows']
        for L in range(4):
            prow[L][0, b, hf * 1024:(hf + 1) * 1024] = rows[:, L * 256:(L + 1) * 256].reshape(1024, 4, 64)
            srow[L][0, c, 0] = R[c]['o_rows_s'][0, L * 256:(L + 1) * 256].reshape(4, 64)
        if hf == 1:
            for L in range(2):
                pwin[L][0, b] = rows[512:1024, (4 + L) * 256:(5 + L) * 256].reshape(512, 4, 64)
        swin[0][0, c] = R[c]['o_swk'].reshape(512, 4, 64)
        swin[1][0, c] = R[c]['o_swv'].reshape(512, 4, 64)
    return (y_prompt, y_sample, prow[0], prow[1], prow[2], prow[3], pwin[0], pwin[1], p_gla,
            srow[0], srow[1], srow[2], srow[3], swin[0], swin[1], s_gla)
```

```python
import numpy as np
from contextlib import ExitStack
import concourse.bass as bass
import concourse.mybir as mybir
from concourse.bass_utils import run_bass_kernel_spmd
from concourse.alu_op_type import AluOpType as ALU

F32 = mybir.dt.float32
BF16 = mybir.dt.bfloat16
I32 = mybir.dt.int32
AF = mybir.ActivationFunctionType
AX = mybir.AxisListType

D = 2048
NPF = 2048
TS = 2048
NPG = 8
SW = 32
NT = NPF + SW
EPS = 1e-6
NEGB = -30000.0

ENGS = ('pe', 'dve', 'act', 'pool', 'sp')
ENG_ATTR = {'pe': 'tensor', 'dve': 'vector', 'act': 'scalar', 'pool': 'gpsimd', 'sp': 'sync'}


class Buf:
    __slots__ = ('name', 'last_w', 'readers')

    def __init__(self, name):
        self.name = name
        self.last_w = None
        self.readers = []


class Op:
    __slots__ = ('eng', 'fn', 'reads', 'writes', 'deps', 'needs_inc', 'sem', 'semval', 'is_dma', 'idx',
                 'extra_waits', 'barrier')

    def __init__(self, eng, fn, reads, writes, is_dma):
        self.eng = eng
        self.fn = fn
        self.reads = reads
        self.writes = writes
        self.deps = []
        self.needs_inc = False
        self.sem = None
        self.semval = 0
        self.is_dma = is_dma
        self.extra_waits = []
        self.barrier = False


class Prog:
    def __init__(self, nc, dma_pool=16):
        self.nc = nc
        self.ops = []
        self.dma_pool = dma_pool

    def buf(self, name):
        return Buf(name)

    def bufs(self, name, n):
        return [Buf(f"{name}{i}") for i in range(n)]

    mute = False

    def op(self, eng, fn, reads=(), writes=()):
        if self.mute:
            return None
        o = Op(eng, fn, tuple(reads), tuple(writes), False)
        self.ops.append(o)
        return o

    def dma(self, eng, out, in_, reads=(), writes=(), **kw):
        if self.mute:
            return None

        def fn(e, out=out, in_=in_, kw=kw):
            return e.dma_start(out=out, in_=in_, **kw)
        o = Op(eng, fn, tuple(reads), tuple(writes), True)
        self.ops.append(o)
        return o

    def dma_fn(self, eng, fn, reads=(), writes=()):
        o = Op(eng, fn, tuple(reads), tuple(writes), True)
        self.ops.append(o)
        return o

    def barrier(self):
        for e in ENGS:
            o = Op(e, lambda eng: eng.nop(), (), (), False)
            o.barrier = True
            self.ops.append(o)

    def finish(self, stack):
        nc = self.nc
        ops = self.ops
        last_on = {e: None for e in ENGS}
        dma_since = []
        for i, o in enumerate(ops):
            o.idx = i
            deps = {}
            if o.barrier:
                for e in ENGS:
                    if last_on[e] is not None:
                        deps[last_on[e].idx] = last_on[e]
                for d in dma_since:
                    deps[d.idx] = d
            for b in o.reads:
                if b.last_w is not None:
                    deps[b.last_w.idx] = b.last_w
            for b in o.writes:
                if b.last_w is not None:
                    deps[b.last_w.idx] = b.last_w
                for r in b.readers:
                    deps[r.idx] = r
            for b in o.reads:
                b.readers.append(o)
            for b in o.writes:
                b.last_w = o
                b.readers = []
            deps.pop(i, None)
            for d in deps.values():
                if d.eng == 'pe' and o.eng == 'pe' and not d.is_dma and not o.is_dma and not o.barrier:
                    continue
                o.deps.append(d)
                d.needs_inc = True
            if o.is_dma:
                o.needs_inc = True
                dma_since.append(o)
            else:
                last_on[o.eng] = o
            if o.barrier and o.eng == ENGS[-1]:
                dma_since = []
        eng_sem = {e: stack.enter_context(nc.semaphore(f"s_{e}")) for e in ENGS}
        eng_cnt = {e: 0 for e in ENGS}
        pools = {e: None for e in ENGS}
        pool_state = {}
        for o in ops:
            if not o.needs_inc:
                continue
            if o.is_dma:
                if pools[o.eng] is None:
                    pools[o.eng] = [stack.enter_context(nc.semaphore(f"d_{o.eng}{k}"))
                                    for k in range(self.dma_pool)]
                    pool_state[o.eng] = [0, [0] * self.dma_pool]
                st = pool_state[o.eng]
                k = st[0] % self.dma_pool
                st[0] += 1
                sem = pools[o.eng][k]
                prev = st[1][k]
                if prev > 0:
                    o.extra_waits.append((sem, prev))
                st[1][k] = prev + 16
                o.sem = sem
                o.semval = prev + 16
            else:
                eng_cnt[o.eng] += 1
                o.sem = eng_sem[o.eng]
                o.semval = eng_cnt[o.eng]
        self.n_waits = 0
        per_eng = {e: [o for o in ops if o.eng == e] for e in ENGS}
        block = stack.enter_context(nc.Block())

        def emit(e, lst):
            waited = {}

            def body(eng):
                for o in lst:
                    ws = [(d.sem, d.semval) for d in o.deps] + o.extra_waits
                    for sem, val in ws:
                        key = id(sem)
                        if waited.get(key, 0) >= val:
                            continue
                        waited[key] = val
                        eng.wait_ge(sem, val)
                        self.n_waits += 1
                    ins = o.fn(eng)
                    if o.needs_inc:
                        ins.then_inc(o.sem, 16 if o.is_dma else 1)
            return body

        for e in ENGS:
            if per_eng[e]:
                getattr(block, ENG_ATTR[e])(emit(e, per_eng[e]))
        return {e: len(per_eng[e]) for e in ENGS}


class K:
    pass


def _consts(hf):
    c = {}
    c['ident_bf'] = np.eye(128, dtype=np.float32)
    c['ident_f'] = np.eye(128, dtype=np.float32)
    s = np.arange(128)
    U01 = (s[:, None] <= s[None, :]).astype(np.float32)
    c['uneg'] = (-U01 / 16.0).astype(np.float32)
    c['u01x4'] = np.tile(U01[:, None, :], (1, 4, 1)).reshape(128, 512).astype(np.float32)
    c['ones_f'] = np.ones((128, 128), np.float32)
    esel = np.zeros((128, 16, 128), np.float32)
    for row in list(range(32)) + list(range(64, 96)):
        j = row % 64
        for kt in range(16):
            key = np.arange(128)
            esel[row, kt, :] = ((kt * 128 + key) // 64 == j)
    c['esel'] = esel.reshape(128, 2048)
    kk = np.arange(128)[:, None]
    qq = np.arange(128)[None, :]
    le = (kk <= qq).astype(np.float32)
    ge = (kk >= qq).astype(np.float32)
    c['tri'] = np.concatenate([le, le, ge, ge], 1).astype(np.float32)
    ii = np.arange(128)[:, None, None]
    qt = np.arange(8)[None, :, None]
    q = np.arange(128)[None, None, :]
    qpos = 1024 + qt * 128 + q
    c['cmk'] = ((16 * ii + 31) <= qpos).astype(np.float32).reshape(128, 1024)
    i2 = np.arange(128)[:, None]
    j2 = np.arange(32)[None, :]
    c['amat'] = ((i2 >= 4 * j2 - 1) & (i2 <= 4 * j2 + 3)).astype(np.float32)
    qpos2 = 1024 + np.arange(8)[None, :, None] * 128 + np.arange(128)[:, None, None]
    jj = np.arange(32)[None, None, :]
    valid = (jj * 64 <= qpos2) & ((hf == 1) | (jj >= 16))
    forced = (jj == (0 if hf == 1 else 16)) | (jj == qpos2 // 64)
    fb = np.where(valid, np.where(forced, 1e4, 0.0), -1e30).astype(np.float32)
    c['fbias'] = fb.reshape(128, 256)
    pkc = np.zeros((128, 4), np.float32)
    if hf == 0:
        pkc[0:64, 0] = -3750.0
        for row in list(range(16)) + list(range(64, 80)):
            pkc[row, 1] = -30000.0
    pkc[:, 2] = 1.0 if hf == 1 else 0.0
    c['pkc'] = pkc
    As = np.zeros((128, 8, 257), np.float32)
    jv = np.arange(257)[None, :]
    for tl in range(8):
        for r in range(128):
            if tl == 3 and r == 127:
                continue
            i = tl * 128 + r if tl < 4 else 511 + (tl - 4) * 128 + r
            As[r, tl, :] = ((i >= 4 * jv - 1) & (i <= 4 * jv + 3))[0]
    c['As'] = As.reshape(128, 8 * 257)
    fbs = np.zeros((128, 257), np.float32)
    fbs[:, 0] = 1e4
    fbs[:, 256] = 1e4
    c['fbs'] = fbs
    rsel = np.zeros((128, 2, 128), np.float32)
    rsel[0, 0, 0:64] = 1.0
    rsel[0, 1, 64:128] = 1.0
    c['rsel'] = rsel.reshape(128, 256)
    gsel = np.zeros((128, 4, 128), np.float32)
    for sl in range(4):
        gsel[0, sl, sl * 32] = 1.0
    c['gsel'] = gsel.reshape(128, 512)
    sel0 = np.zeros((128, 32), np.float32)
    sel0[0::32, 0] = 1.0
    c['sel0'] = sel0
    c['piota'] = np.tile(np.arange(128, dtype=np.float32)[:, None], (1, 128))
    smc = np.zeros((128, 2), np.float32)
    smc[0, 0] = 1.0
    c['smc'] = smc
    return c


def build(dbg=None):
    nc = bass.Bass("TRN2", target_bir_lowering=False)
    k = K()
    k.nc = nc
    k.dbg = dbg
    import os
    k.dbg2 = os.environ.get('DBG2', '')
    k.cut = int(os.environ.get('CUT', '99'))
    k.skip = os.environ.get('SKIP', '').split(',')
    k.nqt = int(os.environ['NQT']) if 'NQT' in os.environ else None
    k.npass = int(os.environ['NPASS']) if 'NPASS' in os.environ else None
    k.nseg = int(os.environ['NSEG']) if 'NSEG' in os.environ else None
    k.npg = int(os.environ['NPG']) if 'NPG' in os.environ else None

    def din(name, shape, dt=F32):
        return nc.dram_tensor(name, list(shape), dt, kind="ExternalInput").ap()

    def dout(name, shape, dt=F32):
        return nc.dram_tensor(name, list(shape), dt, kind="ExternalOutput").ap()

    def dscr(name, shape, dt):
        return nc.dram_tensor(name, list(shape), dt, kind="ExternalOutput").ap()

    xa = din("xa", [NPF, D])
    xs = din("xs", [1, D])
    g1col = din("g1col", [128, 16])
    w_gla_fm = din("w_gla_fm", [D, 1024 + 16])
    w_gla_tm = din("w_gla_tm", [D, 512 + 1024 + 1024])
    w_a2 = din("w_a2", [16, 512])
    b_a = din("b_a", [1, 512])
    gnorm = din("gnorm", [128, 256])
    sg = din("sg", [4, 128, 256])
    ident_bf_d = din("ident_bf", [128, 128])
    uneg_d = din("uneg", [128, 128])
    u01x4_d = din("u01x4", [128, 512])
    ones_d = din("ones_f", [128, 128])
    w_q = din("w_q", [D, 1024])
    w_kT = din("w_kT", [D, 2048])
    w_kv = din("w_kv", [D, 1536])
    w_g = din("w_g", [D, 48])
    bgate_d = din("bgate", [1, 48])
    w1k_d = din("w1k", [128, 4096])
    w1v_d = din("w1v", [128, 4096])
    w2kd_d = din("w2kd", [128, 128])
    w2v_d = din("w2v", [128, 64])
    pek_d = din("pek", [128, 1024])
    pev_d = din("pev", [128, 1024])
    esel_d = din("esel", [128, 2048])
    tri_d = din("tri", [128, 512])
    cmk_d = din("cmk", [128, 1024])
    amat_d = din("amat", [128, 32])
    fbias_d = din("fbias", [128, 256])
    pkc_d = din("pkc", [128, 4])
    w_br = din("w_br", [D, 4096])
    w_bra = din("w_bra", [1024, D])
    w_brb = din("w_brb", [1024, D])
    w_o = din("w_o", [D, D])
    w_up = din("w_up", [D, 8192])
    w_down = din("w_down", [8192, D])
    g2col = din("g2col", [128, 16])
    gfcol = din("gfcol", [128, 16])
    identf_d = din("ident_f", [128, 128])
    mT_d = dscr("mT_d", [16, 128, NT - 1024], BF16)
    o_y = dout("o_y", [1024, D])
    o_ys = dout("o_ys", [SW, D])
    ccK = din("ccK", [163840, 256])
    ccV = din("ccV", [163840, 256])
    csK = din("csK", [163840, 256])
    csV = din("csV", [163840, 256])
    pt_d = din("pt_rep", [128, 128], I32)
    As_d = din("As", [128, 8 * 257])
    fbs_d = din("fbs", [128, 257])
    rsel_d = din("rsel", [128, 256])
    gsel_d = din("gsel", [128, 512])
    sel0_d = din("sel0", [128, 32])
    piota_d = din("piota", [128, 128])
    smc_d = din("smc", [128, 2])
    cwk = din("cwk", [512, 256])
    cwv = din("cwv", [512, 256])
    o_swk = dout("o_swk", [512, 256])
    o_swv = dout("o_swv", [512, 256])
    o_rows = dout("o_rows", [1024, 1536])
    o_rows_s = dout("o_rows_s", [SW, 1536])
    oaT_d = dscr("oaT_d", [8, 128, NT - 1024], BF16)
    o_pgla = dout("o_pgla", [4, 128, 256])
    o_sgla = dout("o_sgla", [4, 128, 256])
    obT_d = dscr("obT_d", [8, 128, NT - 1024], BF16)

    with ExitStack() as top:
        P = Prog(nc)
        k.P = P
        sb = lambda st, name, shape, dt: st.enter_context(nc.sbuf_tensor(name, list(shape), dt))
        ps = [top.enter_context(nc.psum_tensor(f"ps{i}", [128, 512], F32)) for i in range(8)]
        b_ps = P.bufs('ps', 8)
        k.ps_rr = 0

        def nps(lo=0, hi=8):
            i = lo + (k.ps_rr % (hi - lo))
            k.ps_rr += 1
            return ps[i], b_ps[i]
        k.psb_rr = 0

        def npsb():
            i = k.psb_rr % 2
            k.psb_rr += 1
            return psb[:, i * 512:(i + 1) * 512], b_psb[i]

        ident = sb(top, "ident", [128, 128], BF16)
        uneg = sb(top, "uneg_t", [128, 128], F32)
        u01x4 = sb(top, "u01x4_t", [128, 512], F32)
        ones_bf = sb(top, "ones_bf", [128, 128], BF16)
        ones_f = sb(top, "ones_ft", [128, 128], F32)
        g1c = sb(top, "g1c", [128, 16], F32)
        epsc = sb(top, "epsc", [128, 1], F32)
        b_const = P.buf('const')
        P.dma('pool', ident[:], ident_bf_d[:, :], writes=[b_const])
        P.dma('pool', ones_bf[:], ones_d[:, :], writes=[b_const])
        P.dma('sp', ones_f[:], ones_d[:, :], writes=[b_const])
        P.dma('sp', uneg[:], uneg_d[:, :], writes=[b_const])
        P.dma('sp', u01x4[:], u01x4_d[:, :], writes=[b_const])
        P.dma('sp', g1c[:], g1col[:, :], writes=[b_const])
        P.op('pool', lambda e: e.memset(epsc[:], EPS), writes=[b_const])

        NWB = 2
        k.w_rr = 0
        k.wbuf = None
        k.b_wbuf = None

        def alloc_w(st, tag, nelem, nb=2):
            k.nwb = nb
            k.w_rr = 0
            k.wbuf = [sb(st, f"wbuf{tag}{i}", [128, nelem], BF16) for i in range(nb)]
            k.b_wbuf = P.bufs('wbuf' + tag, nb)

        def load_w(src, c0, ncols, nkc=16, row0=0):
            i = k.w_rr % k.nwb
            k.w_rr += 1
            wbuf, b_wbuf = k.wbuf, k.b_wbuf
            wt = wbuf[i][:, 0:nkc * ncols].rearrange("p (k n) -> p k n", n=ncols)
            sv = src[row0:row0 + nkc * 128, c0:c0 + ncols].rearrange("(k p) n -> p k n", p=128)
            P.dma('pool', wt[:, :, :], sv[:, :, :], writes=[b_wbuf[i]])
            return wt, b_wbuf[i]

        with ExitStack() as sB:
            hT = sb(sB, "hT", [128, 16, NT], BF16)
            b_hT = P.bufs('hT', 17)
            tiles = [(t * 128, 128) for t in range(16)] + [(TS, SW)]

            with ExitStack() as s0:
                xt = [sb(s0, f"xt{i}", [128, D], F32) for i in range(3)]
                b_xt = P.bufs('xt', 3)
                junk = sb(s0, "junk", [128, D], BF16)
                b_junk = P.buf('junk')
                xn = [sb(s0, f"xn{i}", [128, D], BF16) for i in range(3)]
                b_xn = P.bufs('xn', 3)
                st_ = sb(s0, "nstat", [128, 17, 2], F32)
                b_st = P.bufs('nst', 17)
                for ti, (t0, m) in enumerate(tiles):
                    if (dbg == 'p0' or 's' in k.dbg2) and ti == 16:
                        continue
                    if 'c' in k.dbg2:
                        continue
                    i = ti % 3
                    if ti < 16:
                        P.dma('sp', xt[i][0:m, :], xa[t0:t0 + m, :], writes=[b_xt[i]])
                    else:
                        P.op('pool', lambda e, i=i, m=m: e.memset(xt[i][0:m, :], 0.0), writes=[b_xt[i]])
                        P.dma('sp', xt[i][0:1, :], xs[0:1, :], writes=[b_xt[i]])
                    if 'b' in k.dbg2:
                        continue
                    P.op('act', lambda e, i=i, m=m, ti=ti: e.activation(
                        out=junk[0:m, :], in_=xt[i][0:m, :], func=AF.Square, accum_out=st_[0:m, ti, 0:1]),
                        reads=[b_xt[i]], writes=[b_junk, b_st[ti]])
                    if 'e' in k.dbg2:
                        continue
                    P.op('act', lambda e, m=m, ti=ti: e.activation(
                        out=st_[0:m, ti, 1:2], in_=st_[0:m, ti, 0:1], func=AF.Sqrt, scale=1.0 / D,
                        bias=epsc[0:m, 0:1]), reads=[b_st[ti], b_const], writes=[b_st[ti]])
                    P.op('dve', lambda e, m=m, ti=ti: e.reciprocal(
                        out=st_[0:m, ti, 1:2], in_=st_[0:m, ti, 1:2]), reads=[b_st[ti]], writes=[b_st[ti]])
                    if 'd' in k.dbg2:
                        continue
                    P.op('dve', lambda e, i=i, m=m, ti=ti: e.tensor_scalar(
                        out=xn[i][0:m, :], in0=xt[i][0:m, :], scalar1=st_[0:m, ti, 1:2], scalar2=None, op0=ALU.mult),
                        reads=[b_xt[i], b_st[ti]], writes=[b_xn[i]])
                    for q4 in range(4 if 'a' not in k.dbg2 else 0):
                        pt_, bpt = nps()
                        for j in range(4):
                            kc = q4 * 4 + j
                            P.op('pe', lambda e, i=i, m=m, kc=kc, j=j, pt_=pt_: e.matmul(
                                pt_[:, j * 128:j * 128 + m], lhsT=xn[i][0:m, kc * 128:(kc + 1) * 128],
                                rhs=ident[0:m, 0:m], start=True, stop=True), reads=[b_xn[i], b_const], writes=[bpt])
                        for j in range(4):
                            kc = q4 * 4 + j
                            eng = 'dve' if j % 2 == 0 else 'act'
                            if eng == 'dve':
                                P.op('dve', lambda e, m=m, kc=kc, j=j, pt_=pt_, t0=t0: e.tensor_scalar(
                                    out=hT[:, kc, t0:t0 + m], in0=pt_[:, j * 128:j * 128 + m],
                                    scalar1=g1c[:, kc:kc + 1], scalar2=None, op0=ALU.mult),
                                    reads=[bpt, b_const], writes=[b_hT[ti]])
                            else:
                                P.op('act', lambda e, m=m, kc=kc, j=j, pt_=pt_, t0=t0: e.activation(
                                    out=hT[:, kc, t0:t0 + m], in_=pt_[:, j * 128:j * 128 + m], func=AF.Copy,
                                    scale=g1c[:, kc:kc + 1]), reads=[bpt, b_const], writes=[b_hT[ti]])
            P.barrier()

            with ExitStack() as s1:
              if dbg != 'p0':
                  alloc_w(s1, 'g', 4096)
                  qT = sb(s1, "g_qT", [128, 4, 1024 + SW], BF16)
                  kT = sb(s1, "g_kT", [128, 4, 1024 + SW], BF16)
                  lrT = sb(s1, "g_lrT", [16, NT], BF16)
                  ktok = sb(s1, "g_ktok", [128, 17, 512], BF16)
                  vtok = sb(s1, "g_vtok", [128, 17, 1024], BF16)
                  rtok = sb(s1, "g_rtok", [128, 9, 1024], BF16)
                  b_qT, b_kT, b_lrT = P.buf('gqT'), P.buf('gkT'), P.buf('glrT')
                  b_ktok, b_vtok, b_rtok = P.bufs('gktok', 17), P.bufs('gvtok', 17), P.bufs('grtok', 9)
                  wa2 = sb(s1, "wa2", [16, 512], BF16)
                  ba = sb(s1, "ba", [1, 512], BF16)
                  gn = sb(s1, "gn", [128, 256], F32)
                  P.dma('pool', wa2[:], w_a2[:, :], writes=[b_const])
                  P.dma('pool', ba[:], b_a[:, :], writes=[b_const])
                  P.dma('sp', gn[:], gnorm[:, :], writes=[b_const])
                  evac_rr = [0]

                  def evac(out_ap, in_ap, reads, writes, mul=None):
                      evac_rr[0] += 1
                      if mul is not None:
                          P.op('act', lambda e: e.activation(out=out_ap, in_=in_ap, func=AF.Copy, scale=mul),
                               reads=reads, writes=writes)
                      elif evac_rr[0] % 2:
                          P.op('act', lambda e: e.activation(out=out_ap, in_=in_ap, func=AF.Copy),
                               reads=reads, writes=writes)
                      else:
                          P.op('dve', lambda e: e.tensor_copy(out=out_ap, in_=in_ap), reads=reads, writes=writes)

                  fm_tok = [(1024, 512), (1536, 512), (TS, SW)]
                  if 's' in k.dbg2:
                      fm_tok = fm_tok[:2]
                  for hq in range(4):
                      wt, bw = load_w(w_gla_fm, hq * 256, 256)
                      dst, bdst = (qT, b_qT) if hq < 2 else (kT, b_kT)
                      for h2 in range(2):
                          h = (hq % 2) * 2 + h2
                          for (t0, n) in fm_tok:
                              pp, bpp = nps()
                              for kc in range(16):
                                  P.op('pe', lambda e, pp=pp, wt=wt, kc=kc, h2=h2, t0=t0, n=n: e.matmul(
                                      pp[:, 0:n], lhsT=wt[:, kc, h2 * 128:(h2 + 1) * 128], rhs=hT[:, kc, t0:t0 + n],
                                      start=(kc == 0), stop=(kc == 15)),
                                      reads=[bw] + b_hT, writes=[bpp])
                              evac(dst[:, h, t0 - 1024:t0 - 1024 + n], pp[:, 0:n], [bpp], [bdst],
                                   mul=(128.0 ** -0.5 if hq < 2 else None))
                  wt, bw = load_w(w_gla_fm, 1024, 16)
                  for (t0, n) in [(0, 512), (512, 512), (1024, 512), (1536, 512), (TS, SW)][:4 if 's' in k.dbg2 else 5]:
                      pp, bpp = nps()
                      for kc in range(16):
                          P.op('pe', lambda e, pp=pp, wt=wt, kc=kc, t0=t0, n=n: e.matmul(
                              pp[0:16, 0:n], lhsT=wt[:, kc, 0:16], rhs=hT[:, kc, t0:t0 + n],
                              start=(kc == 0), stop=(kc == 15)), reads=[bw] + b_hT, writes=[bpp])
                      evac(lrT[:, t0:t0 + n], pp[0:16, 0:n], [bpp], [b_lrT])
                  for ci in range(10):
                      wt, bw = load_w(w_gla_tm, ci * 256, 256)
                      for ti, (t0, m) in enumerate(tiles):
                          if ci >= 6 and ti < 8:
                              continue
                          if 's' in k.dbg2 and ti == 16:
                              continue
                          pp, bpp = nps()
                          for kc in range(16):
                              P.op('pe', lambda e, pp=pp, wt=wt, kc=kc, t0=t0, m=m: e.matmul(
                                  pp[0:m, 0:256], lhsT=hT[:, kc, t0:t0 + m], rhs=wt[:, kc, :],
                                  start=(kc == 0), stop=(kc == 15)), reads=[bw, b_hT[ti]], writes=[bpp])
                          if ci < 2:
                              evac(ktok[0:m, ti, ci * 256:(ci + 1) * 256], pp[0:m, 0:256], [bpp], [b_ktok[ti]])
                          elif ci < 6:
                              evac(vtok[0:m, ti, (ci - 2) * 256:(ci - 1) * 256], pp[0:m, 0:256], [bpp], [b_vtok[ti]])
                          else:
                              evac(rtok[0:m, ti - 8, (ci - 6) * 256:(ci - 5) * 256], pp[0:m, 0:256], [bpp], [b_rtok[ti - 8]])

                  S = sb(s1, "g_S", [128, 4, 256], F32)
                  Sb = sb(s1, "g_Sb", [128, 4, 256], BF16)
                  b_S, b_Sb = P.buf('S'), P.buf('Sb')
                  P.dma('sp', S[:], sg.rearrange("h d e -> d h e"), writes=[b_S])
                  P.op('act', lambda e: e.activation(out=Sb[:], in_=S[:], func=AF.Copy), reads=[b_S], writes=[b_Sb])
                  gpos = sb(s1, "g_gpos", [128, 512], F32)
                  ebn = sb(s1, "g_ebn", [128, 512], F32)
                  ktl = sb(s1, "g_ktl", [128, 512], BF16)
                  ebT = sb(s1, "g_ebT", [128, 512], F32)
                  enbT = sb(s1, "g_enbT", [128, 512], F32)
                  qtl = sb(s1, "g_qtl", [128, 512], BF16)
                  ktlT = sb(s1, "g_ktlT", [128, 512], BF16)
                  aT = sb(s1, "g_aT", [128, 512], BF16)
                  osb = sb(s1, "g_osb", [128, 1024], F32)
                  sr = sb(s1, "g_sr", [128, 1024], BF16)
                  ob = sb(s1, "g_ob", [128, 1024], BF16)
                  gst = sb(s1, "g_st", [128, 8], F32)
                  obT = sb(s1, "g_obT", [128, 8, 128], BF16)
                  one1 = sb(s1, "g_one1", [128, 1], F32)
                  P.op('pool', lambda e: e.memset(one1[:], 1.0), writes=[b_const])
                  (b_gpos, b_ebn, b_ktl, b_ebT, b_enbT, b_qtl, b_ktlT, b_aT, b_osb, b_sr, b_ob, b_gst, b_obT,
                   ) = P.bufs('gl', 13)
                  b_obTd = P.buf('obTd')
                  b_out_gla = P.buf('out_gla')

                  chunk_list = [(16, tiles[16])] + list(enumerate(tiles[:16]))
                  if dbg == 'p1a':
                      chunk_list = []
                  elif dbg == 'p1b':
                      chunk_list = chunk_list[:1]
                  elif dbg == 'p1c':
                      chunk_list = chunk_list[1:2]
                  elif dbg == 'p1d':
                      chunk_list = chunk_list[9:10]
                  for ti, (t0, m) in chunk_list:
                      own = ti >= 8
                      samp = ti == 16
                      cS, cSb, bS, bSb = (S, Sb, b_S, b_Sb)
                      pz, bpz = nps()
                      P.op('pe', lambda e, pz=pz, t0=t0, m=m: e.matmul(
                          pz[0:m, :], lhsT=lrT[0:16, t0:t0 + m], rhs=wa2[0:16, :], start=True, stop=False),
                          reads=[b_lrT, b_const], writes=[bpz])
                      P.op('pe', lambda e, pz=pz, m=m: e.matmul(
                          pz[0:m, :], lhsT=ones_bf[0:1, 0:m], rhs=ba[0:1, :], start=False, stop=True),
                          reads=[b_const], writes=[bpz])
                      P.op('act', lambda e, pz=pz, m=m: e.activation(out=gpos[0:m, :], in_=pz[0:m, :], func=AF.Exp,
                                                                      scale=-1.0), reads=[bpz], writes=[b_gpos])
                      P.op('act', lambda e, m=m: e.activation(out=gpos[0:m, :], in_=gpos[0:m, :], func=AF.Ln,
                                                               bias=one1[0:m, 0:1]), reads=[b_gpos, b_const],
                           writes=[b_gpos])
                      pb_, bpb = nps()
                      P.op('pe', lambda e, pb_=pb_, m=m: e.matmul(pb_[0:m, :], lhsT=uneg[0:m, 0:m], rhs=gpos[0:m, :],
                                                                    start=True, stop=True),
                           reads=[b_gpos, b_const], writes=[bpb])
                      pbT, bpbT = nps()
                      for h in range(4):
                          P.op('pe', lambda e, pbT=pbT, m=m, h=h: e.matmul(
                              pbT[:, h * 128:h * 128 + m], lhsT=gpos[0:m, h * 128:(h + 1) * 128], rhs=uneg[0:m, 0:m],
                              start=(h == 0), stop=(h == 3), skip_group_check=True),
                              reads=[b_gpos, b_const], writes=[bpbT])
                      P.op('act', lambda e, pb_=pb_, m=m: e.activation(out=ebn[0:m, :], in_=pb_[0:m, :], func=AF.Exp,
                                                                        scale=-1.0), reads=[bpb], writes=[b_ebn])
                      P.op('dve', lambda e, m=m, ti=ti: e.tensor_tensor(out=ktl[0:m, :], in0=ktok[0:m, ti, :],
                                                                         in1=ebn[0:m, :], op=ALU.mult),
                           reads=[b_ebn, b_ktok[ti]], writes=[b_ktl])
                      pT3 = pbT[:, :].rearrange("p (h c) -> p h c", c=128)
                      ebT3 = ebT[:, :].rearrange("p (h c) -> p h c", c=128)
                      enbT3 = enbT[:, :].rearrange("p (h c) -> p h c", c=128)
                      P.op('act', lambda e, m=m, pT3=pT3, ebT3=ebT3: e.activation(
                          out=ebT3[:, :, 0:m], in_=pT3[:, :, 0:m], func=AF.Exp), reads=[bpbT], writes=[b_ebT])
                      if own:
                          c0 = t0 - 1024
                          P.mute = k.cut < 1
                          P.op('act', lambda e, m=m, pT3=pT3, enbT3=enbT3: e.activation(
                              out=enbT3[:, :, 0:m], in_=pT3[:, :, 0:m], func=AF.Exp, scale=-1.0),
                              reads=[bpbT], writes=[b_enbT])
                          qtl3 = qtl[:, :].rearrange("p (h c) -> p h c", c=128)
                          ktlT3 = ktlT[:, :].rearrange("p (h c) -> p h c", c=128)
                          P.op('dve', lambda e, m=m, c0=c0, qtl3=qtl3, ebT3=ebT3: e.tensor_tensor(
                              out=qtl3[:, :, 0:m], in0=qT[:, :, c0:c0 + m], in1=ebT3[:, :, 0:m], op=ALU.mult),
                              reads=[b_qT, b_ebT], writes=[b_qtl])
                          P.op('dve', lambda e, m=m, c0=c0, ktlT3=ktlT3, enbT3=enbT3: e.tensor_tensor(
                              out=ktlT3[:, :, 0:m], in0=kT[:, :, c0:c0 + m], in1=enbT3[:, :, 0:m], op=ALU.mult),
                              reads=[b_kT, b_enbT], writes=[b_ktlT])
                          P.mute = k.cut < 2
                          pa, bpa = nps()
                          for h in range(4):
                              P.op('pe', lambda e, pa=pa, m=m, h=h: e.matmul(
                                  pa[0:m, h * 128:h * 128 + m], lhsT=ktlT[:, h * 128:h * 128 + m],
                                  rhs=qtl[:, h * 128:h * 128 + m], start=(h == 0), stop=(h == 3),
                                  skip_group_check=True), reads=[b_qtl, b_ktlT], writes=[bpa])
                          pa3 = pa[:, :].rearrange("p (h c) -> p h c", c=128)
                          aT3 = aT[:, :].rearrange("p (h c) -> p h c", c=128)
                          u3 = u01x4[:, :].rearrange("p (h c) -> p h c", c=128)
                          P.op('dve', lambda e, m=m, pa3=pa3, aT3=aT3, u3=u3: e.tensor_tensor(
                              out=aT3[0:m, :, 0:m], in0=pa3[0:m, :, 0:m], in1=u3[0:m, :, 0:m], op=ALU.mult),
                              reads=[bpa, b_const], writes=[b_aT])
                          P.mute = k.cut < 3
                          po = [nps(), nps()]
                          for h in range(4):
                              pp, bpp = po[h // 2]
                              cc = (h % 2) * 256
                              P.op('pe', lambda e, pp=pp, m=m, h=h, cc=cc, ti=ti: e.matmul(
                                  pp[0:m, cc:cc + 256], lhsT=aT[0:m, h * 128:h * 128 + m],
                                  rhs=vtok[0:m, ti, h * 256:(h + 1) * 256], start=(h % 2 == 0), stop=False,
                                  skip_group_check=True), reads=[b_aT, b_vtok[ti]], writes=[bpp])
                              P.op('pe', lambda e, pp=pp, m=m, h=h, cc=cc, cSb=cSb: e.matmul(
                                  pp[0:m, cc:cc + 256], lhsT=qtl[:, h * 128:h * 128 + m], rhs=cSb[:, h, :],
                                  start=False, stop=True, skip_group_check=True), reads=[b_qtl, bSb], writes=[bpp])
                          P.mute = k.cut < 4
                          for h in range(4):
                              pp, bpp = po[h // 2]
                              cc = (h % 2) * 256
                              P.op('dve', lambda e, pp=pp, m=m, h=h, cc=cc: e.tensor_copy(
                                  out=osb[0:m, h * 256:(h + 1) * 256], in_=pp[0:m, cc:cc + 256]),
                                  reads=[bpp], writes=[b_osb])
                              P.op('act', lambda e, pp=pp, m=m, h=h, cc=cc: e.activation(
                                  out=ob[0:m, h * 256:(h + 1) * 256], in_=osb[0:m, h * 256:(h + 1) * 256], func=AF.Square,
                                  accum_out=gst[0:m, h:h + 1]), reads=[b_osb], writes=[b_ob, b_gst])
                          for h in range(4):
                              P.op('act', lambda e, m=m, h=h: e.activation(
                                  out=gst[0:m, 4 + h:5 + h], in_=gst[0:m, h:h + 1], func=AF.Sqrt, scale=1.0 / 256,
                                  bias=epsc[0:m, 0:1]), reads=[b_gst, b_const], writes=[b_gst])
                          P.op('dve', lambda e, m=m: e.reciprocal(out=gst[0:m, 4:8], in_=gst[0:m, 4:8]),
                               reads=[b_gst], writes=[b_gst])
                          P.mute = k.cut < 5
                          P.op('act', lambda e, m=m, ti=ti: e.activation(out=sr[0:m, :], in_=rtok[0:m, ti - 8, :],
                                                                           func=AF.Silu),
                               reads=[b_rtok[ti - 8], b_sr], writes=[b_sr])
                          P.mute = k.cut < 6
                          for h in range(4):
                              P.op('act', lambda e, m=m, h=h: e.activation(
                                  out=osb[0:m, h * 256:(h + 1) * 256], in_=osb[0:m, h * 256:(h + 1) * 256],
                                  func=AF.Copy, scale=gst[0:m, 4 + h:5 + h]), reads=[b_osb, b_gst], writes=[b_osb])
                              P.op('dve', lambda e, m=m, h=h: e.tensor_tensor(
                                  out=osb[0:m, h * 256:(h + 1) * 256], in0=osb[0:m, h * 256:(h + 1) * 256],
                                  in1=gn[0:m, :], op=ALU.mult), reads=[b_osb, b_const], writes=[b_osb])
                          P.op('dve', lambda e, m=m: e.tensor_tensor(out=ob[0:m, :], in0=osb[0:m, :], in1=sr[0:m, :],
                                                                      op=ALU.mult),
                               reads=[b_osb, b_sr], writes=[b_ob])
                          P.mute = k.cut < 7
                          for q4 in range(2):
                              pt_, bpt = nps()
                              for j in range(4):
                                  fc = q4 * 4 + j
                                  P.op('pe', lambda e, m=m, fc=fc, j=j, pt_=pt_: e.matmul(
                                      pt_[:, j * 128:j * 128 + m], lhsT=ob[0:m, fc * 128:(fc + 1) * 128],
                                      rhs=ident[0:m, 0:m], start=True, stop=True),
                                      reads=[b_ob, b_const], writes=[bpt])
                              ptv = pt_[:, :].rearrange("p (j c) -> p j c", c=128)
                              P.op('act', lambda e, m=m, q4=q4, ptv=ptv: e.activation(
                                  out=obT[:, q4 * 4:(q4 + 1) * 4, 0:m], in_=ptv[:, :, 0:m], func=AF.Copy),
                                  reads=[bpt], writes=[b_obT])
                          P.mute = k.cut < 8
                          P.dma('sp', obT_d[:, :, c0:c0 + m].rearrange("f p c -> p f c"), obT[:, :, 0:m],
                                reads=[b_obT], writes=[b_obTd], allow_slow_non_contiguous=(m < 128))
                      P.mute = k.cut < 0
                      for h in range(4):
                          pk, bpk = nps()
                          P.op('pe', lambda e, pk=pk, m=m, h=h, ti=ti: e.matmul(
                              pk[:, 0:256], lhsT=ktl[0:m, h * 128:(h + 1) * 128],
                              rhs=vtok[0:m, ti, h * 256:(h + 1) * 256], start=True, stop=True),
                              reads=[b_ktl, b_vtok[ti]], writes=[bpk])
                          P.op('dve', lambda e, pk=pk, h=h, cS=cS: e.tensor_tensor(
                              out=cS[:, h, :], in0=pk[:, 0:256], in1=cS[:, h, :], op=ALU.add),
                              reads=[bpk, bS], writes=[bS])
                          li = h * 128 + (0 if samp else m - 1)
                          P.op('dve', lambda e, h=h, cS=cS, li=li: e.tensor_scalar(
                              out=cS[:, h, :], in0=cS[:, h, :], scalar1=ebT[:, li:li + 1],
                              scalar2=None, op0=ALU.mult), reads=[b_ebT, bS], writes=[bS])
                      if samp:
                          P.dma('sp', o_sgla.rearrange("h d e -> d h e"), S[:], reads=[b_S], writes=[b_out_gla])
                          P.op('dve', lambda e: e.memset(S[:], 0.0), reads=[b_S], writes=[b_S])
                          P.op('pool', lambda e: e.memset(Sb[:], 0.0), reads=[b_Sb], writes=[b_Sb])
                      else:
                          P.op('act', lambda e, cS=cS, cSb=cSb: e.activation(out=cSb[:], in_=cS[:], func=AF.Copy),
                               reads=[bS], writes=[bSb])
                  P.dma('sp', o_pgla.rearrange("h d e -> d h e"), S[:], reads=[b_S], writes=[b_out_gla])
            P.barrier()
            def MM(out, lhsT, rhs, start, stop, reads, writes):
                P.op('pe', lambda e: e.matmul(out, lhsT=lhsT, rhs=rhs, start=start, stop=stop,
                                              skip_group_check=True), reads=reads, writes=writes)

            def ACT(out, in_, func, reads, writes, **kw):
                P.op('act', lambda e: e.activation(out=out, in_=in_, func=func, **kw), reads=reads, writes=writes)

            def TT(out, a, b, op, reads, writes):
                P.op('dve', lambda e: e.tensor_tensor(out=out, in0=a, in1=b, op=op), reads=reads, writes=writes)

            def TSC(out, a, s1, s2, op0, op1, reads, writes):
                if op1 is None:
                    P.op('dve', lambda e: e.tensor_scalar(out=out, in0=a, scalar1=s1, scalar2=None, op0=op0),
                         reads=reads, writes=writes)
                else:
                    P.op('dve', lambda e: e.tensor_scalar(out=out, in0=a, scalar1=s1, scalar2=s2, op0=op0, op1=op1),
                         reads=reads, writes=writes)

            def CP(out, in_, reads, writes):
                P.op('dve', lambda e: e.tensor_copy(out=out, in_=in_), reads=reads, writes=writes)

            def RCP(out, in_, reads, writes):
                P.op('dve', lambda e: e.reciprocal(out=out, in_=in_), reads=reads, writes=writes)
            ev2 = [0]

            def EV(out, in_, reads, writes):
                ev2[0] += 1
                if ev2[0] % 2:
                    ACT(out, in_, AF.Copy, reads, writes)
                else:
                    CP(out, in_, reads, writes)

            def transpose_store(src, bsrc, m, nfc, dstT, bdstT, dram, c0, bdram):
                for q4 in range(nfc // 4):
                    pt_, bpt = nps(0, 4)
                    for j in range(4):
                        fc = q4 * 4 + j
                        MM(pt_[:, j * 128:j * 128 + m], src[0:m, fc * 128:(fc + 1) * 128], ident[0:m, 0:m], True, True,
                           [bsrc, b_const], [bpt])
                    ptv = pt_[:, :].rearrange("p (j c) -> p j c", c=128)
                    ACT(dstT[:, q4 * 4:(q4 + 1) * 4, 0:m], ptv[:, :, 0:m], AF.Copy, [bpt], [bdstT])
                P.dma('sp', dram[:, :, c0:c0 + m].rearrange("f p c -> p f c"), dstT[:, :, 0:m],
                      reads=[bdstT], writes=[bdram], allow_slow_non_contiguous=(m < 128))

            b_oaTd = P.buf('oaTd')
            b_rows = P.buf('rows_out')
            with ExitStack() as s2:
                w1 = [sb(s2, f"n_w1{t}", [128, 32, 128], BF16) for t in range(2)]
                w2kd = sb(s2, "n_w2kd", [128, 128], BF16)
                w2v = sb(s2, "n_w2v", [128, 64], BF16)
                cvec = sb(s2, "n_cvec", [128, 2], F32)
                bg = sb(s2, "n_bg", [128, 48], BF16)
                e0 = sb(s2, "n_e0", [128, 128], BF16)
                nqTs = sb(s2, "n_qTs", [128, 8, SW], BF16)
                knew = sb(s2, "n_knew", [128, 8, SW], BF16)
                vnew = sb(s2, "n_vnew", [128, 2, 4, 80], BF16)
                gats = sb(s2, "n_gats", [128, 48], F32)
                smc = sb(s2, "n_smc", [128, 2], F32)
                P.dma('sp', smc[:], smc_d[:, :], writes=[b_const])
                b_samp = P.buf('sampbits')
                s2big = ExitStack()
                alloc_w(s2big, 'n', 4096)
                nqT = sb(s2big, "n_qT", [128, 8, 1024 + SW], BF16)
                ksT = sb(s2big, "n_ksT", [128, 4, NT], BF16)
                kwT = sb(s2big, "n_kwT", [128, 4, NT], BF16)
                vS = sb(s2big, "n_vS", [128, 17, 4, 80], BF16)
                vW = sb(s2big, "n_vW", [128, 17, 4, 80], BF16)
                gat = sb(s2big, "n_gat", [128, 9, 48], F32)
                kcT = sb(s2big, "n_kcT", [128, 4, 128], BF16)
                vc = sb(s2big, "n_vc", [128, 4, 80], BF16)
                b_nqT, b_gat, b_kcT, b_vc, b_cvec = P.bufs('nq', 5)
                b_ksT, b_kwT = P.bufs('nksT', 5), P.bufs('nkwT', 5)
                b_vS, b_vW = P.bufs('nvS', 17), P.bufs('nvW', 17)
                P.dma('pool', w1[0][:], w1k_d.rearrange("p (j m) -> p j m", m=128), writes=[b_const])
                P.dma('pool', w1[1][:], w1v_d.rearrange("p (j m) -> p j m", m=128), writes=[b_const])
                P.dma('pool', w2kd[:], w2kd_d[:, :], writes=[b_const])
                P.dma('pool', w2v[:], w2v_d[:, :], writes=[b_const])
                P.op('pool', lambda e: e.memset(bg[:], 0.0), writes=[b_const])
                P.op('pool', lambda e: e.memset(e0[:], 0.0), writes=[b_const])
                P.dma('pool', bg[0:1, :], bgate_d[:, :], writes=[b_const])
                P.dma('pool', e0[0:1, :], ones_d[0:1, :], writes=[b_const])
                P.op('pool', lambda e: e.memset(vS[:], 1.0), writes=b_vS)
                P.op('pool', lambda e: e.memset(vW[:], 1.0), writes=b_vW)
                P.op('pool', lambda e: e.memset(vc[:], 1.0), writes=[b_vc])
                fm5 = [(0, 512), (512, 512), (1024, 512), (1536, 512), (TS, SW)]
                P.mute = 'q' in k.skip
                for L in range(4):
                    wt, bw = load_w(w_q, L * 256, 256)
                    for c2 in range(2):
                        ch = L * 2 + c2
                        for (t0, n) in fm5[2:]:
                            pp, bpp = nps(0, 4)
                            for kc in range(16):
                                MM(pp[:, 0:n], wt[:, kc, c2 * 128:(c2 + 1) * 128], hT[:, kc, t0:t0 + n], kc == 0, kc == 15,
                                   [bw] + b_hT, [bpp])
                            EV(nqT[:, ch, t0 - 1024:t0 - 1024 + n], pp[:, 0:n], [bpp], [b_nqT])
                P.mute = 'kT' in k.skip
                with ExitStack() as s2c:
                    cT = [sb(s2c, f"n_cT{i}", [128, 16, 128], BF16) for i in range(2)]
                    b_cT = P.bufs('ncT', 2)
                    xg = sb(s2c, "n_xg", [128, 128], F32)
                    x2 = sb(s2c, "n_x2", [128, 128], F32)
                    gel = sb(s2c, "n_gel", [128, 128], BF16)
                    b_xg, b_x2, b_gel = P.bufs('ngel', 3)
                    peT = [sb(s2c, f"n_peT{t}", [128, 32, 32], BF16) for t in range(2)]
                    P.dma('pool', peT[0][:], pek_d.rearrange("p (j r) -> p j r", r=32), writes=[b_const])
                    P.dma('pool', peT[1][:], pev_d.rearrange("p (j r) -> p j r", r=32), writes=[b_const])
                    for t in range(2):
                        pp, bpp = nps(0, 4)
                        for j in range(32):
                            MM(pp[:, 0:32], w1[t][0:64, j, :], peT[t][0:64, j, :], j == 0, j == 31, [b_const], [bpp])
                        CP(cvec[:, t:t + 1], pp[:, 0:1], [bpp], [b_cvec])
                    for L in range(8):
                        wt, bw = load_w(w_kT, L * 256, 256)
                        for c2 in range(2):
                            ch = L * 2 + c2
                            typ, g = ch // 4, ch % 4
                            ci = ch % 2
                            for fi, (t0, n) in enumerate(fm5):
                                pp, bpp = nps(0, 4)
                                for kc in range(16):
                                    MM(pp[:, 0:n], wt[:, kc, c2 * 128:(c2 + 1) * 128], hT[:, kc, t0:t0 + n], kc == 0,
                                       kc == 15, [bw] + b_hT, [bpp])
                                if typ < 2:
                                    if fi < 4:
                                        EV(cT[ci][:, :, t0 // 16:(t0 + n) // 16].rearrange("p s c -> p c s"),
                                           pp[:, 0:n].rearrange("p (c s) -> p c s", s=16), [bpp], [b_cT[ci]])
                                elif typ == 2:
                                    EV(ksT[:, g, t0:t0 + n], pp[:, 0:n], [bpp], [b_ksT[fi]])
                                else:
                                    EV(kwT[:, g, t0:t0 + n], pp[:, 0:n], [bpp], [b_kwT[fi]])
                            if typ < 2 and 'cmp' not in k.skip:
                                pp, bpp = nps(0, 4)
                                for j in range(32):
                                    MM(pp[:, 0:127], w1[typ][0:64, j, :], cT[ci][0:64, j % 16, (j // 16):(j // 16) + 127],
                                       j == 0, j == 31, [b_const, b_cT[ci]], [bpp])
                                TSC(xg[:, 0:127], pp[:, 0:127], cvec[:, typ:typ + 1], None, ALU.add, None, [bpp, b_cvec], [b_xg])
                                TT(x2[:, 0:127], xg[:, 0:127], xg[:, 0:127], ALU.mult, [b_xg], [b_x2])
                                TSC(x2[:, 0:127], x2[:, 0:127], 0.044715, 1.0, ALU.mult, ALU.add, [b_x2], [b_x2])
                                TT(x2[:, 0:127], x2[:, 0:127], xg[:, 0:127], ALU.mult, [b_x2, b_xg], [b_x2])
                                ACT(x2[:, 0:127], x2[:, 0:127], AF.Exp, [b_x2], [b_x2], scale=-1.5957691216057308)
                                TSC(x2[:, 0:127], x2[:, 0:127], 1.0, None, ALU.add, None, [b_x2], [b_x2])
                                RCP(x2[:, 0:127], x2[:, 0:127], [b_x2], [b_x2])
                                TT(gel[:, 0:127], x2[:, 0:127], xg[:, 0:127], ALU.mult, [b_x2, b_xg], [b_gel])
                                pp, bpp = nps(0, 4)
                                if typ == 0:
                                    MM(pp[:, 0:127], w2kd[:, :], gel[:, 0:127], True, True, [b_const, b_gel], [bpp])
                                    EV(kcT[:, g, 0:127], pp[:, 0:127], [bpp], [b_kcT])
                                else:
                                    MM(pp[0:127, 0:64], gel[:, 0:127], w2v[:, :], True, True, [b_const, b_gel], [bpp])
                                    EV(vc[0:127, g, 0:64], pp[0:127, 0:64], [bpp], [b_vc])
                P.mute = False
                P.barrier()
                P.mute = 'rows' in k.skip
                with ExitStack() as s2r:
                    rows = [sb(s2r, f"n_rows{i}", [128, 256], F32) for i in range(3)]
                    b_rw = P.bufs('nrows', 3)
                    cwt = [sb(s2r, f"n_cwt{i}", [128, 4, 256], F32) for i in range(2)]
                    b_cwt = P.bufs('ncwt', 2)
                    for i_, (src_, dst_) in enumerate(((cwk, o_swk), (cwv, o_swv))):
                        P.dma('sp', cwt[i_][:, :, :], src_.rearrange("(t p) n -> p t n", p=128), writes=[b_cwt[i_]])
                        P.dma('sp', dst_[0:127, :], cwt[i_][1:128, 0, :], reads=[b_cwt[i_]], writes=[b_rows])
                        for t_ in range(1, 4):
                            P.dma('sp', dst_[t_ * 128 - 1:t_ * 128 + 127, :], cwt[i_][:, t_, :], reads=[b_cwt[i_]],
                                  writes=[b_rows])
                    rr = 0
                    for L in range(6):
                        wt, bw = load_w(w_kv, L * 256, 256)
                        for ti, (t0, m) in enumerate(tiles):
                            if ti < 8 and L not in (3, 5):
                                continue
                            pp, bpp = nps(0, 4)
                            for kc in range(16):
                                MM(pp[0:m, 0:256], hT[:, kc, t0:t0 + m], wt[:, kc, :], kc == 0, kc == 15, [bw, b_hT[ti]], [bpp])
                            if L in (3, 5):
                                dstv, bdv = (vS, b_vS) if L == 3 else (vW, b_vW)
                                for g_ in range(4):
                                    EV(dstv[0:m, ti, g_, 0:64], pp[0:m, g_ * 64:(g_ + 1) * 64], [bpp], [bdv[ti]])
                            if ti >= 8:
                                i = rr % 3
                                rr += 1
                                CP(rows[i][0:m, :], pp[0:m, 0:256], [bpp], [b_rw[i]])
                                if ti < 16:
                                    P.dma('sp', o_rows[t0 - 1024:t0 - 1024 + m, L * 256:(L + 1) * 256], rows[i][0:m, :],
                                          reads=[b_rw[i]], writes=[b_rows])
                                else:
                                    P.dma('sp', o_rows_s[0:m, L * 256:(L + 1) * 256], rows[i][0:m, :],
                                          reads=[b_rw[i]], writes=[b_rows])
                                    if L >= 4:
                                        P.dma('sp', (o_swk if L == 4 else o_swv)[511:512, :], rows[i][0:1, :],
                                              reads=[b_rw[i]], writes=[b_rows])
                P.mute = False
                P.barrier()
                P.mute = 'gates' in k.skip
                wt, bw = load_w(w_g, 0, 48)
                for ti, (t0, m) in enumerate(tiles):
                    if ti < 8:
                        continue
                    pp, bpp = nps(0, 4)
                    for kc in range(16):
                        MM(pp[0:m, 0:48], hT[:, kc, t0:t0 + m], wt[:, kc, :], kc == 0, False, [bw, b_hT[ti]], [bpp])
                    MM(pp[0:m, 0:48], e0[:, 0:m], bg[:, :], False, True, [b_const], [bpp])
                    ACT(gat[0:m, ti - 8, :], pp[0:m, 0:48], AF.Sigmoid, [bpp], [b_gat])

                P.mute = False
                oacc = sb(s2big, "n_oacc", [128, 1024], F32)
                otmp = sb(s2big, "n_otmp", [128, 64], F32)
                oab = sb(s2big, "n_oab", [128, 1024], BF16)
                oaT = sb(s2big, "n_oaT", [128, 8, 128], BF16)
                dn = sb(s2big, "n_dn", [128, 8], F32)
                pT = [[sb(s2big, f"n_pT{i}{h}", [128, 256], BF16) for h in range(2)] for i in range(2)]
                b_pT = [P.bufs(f'npT{i}', 2) for i in range(2)]
                b_oacc, b_otmp, b_oab, b_oaT, b_dn = P.bufs('noa', 5)
                pt_rr = [0]
                gat3 = gat[:, :, :].rearrange("p t (h b) -> p t h b", b=3)

                def finalize(accs, nq, g, tix, br, first):
                    for s_ in range(4):
                        pa_, bpa_ = accs[s_]
                        TSC(dn[0:nq, s_:s_ + 1], pa_[0:nq, 64:65], 1e-30, None, ALU.max, None, [bpa_], [b_dn])
                    RCP(dn[0:nq, 0:4], dn[0:nq, 0:4], [b_dn], [b_dn])
                    TT(dn[0:nq, 4:8], dn[0:nq, 0:4], gat3[0:nq, tix, 4 * g:4 * g + 4, br], ALU.mult, [b_dn, b_gat], [b_dn])
                    for s_ in range(4):
                        pa_, bpa_ = accs[s_]
                        col = (4 * g + s_) * 64
                        if first:
                            ACT(oacc[0:nq, col:col + 64], pa_[0:nq, 0:64], AF.Copy, [bpa_, b_dn], [b_oacc],
                                scale=dn[0:nq, 4 + s_:5 + s_])
                        else:
                            ACT(otmp[0:nq, :], pa_[0:nq, 0:64], AF.Copy, [bpa_, b_dn], [b_otmp],
                                scale=dn[0:nq, 4 + s_:5 + s_])
                            TT(oacc[0:nq, col:col + 64], oacc[0:nq, col:col + 64], otmp[0:nq, :], ALU.add,
                               [b_oacc, b_otmp], [b_oacc])

                def pv_tile(accs_, outs, V, bV, kt, g, first, last):
                    for s_ in range(4):
                        _, _, pt2, bpt2 = outs[s_ // 2]
                        MM(accs_[s_][0][0:128, 0:65], pt2[:, (s_ % 2) * 128:(s_ % 2 + 1) * 128], V[:, kt, g, 0:65],
                           first, last, [bpt2, bV[kt]], [accs_[s_][1]])

                def qk_tile(g, qc0, nq, KT, nk, bKT, bias=None):
                    i = pt_rr[0] % 2
                    pt_rr[0] += 1
                    outs = []
                    for h in range(2):
                        pp, bpp = ps[h * 2 + i], b_ps[h * 2 + i]
                        rhs = nqT[h * 64:(h + 1) * 64, 2 * g:2 * g + 2, qc0:qc0 + nq]
                        if bias is not None:
                            bl, br_, bb = bias
                            MM(pp[0:nk, 0:2 * nq], bl[h * 64:(h + 1) * 64, :], br_[h * 64:(h + 1) * 64, :], True, False, bb, [bpp])
                        MM(pp[0:nk, 0:2 * nq], KT[h * 64:(h + 1) * 64, :], rhs, bias is None, True, [b_nqT] + bKT, [bpp])
                        outs.append((pp, bpp, pT[i][h], b_pT[i][h]))
                    return outs

                if 'nsap' not in k.skip:
                  with ExitStack() as s2p:
                    esel = sb(s2p, "n_esel", [128, 16, 128], BF16)
                    tri = sb(s2p, "n_tri", [128, 2, 256], BF16)
                    cmk = sb(s2p, "n_cmk", [128, 8, 128], F32)
                    amat = sb(s2p, "n_amat", [128, 32], F32)
                    fbias = sb(s2p, "n_fbias", [128, 8, 32], F32)
                    pkc = sb(s2p, "n_pkc", [128, 4], F32)
                    ef = sb(s2p, "n_ef", [128, 512], F32)
                    eb = sb(s2p, "n_eb", [128, 512], BF16)
                    rden = sb(s2p, "n_rden", [128, 512], F32)
                    sc = sb(s2p, "n_sc", [128, 32], F32)
                    sc2 = sb(s2p, "n_sc2", [128, 32], F32)
                    m8 = sb(s2p, "n_m8", [128, 16], F32)
                    selbb = sb(s2p, "n_selbb", [128, 128], BF16)
                    selbT = sb(s2p, "n_selbT", [128, 256], BF16)
                    b_ef, b_eb, b_rden, b_sc, b_sc2, b_m8, b_selbb, b_selbT = P.bufs('npa', 8)
                    P.dma('pool', esel[:], esel_d.rearrange("p (t k) -> p t k", k=128), writes=[b_const])
                    P.dma('pool', tri[:], tri_d.rearrange("p (t k) -> p t k", k=256), writes=[b_const])
                    P.dma('sp', cmk[:], cmk_d.rearrange("p (t k) -> p t k", k=128), writes=[b_const])
                    P.dma('sp', amat[:], amat_d[:, :], writes=[b_const])
                    P.dma('sp', fbias[:], fbias_d.rearrange("p (t k) -> p t k", k=32), writes=[b_const])
                    P.dma('sp', pkc[:], pkc_d[:, :], writes=[b_const])
                    P.op('pool', lambda e: e.memset(selbb[:], 0.0), writes=[b_selbb])
                    accs = [(ps[4 + s_], b_ps[4 + s_]) for s_ in range(4)]
                    for qt in range(8 if k.nqt is None else k.nqt):
                        qc0 = qt * 128
                        for g in range(4):
                            outs = qk_tile(g, qc0, 128, kcT[:, g, 0:127], 127, [b_kcT])
                            for h, (pp, bpp, _, _) in enumerate(outs):
                                ACT(ef[0:127, h * 256:(h + 1) * 256], pp[0:127, 0:256], AF.Exp, [bpp, b_const], [b_ef],
                                    scale=0.125, bias=pkc[0:127, 0:1])
                            ef3_ = ef[0:127, :].rearrange("p (s q) -> p s q", q=128)
                            TT(ef3_, ef3_, cmk[0:127, qt:qt + 1, :].broadcast_to([127, 4, 128]), ALU.mult, [b_ef, b_const], [b_ef])
                            CP(eb[0:127, :], ef[0:127, :], [b_ef], [b_eb])
                            pd, bpd = nps(0, 4)
                            MM(pd[:, 0:512], ones_bf[0:127, 0:128], eb[0:127, :], True, True, [b_const, b_eb], [bpd])
                            TSC(rden[0:127, :], pd[0:127, :], 1e-30, None, ALU.max, None, [bpd], [b_rden])
                            RCP(rden[0:127, :], rden[0:127, :], [b_rden], [b_rden])
                            TT(ef[0:127, :], ef[0:127, :], rden[0:127, :], ALU.mult, [b_ef, b_rden], [b_ef])
                            pi_, bpi = nps(0, 4)
                            for s_ in range(4):
                                MM(pi_[:, 0:32], ef[0:127, s_ * 128:(s_ + 1) * 128], amat[0:127, :], s_ == 0, s_ == 3,
                                   [b_ef, b_const], [bpi])
                            TT(sc[:, :], pi_[:, 0:32], fbias[:, qt, :], ALU.add, [bpi, b_const], [b_sc])
                            P.op('dve', lambda e: e.max(out=m8[:, 0:8], in_=sc[:, :]), reads=[b_sc], writes=[b_m8])
                            P.op('dve', lambda e: e.match_replace(out=sc2[:, :], in_to_replace=m8[:, 0:8], in_values=sc[:, :],
                                                                  imm_value=-3e38), reads=[b_sc, b_m8], writes=[b_sc2])
                            P.op('dve', lambda e: e.max(out=m8[:, 8:16], in_=sc2[:, :]), reads=[b_sc2], writes=[b_m8])
                            TSC(sc2[:, :], sc[:, :], m8[:, 15:16], None, ALU.is_ge, None, [b_sc, b_m8], [b_sc2])
                            TSC(selbb[:, 0:32], sc2[:, :], -1.0, 30000.0, ALU.add, ALU.mult, [b_sc2], [b_selbb])
                            TSC(selbb[:, 64:96], sc2[:, :], -1.0, 30000.0, ALU.add, ALU.mult, [b_sc2], [b_selbb])
                            pt_, bpt = nps(0, 4)
                            MM(pt_[:, 0:128], selbb[:, :], ident[:, :], True, True, [b_selbb, b_const], [bpt])
                            for r_ in range(2):
                                TSC(selbT[:, r_ * 128:(r_ + 1) * 128], pt_[:, 0:128], pkc[:, 1:2], None, ALU.add, None,
                                   [bpt, b_const], [b_selbT])
                            for s_ in range(4):
                                MM(accs[s_][0][0:128, 0:65], eb[0:127, s_ * 128:(s_ + 1) * 128], vc[0:127, g, 0:65], True, True,
                                   [b_eb, b_vc], [accs[s_][1]])
                            finalize(accs, 128, g, qt, 0, True)
                            pend = None
                            for ix, kt in enumerate(range(qt + 4, qt + 9)):
                                outs = qk_tile(g, qc0, 128, kwT[:, g, kt * 128:(kt + 1) * 128], 128, [b_kwT[kt // 4]])
                                for h, (pp, bpp, pt2, bpt2) in enumerate(outs):
                                    ACT(pt2[:, :], pp[:, 0:256], AF.Exp, [bpp], [bpt2], scale=0.125)
                                    if ix == 0:
                                        TT(pt2[:, :], pt2[:, :], tri[:, 1, :], ALU.mult, [bpt2, b_const], [bpt2])
                                    if ix == 4:
                                        TT(pt2[:, :], pt2[:, :], tri[:, 0, :], ALU.mult, [bpt2, b_const], [bpt2])
                                    if kt < 8:
                                        TSC(pt2[:, :], pt2[:, :], pkc[:, 2:3], None, ALU.mult, None, [bpt2, b_const], [bpt2])
                                if pend is not None:
                                    pv_tile(accs, pend[0], vW, b_vW, pend[1], g, pend[2] == 0, False)
                                pend = (outs, kt, ix)
                            pv_tile(accs, pend[0], vW, b_vW, pend[1], g, False, True)
                            finalize(accs, 128, g, qt, 2, False)
                            nkt = 9 + qt
                            pend = None
                            for kt in range(nkt):
                                outs = qk_tile(g, qc0, 128, ksT[:, g, kt * 128:(kt + 1) * 128], 128, [b_ksT[kt // 4]],
                                               bias=(esel[:, kt, :], selbT[:, :], [b_const, b_selbT]))
                                for h, (pp, bpp, pt2, bpt2) in enumerate(outs):
                                    ACT(pt2[:, :], pp[:, 0:256], AF.Exp, [bpp], [bpt2], scale=0.125)
                                    if kt == nkt - 1:
                                        TT(pt2[:, :], pt2[:, :], tri[:, 0, :], ALU.mult, [bpt2, b_const], [bpt2])
                                if pend is not None:
                                    pv_tile(accs, pend[0], vS, b_vS, pend[1], g, pend[1] == 0, False)
                                pend = (outs, kt)
                            pv_tile(accs, pend[0], vS, b_vS, pend[1], g, nkt == 1, True)
                            finalize(accs, 128, g, qt, 1, False)
                        CP(oab[:, :], oacc[:, :], [b_oacc], [b_oab])
                        transpose_store(oab, b_oab, 128, 8, oaT, b_oaT, oaT_d, qc0, b_oaTd)
                P.mute = False
                CP(nqTs[:, :, :], nqT[:, :, 1024:1024 + SW], [b_nqT], [b_samp])
                for g_ in range(4):
                    CP(knew[:, g_, :], ksT[:, g_, TS:TS + SW], [b_ksT[4]], [b_samp])
                    CP(knew[:, 4 + g_, :], kwT[:, g_, TS:TS + SW], [b_kwT[4]], [b_samp])
                    TSC(vnew[0:SW, 0, g_, 0:65], vS[0:SW, 16, g_, 0:65], smc[0:SW, 0:1], None, ALU.mult, None,
                        [b_vS[16], b_const], [b_samp])
                    TSC(vnew[0:SW, 1, g_, 0:65], vW[0:SW, 16, g_, 0:65], smc[0:SW, 0:1], None, ALU.mult, None,
                        [b_vW[16], b_const], [b_samp])
                CP(gats[0:SW, :], gat[0:SW, 8, :], [b_gat], [b_samp])
                P.barrier()
                s2big.close()
                if 'nsas' not in k.skip:
                  with ExitStack() as s2s:
                    As = sb(s2s, "s_As", [128, 8, 257], F32)
                    fbs = sb(s2s, "s_fbs", [128, 257], F32)
                    rsel = sb(s2s, "s_rsel", [128, 2, 128], F32)
                    gsel = sb(s2s, "s_gsel", [128, 4, 128], F32)
                    sel0 = sb(s2s, "s_sel0", [128, 32], F32)
                    piota = sb(s2s, "s_piota", [128, 128], F32)
                    pti = sb(s2s, "s_pti", [128, 128], I32)
                    ptf = sb(s2s, "s_ptf", [128, 128], F32)
                    idx = sb(s2s, "s_idx", [128, 128], I32)
                    b_idx = P.buf('sidx')
                    P.dma('sp', As[:], As_d.rearrange("p (s j) -> p s j", j=257), writes=[b_const])
                    P.dma('sp', fbs[:], fbs_d[:, :], writes=[b_const])
                    P.dma('sp', rsel[:], rsel_d.rearrange("p (r k) -> p r k", k=128), writes=[b_const])
                    P.dma('sp', gsel[:], gsel_d.rearrange("p (r k) -> p r k", k=128), writes=[b_const])
                    P.dma('sp', sel0[:], sel0_d[:, :], writes=[b_const])
                    P.dma('sp', piota[:], piota_d[:, :], writes=[b_const])
                    P.dma('sp', pti[:], pt_d[:, :], writes=[b_idx])
                    CP(ptf[:, :], pti[:, :], [b_idx], [b_idx])
                    TSC(ptf[:, :], ptf[:, :], 128.0, None, ALU.mult, None, [b_idx], [b_idx])
                    TT(ptf[:, :], ptf[:, :], piota[:, :], ALU.add, [b_idx, b_const], [b_idx])
                    CP(idx[:, :], ptf[:, :], [b_idx], [b_idx])
                    segb = sb(s2s, "s_seg", [128, 2, 16, 513], BF16)
                    b_seg = P.buf('sseg')
                    halo = [sb(s2s, f"s_halo{t}", [128, 2, 16, 1], BF16) for t in range(2)]
                    b_halo = P.bufs('shalo', 2)
                    pg = [sb(s2s, f"s_pg{i}", [128, 256], F32) for i in range(NPG)]
                    b_pg = P.bufs('spg', NPG)
                    pgb = [sb(s2s, f"s_pgb{i}", [128, 256], BF16) for i in range(4)]
                    b_pgb = P.bufs('spgb', 4)
                    pgb_rr = [0]
                    kcTs = sb(s2s, "s_kcTs", [128, 4, 1024], BF16)
                    vcs = sb(s2s, "s_vcs", [128, 8, 4, 80], BF16)
                    b_kcTs, b_vcs = P.buf('skcTs'), P.buf('svcs')
                    sxg = sb(s2s, "s_xg", [128, 512], F32)
                    sx2 = sb(s2s, "s_x2", [128, 512], F32)
                    sgel = sb(s2s, "s_gel", [128, 512], BF16)
                    b_sxg, b_sx2, b_sgel = P.bufs('sgel', 3)
                    P.op('pool', lambda e: e.memset(vcs[:], 1.0), writes=[b_vcs])
                    pg_rr = [0]

                    def gather(cache, page):
                        i = pg_rr[0] % NPG
                        pg_rr[0] += 1

                        def fn(e, i=i, page=page, cache=cache):
                            return e.indirect_dma_start(out=pg[i][:, :], out_offset=None, in_=cache[:, :],
                                                        in_offset=bass.IndirectOffsetOnAxis(ap=idx[:, page:page + 1], axis=0))
                        P.dma_fn('pool', fn, reads=[b_idx], writes=[b_pg[i]])
                        return pg[i], b_pg[i]

                    nseg = 8
                    tile_nk = [128, 128, 128, 127, 128, 128, 128, 128]
                    for sg_ in range(2):
                        b0, nb = (1, 511) if sg_ == 0 else (0, 512)
                        for t in range(2):
                            if sg_ > 0:
                                CP(segb[:, :, :, 0:1], halo[t][:, :, :, :], [b_halo[t]], [b_seg])
                            for pgi in range(64):
                                page = sg_ * 64 + pgi
                                src, bsrc = gather(ccK if t == 0 else ccV, page)
                                j = pgb_rr[0] % 4
                                pgb_rr[0] += 1
                                EV(pgb[j][:, :], src[:, :], [bsrc], [b_pgb[j]])
                                pt_, bpt = nps(0, 4)
                                for pr in range(2):
                                    MM(pt_[:, pr * 128:(pr + 1) * 128], pgb[j][:, pr * 128:(pr + 1) * 128], ident[:, :], True, True,
                                       [b_pgb[j], b_const], [bpt])
                                EV(segb[:, :, :, 1 + pgi * 8:9 + pgi * 8].rearrange("p r s c -> p r c s"),
                                   pt_[:, 0:256].rearrange("p (r c s) -> p r c s", r=2, s=16), [bpt], [b_seg])
                            if sg_ == 0:
                                CP(halo[t][:, :, :, :], segb[:, :, :, 512:513], [b_seg], [b_halo[t]])
                            for g in range(4):
                                pr, hh = g // 2, g % 2
                                pp, bpp = nps(0, 4)
                                for j in range(32):
                                    MM(pp[:, 0:nb], w1[t][hh * 64:(hh + 1) * 64, j, :],
                                       segb[hh * 64:(hh + 1) * 64, pr, j % 16, b0 + j // 16:b0 + j // 16 + nb], j == 0, j == 31,
                                       [b_const, b_seg], [bpp])
                                xg_, x2_, gl_ = sxg, sx2, sgel
                                bxg_, bx2_, bgl_ = b_sxg, b_sx2, b_sgel
                                TSC(xg_[:, 0:nb], pp[:, 0:nb], cvec[:, t:t + 1], None, ALU.add, None, [bpp, b_cvec], [bxg_])
                                TT(x2_[:, 0:nb], xg_[:, 0:nb], xg_[:, 0:nb], ALU.mult, [bxg_], [bx2_])
                                TSC(x2_[:, 0:nb], x2_[:, 0:nb], 0.044715, 1.0, ALU.mult, ALU.add, [bx2_], [bx2_])
                                TT(x2_[:, 0:nb], x2_[:, 0:nb], xg_[:, 0:nb], ALU.mult, [bx2_, bxg_], [bx2_])
                                ACT(x2_[:, 0:nb], x2_[:, 0:nb], AF.Exp, [bx2_], [bx2_], scale=-1.5957691216057308)
                                TSC(x2_[:, 0:nb], x2_[:, 0:nb], 1.0, None, ALU.add, None, [bx2_], [bx2_])
                                RCP(x2_[:, 0:nb], x2_[:, 0:nb], [bx2_], [bx2_])
                                TT(gl_[:, 0:nb], x2_[:, 0:nb], xg_[:, 0:nb], ALU.mult, [bx2_, bxg_], [bgl_])
                                if t == 0:
                                    pp, bpp = nps(0, 4)
                                    MM(pp[:, 0:nb], w2kd[:, :], gl_[:, 0:nb], True, True, [b_const, bgl_], [bpp])
                                    EV(kcTs[:, g, sg_ * 512:sg_ * 512 + nb], pp[:, 0:nb], [bpp], [b_kcTs])
                                else:
                                    for c_ in range(4):
                                        nk_ = tile_nk[sg_ * 4 + c_]
                                        pp, bpp = nps(0, 4)
                                        MM(pp[0:nk_, 0:64], gl_[:, c_ * 128:c_ * 128 + nk_], w2v[:, :], True, True,
                                           [b_const, bgl_], [bpp])
                                        EV(vcs[0:nk_, sg_ * 4 + c_, g, 0:64], pp[0:nk_, 0:64], [bpp], [b_vcs])

                    pTa = [sb(s2s, f"s_pTa{i}", [128, 4, 128], BF16) for i in range(2)]
                    b_pTa = P.bufs('spTa', 2)
                    efa = sb(s2s, "s_efa", [128, 8, 4, 128], F32)
                    b_efa = P.buf('sefa')
                    accb = [sb(s2s, f"s_acc{i}", [128, 260], F32) for i in range(3)]
                    b_accb = P.bufs('sacc', 3)
                    st_rr = [0]

                    s_pend = [None]

                    def s_flush():
                        if s_pend[0] is not None:
                            s_pend[0]()
                            s_pend[0] = None

                    def s_tile(KT_of_g, nk, bKT, V_of_g, bV, acc_i, first, capture=None):
                        i = st_rr[0] % 2
                        st_rr[0] += 1
                        for h in range(2):
                            pp, bpp = ps[h * 2 + i], b_ps[h * 2 + i]
                            for g in range(4):
                                MM(pp[0:nk, g * 64:(g + 1) * 64], KT_of_g(g)[h * 64:(h + 1) * 64, :],
                                   nqTs[h * 64:(h + 1) * 64, 2 * g:2 * g + 2, :], True, True, [b_samp] + bKT, [bpp])
                            ppv = pp[0:nk, 0:256].rearrange("p (g c) -> p g c", c=64)
                            if capture is None:
                                ACT(pTa[i][0:nk, :, h * 64:(h + 1) * 64], ppv, AF.Exp, [bpp], [b_pTa[i]], scale=0.125)
                            else:
                                ACT(efa[0:nk, capture, :, h * 64:(h + 1) * 64], ppv, AF.Exp, [bpp], [b_efa], scale=0.125)
                        if capture is not None:
                            CP(pTa[i][0:nk, :, :], efa[0:nk, capture, :, :], [b_efa], [b_pTa[i]])
                        s_flush()

                        def pv_part(i=i, nk=nk, V_of_g=V_of_g, bV=bV, acc_i=acc_i, first=first):
                            pv, bpv = ps[4 + i], b_ps[4 + i]
                            for g in range(4):
                                MM(pv[:, g * 65:(g + 1) * 65], pTa[i][0:nk, g, :], V_of_g(g), True, True, [b_pTa[i]] + bV, [bpv])
                            if first:
                                CP(accb[acc_i][:, :], pv[:, 0:260], [bpv], [b_accb[acc_i]])
                            else:
                                TT(accb[acc_i][:, :], accb[acc_i][:, :], pv[:, 0:260], ALU.add, [bpv, b_accb[acc_i]],
                                   [b_accb[acc_i]])
                        s_pend[0] = pv_part

                    gT = sb(s2s, "s_gT", [128, 12], F32)
                    oaccs = sb(s2s, "s_oacc", [128, 256], F32)
                    otmps = sb(s2s, "s_otmp", [128, 64], F32)
                    dnc = sb(s2s, "s_dnc", [128, 8], F32)
                    b_gT, b_oaccs, b_otmps, b_dnc = P.bufs('sfin', 4)
                    gats3 = gats[:, :].rearrange("p (h b) -> p h b", b=3)
                    pg_, bpg_ = nps(0, 4)
                    for slot in range(4):
                        MM(pg_[:, 0:12], gsel[0:32, slot, :], gats3[0:32, slot:16:4, :], slot == 0, slot == 3, [b_const, b_samp], [bpg_])
                    CP(gT[:, :], pg_[:, 0:12], [bpg_], [b_gT])
                    gT3 = gT[:, :].rearrange("p (g b) -> p g b", b=3)

                    def fin_s(acc_i, br, first):
                        for g in range(4):
                            TSC(dnc[:, g:g + 1], accb[acc_i][:, g * 65 + 64:g * 65 + 65], 1e-30, None, ALU.max, None,
                                [b_accb[acc_i]], [b_dnc])
                        RCP(dnc[:, 0:4], dnc[:, 0:4], [b_dnc], [b_dnc])
                        TT(dnc[:, 4:8], dnc[:, 0:4], gT3[:, :, br], ALU.mult, [b_dnc, b_gT], [b_dnc])
                        for g in range(4):
                            if first:
                                ACT(oaccs[:, g * 64:(g + 1) * 64], accb[acc_i][:, g * 65:g * 65 + 64], AF.Copy,
                                    [b_accb[acc_i], b_dnc], [b_oaccs], scale=dnc[:, 4 + g:5 + g])
                            else:
                                ACT(otmps[:, :], accb[acc_i][:, g * 65:g * 65 + 64], AF.Copy, [b_accb[acc_i], b_dnc], [b_otmps],
                                    scale=dnc[:, 4 + g:5 + g])
                                TT(oaccs[:, g * 64:(g + 1) * 64], oaccs[:, g * 64:(g + 1) * 64], otmps[:, :], ALU.add,
                                   [b_oaccs, b_otmps], [b_oaccs])

                    for s_ in range(nseg):
                        nk = tile_nk[s_]
                        s_tile(lambda g, s_=s_, nk=nk: kcTs[:, g, s_ * 128:s_ * 128 + nk], nk, [b_kcTs],
                               lambda g, s_=s_, nk=nk: vcs[0:nk, s_, g, 0:65], [b_vcs], 0, s_ == 0, capture=s_)
                    s_flush()
                    fin_s(0, 0, True)
                    usb = sb(s2s, "s_usb", [128, 257], F32)
                    lhsg = sb(s2s, "s_lhsg", [128, 32], F32)
                    scs = sb(s2s, "s_scs", [128, 257], F32)
                    scs2 = sb(s2s, "s_scs2", [128, 257], F32)
                    m8s = sb(s2s, "s_m8s", [128, 16], F32)
                    sels = sb(s2s, "s_sels", [128, 257], F32)
                    m01 = sb(s2s, "s_m01", [128, 4, 128], F32)
                    b_usb, b_lhsg, b_scs, b_scs2, b_m8s, b_sels, b_m01 = P.bufs('simp', 7)
                    for g in range(4):
                        pu, bpu = ps[6], b_ps[6]
                        for s_ in range(nseg):
                            nk = tile_nk[s_]
                            MM(pu[:, 0:257], efa[0:nk, s_, g, :], As[0:nk, s_, :], s_ == 0, s_ == nseg - 1, [b_efa, b_const], [bpu])
                        CP(usb[:, :], pu[:, 0:257], [bpu], [b_usb])
                        TSC(dnc[:, 0:1], accb[0][:, g * 65 + 64:g * 65 + 65], 1e-30, None, ALU.max, None, [b_accb[0]], [b_dnc])
                        RCP(dnc[:, 0:1], dnc[:, 0:1], [b_dnc], [b_dnc])
                        TSC(lhsg[:, :], sel0[:, :], dnc[:, 0:1], None, ALU.mult, None, [b_dnc, b_const], [b_lhsg])
                        pim, bpim = ps[7], b_ps[7]
                        MM(pim[0:32, 0:257], lhsg[:, :], usb[:, :], True, True, [b_lhsg, b_usb], [bpim])
                        TT(scs[0:32, :], pim[0:32, 0:257], fbs[0:32, :], ALU.add, [bpim, b_const], [b_scs])
                        P.op('dve', lambda e: e.max(out=m8s[0:32, 0:8], in_=scs[0:32, :]), reads=[b_scs], writes=[b_m8s])
                        P.op('dve', lambda e: e.match_replace(out=scs2[0:32, :], in_to_replace=m8s[0:32, 0:8],
                                                              in_values=scs[0:32, :], imm_value=-3e38),
                             reads=[b_scs, b_m8s], writes=[b_scs2])
                        P.op('dve', lambda e: e.max(out=m8s[0:32, 8:16], in_=scs2[0:32, :]), reads=[b_scs2], writes=[b_m8s])
                        TSC(sels[0:32, :], scs[0:32, :], m8s[0:32, 15:16], None, ALU.is_ge, None, [b_scs, b_m8s], [b_sels])
                        pmk, bpmk = ps[6], b_ps[6]
                        MM(pmk[:, 0:128], rsel[0:32, 0, :], sels[0:32, 0:256:2], True, False, [b_const, b_sels], [bpmk])
                        MM(pmk[:, 0:128], rsel[0:32, 1, :], sels[0:32, 1:256:2], False, True, [b_const, b_sels], [bpmk])
                        CP(m01[:, g, :], pmk[:, 0:128], [bpmk], [b_m01])

                    kpd = [sb(s2s, f"s_kpd{i}", [128, 4, 128], BF16) for i in range(3)]
                    kTs = [sb(s2s, f"s_kTs{i}", [128, 4, 128], BF16) for i in range(3)]
                    vpg = [sb(s2s, f"s_vpg{i}", [128, 4, 80], BF16) for i in range(3)]
                    b_kpd, b_kTs, b_vpg = P.bufs('skpd', 3), P.bufs('skTs', 3), P.bufs('svpg', 3)
                    for i in range(3):
                        P.op('pool', lambda e, i=i: e.memset(vpg[i][:], 1.0), writes=[b_vpg[i]])
                    pt_rr2 = [0]

                    def page_prep(Ksrc, bK, Vsrc, bV, page):
                        i = pt_rr2[0] % 3
                        pt_rr2[0] += 1
                        k3 = Ksrc[:, :].rearrange("p (g d) -> p g d", d=64)
                        EV(kpd[i][:, :, 0:64], k3, [bK], [b_kpd[i]])
                        EV(kpd[i][:, :, 64:128], k3, [bK], [b_kpd[i]])
                        pt_, bpt = ps[6], b_ps[6]
                        for g in range(4):
                            MM(pt_[:, g * 128:(g + 1) * 128], kpd[i][:, g, :], ident[:, :], True, True, [b_kpd[i], b_const], [bpt])
                        EV(kTs[i][:, :, :], pt_[:, :].rearrange("p (g c) -> p g c", c=128), [bpt], [b_kTs[i]])
                        if page is None:
                            EV(vpg[i][:, :, 0:64], Vsrc[:, :].rearrange("p (g d) -> p g d", d=64), [bV], [b_vpg[i]])
                        else:
                            TT(vpg[i][:, :, 0:64], Vsrc[:, :].rearrange("p (g d) -> p g d", d=64),
                               m01[:, :, page:page + 1].broadcast_to([128, 4, 64]), ALU.mult, [bV, b_m01], [b_vpg[i]])
                            CP(vpg[i][:, :, 64], m01[:, :, page], [b_m01], [b_vpg[i]])
                        return i

                    def page_attend(i, acc_i, first):
                        s_tile(lambda g, i=i: kTs[i][:, g, :], 128, [b_kTs[i]], lambda g, i=i: vpg[i][:, g, 0:65], [b_vpg[i]],
                               acc_i, first)

                    def win_prep(t_):
                        i1 = pg_rr[0] % NPG
                        pg_rr[0] += 1
                        i2 = pg_rr[0] % NPG
                        pg_rr[0] += 1
                        P.dma('sp', pg[i1][:, :], cwk[t_ * 128:(t_ + 1) * 128, :], writes=[b_pg[i1]])
                        P.dma('sp', pg[i2][:, :], cwv[t_ * 128:(t_ + 1) * 128, :], writes=[b_pg[i2]])
                        return page_prep(pg[i1], b_pg[i1], pg[i2], b_pg[i2], None)
                    nxt = win_prep(0)
                    for t_ in range(4):
                        cur = nxt
                        if t_ + 1 < 4:
                            nxt = win_prep(t_ + 1)
                        page_attend(cur, 2, t_ == 0)
                    s_tile(lambda g: knew[:, 4 + g, :], SW, [b_samp], lambda g: vnew[0:SW, 1, g, 0:65], [b_samp], 2, False)
                    s_flush()
                    fin_s(2, 2, False)
                    npg = 128 if k.npg is None else k.npg
                    def slc_prep(page):
                        srcK, bK = gather(csK, page)
                        srcV, bV = gather(csV, page)
                        return page_prep(srcK, bK, srcV, bV, page)
                    nxt = slc_prep(0)
                    for page in range(npg):
                        cur = nxt
                        if page + 1 < npg:
                            nxt = slc_prep(page + 1)
                        page_attend(cur, 1, page == 0)
                    s_tile(lambda g: knew[:, g, :], SW, [b_samp], lambda g: vnew[0:SW, 0, g, 0:65], [b_samp], 1, False)
                    s_flush()
                    fin_s(1, 1, False)
                    oabs = sb(s2s, "s_oabs", [128, 256], BF16)
                    oTs = sb(s2s, "s_oTs", [128, 2, 128], BF16)
                    b_oabs, b_oTs = P.bufs('sout', 2)
                    CP(oabs[:, :], oaccs[:, :], [b_oaccs], [b_oabs])
                    pt_, bpt = nps(0, 4)
                    for c_ in range(2):
                        MM(pt_[:, c_ * 128:(c_ + 1) * 128], oabs[:, c_ * 128:(c_ + 1) * 128], ident[:, :], True, True,
                           [b_oabs, b_const], [bpt])
                    EV(oTs[:, :, :], pt_[:, 0:256].rearrange("p (c q) -> p c q", q=128), [bpt], [b_oTs])
                    for g in range(4):
                        for slot in range(4):
                            P.dma('sp', oaT_d[2 * g + slot // 2, (slot % 2) * 64:(slot % 2) * 64 + 64, 1024:1024 + SW],
                                  oTs[(g % 2) * 64:(g % 2) * 64 + 64, g // 2, slot * 32:slot * 32 + SW],
                                  reads=[b_oTs], writes=[b_oaTd])

            P.barrier()
            b_mTd = P.buf('mTd')
            if 'merge' not in k.skip:
              with ExitStack() as s3:
                alloc_w(s3, 'm', 4096, nb=4)
                oaTs = sb(s3, "m_oaTs", [128, 8, 1024 + SW], BF16)
                obTs = sb(s3, "m_obTs", [128, 8, 1024 + SW], BF16)
                sga = sb(s3, "m_sga", [128, 2, 1024 + SW], BF16)
                sgb = sb(s3, "m_sgb", [128, 2, 1024 + SW], BF16)
                t1 = sb(s3, "m_t1", [128, 2, 1024 + SW], F32)
                t2 = sb(s3, "m_t2", [128, 512], F32)
                mst = sb(s3, "m_mst", [128, 2, 1024 + SW], BF16)
                b_oaTs, b_obTs, b_sga, b_sgb, b_t1, b_t2, b_mst = P.bufs('mrg', 7)
                for f in range(8):
                    P.dma('sp', oaTs[:, f, :], oaT_d[f, :, :], reads=[b_oaTd], writes=[b_oaTs])
                    P.dma('sp', obTs[:, f, :], obT_d[f, :, :], reads=[b_obTd], writes=[b_obTs])
                own3 = [(0, 512), (512, 512), (1024, SW)]
                for L in range(8):
                    for which in range(4):
                        if which < 2:
                            wt, bw = load_w(w_br, which * 2048 + L * 256, 256)
                        else:
                            wt, bw = load_w(w_bra if which == 2 else w_brb, L * 256, 256, nkc=8)
                        for c2 in range(2):
                            for (t0, n) in own3:
                                pp, bpp = nps(0, 8)
                                nk_ = 16 if which < 2 else 8
                                for kc in range(nk_):
                                    if which < 2:
                                        rhs = hT[:, kc, 1024 + t0:1024 + t0 + n]
                                        rd = [bw] + b_hT
                                    elif which == 2:
                                        rhs = oaTs[:, kc, t0:t0 + n]
                                        rd = [bw, b_oaTs]
                                    else:
                                        rhs = obTs[:, kc, t0:t0 + n]
                                        rd = [bw, b_obTs]
                                    MM(pp[:, 0:n], wt[:, kc, c2 * 128:(c2 + 1) * 128], rhs, kc == 0, kc == nk_ - 1, rd, [bpp])
                                if which == 0:
                                    ACT(sga[:, c2, t0:t0 + n], pp[:, 0:n], AF.Sigmoid, [bpp], [b_sga])
                                elif which == 1:
                                    ACT(sgb[:, c2, t0:t0 + n], pp[:, 0:n], AF.Sigmoid, [bpp], [b_sgb])
                                elif which == 2:
                                    TT(t1[:, c2, t0:t0 + n], pp[:, 0:n], sga[:, c2, t0:t0 + n], ALU.mult, [bpp, b_sga], [b_t1])
                                else:
                                    TT(t2[:, 0:n], pp[:, 0:n], sgb[:, c2, t0:t0 + n], ALU.mult, [bpp, b_sgb], [b_t2])
                                    TT(mst[:, c2, t0:t0 + n], t2[:, 0:n], t1[:, c2, t0:t0 + n], ALU.add, [b_t2, b_t1], [b_mst])
                    for c2 in range(2):
                        P.dma('sp', mT_d[L * 2 + c2, :, :], mst[:, c2, :], reads=[b_mst], writes=[b_mTd])
        P.barrier()
        b_yout = P.buf('yout')
        if 'tail' not in k.skip:
          with ExitStack() as s4:
            x1T = sb(s4, "t_x1T", [128, 16, 1024 + SW], F32)
            b_x1T = P.bufs('x1T', 3)
            identf = sb(s4, "t_identf", [128, 128], F32)
            g2c = sb(s4, "t_g2c", [128, 16], F32)
            gfc = sb(s4, "t_gfc", [128, 16], F32)
            P.dma('sp', identf[:], identf_d[:, :], writes=[b_const])
            P.dma('sp', g2c[:], g2col[:, :], writes=[b_const])
            P.dma('sp', gfc[:], gfcol[:, :], writes=[b_const])
            own3 = [(0, 512), (512, 512), (1024, SW)]
            with ExitStack() as s4a:
                alloc_w(s4a, 'o', 4096, nb=4)
                mTs = sb(s4a, "t_mTs", [128, 16, 1024 + SW], BF16)
                b_mTs = P.buf('mTs')
                for f in range(16):
                    P.dma('sp', mTs[:, f, :], mT_d[f, :, :], reads=[b_mTd], writes=[b_mTs])
                xq = [sb(s4a, f"t_xq{i}", [128, D], F32) for i in range(2)]
                b_xq = P.bufs('xq', 2)
                for ti in range(9):
                    i = ti % 2
                    t0, m = (ti * 128, 128) if ti < 8 else (1024, SW)
                    ri = 0 if t0 < 512 else (1 if t0 < 1024 else 2)
                    if ti < 8:
                        P.dma('sp', xq[i][0:m, :], xa[1024 + t0:1024 + t0 + m, :], writes=[b_xq[i]])
                    else:
                        P.op('pool', lambda e, i=i, m=m: e.memset(xq[i][0:m, :], 0.0), writes=[b_xq[i]])
                        P.dma('sp', xq[i][0:1, :], xs[0:1, :], writes=[b_xq[i]])
                    for q4 in range(4):
                        pt_, bpt = nps(0, 8)
                        for j in range(4):
                            kc = q4 * 4 + j
                            MM(pt_[:, j * 128:j * 128 + m], xq[i][0:m, kc * 128:(kc + 1) * 128], identf[0:m, 0:m], True, True,
                               [b_xq[i], b_const], [bpt])
                        ptv = pt_[:, :].rearrange("p (j c) -> p j c", c=128)
                        EV(x1T[:, q4 * 4:(q4 + 1) * 4, t0:t0 + m], ptv[:, :, 0:m], [bpt], [b_x1T[ri]])
                for L in range(8):
                    wt, bw = load_w(w_o, L * 256, 256)
                    for c2 in range(2):
                        ch = L * 2 + c2
                        for ri, (t0, n) in enumerate(own3):
                            pp, bpp = nps(0, 8)
                            for kc in range(16):
                                MM(pp[:, 0:n], wt[:, kc, c2 * 128:(c2 + 1) * 128], mTs[:, kc, t0:t0 + n], kc == 0, kc == 15,
                                   [bw, b_mTs], [bpp])
                            TT(x1T[:, ch, t0:t0 + n], x1T[:, ch, t0:t0 + n], pp[:, 0:n], ALU.add, [bpp, b_x1T[ri]], [b_x1T[ri]])
            P.barrier()
            sqt = [sb(s4, f"t_sq{i}", [128, 512], F32) for i in range(2)]
            b_sqt = P.bufs('sqt', 2)
            rs = sb(s4, "t_rs", [128, 512], F32)
            tmpn = [sb(s4, f"t_tmpn{i}", [128, 512], F32) for i in range(2)]
            b_rs = P.buf('rs')
            b_tmpn = P.bufs('tmpn', 2)
            nrm_rr = [0]

            def fm_rstd(ri, t0, n):
                pss, bpss = nps(0, 8)
                for kc in range(16):
                    i = nrm_rr[0] % 2
                    nrm_rr[0] += 1
                    ACT(sqt[i][:, 0:n], x1T[:, kc, t0:t0 + n], AF.Square, [b_x1T[ri]], [b_sqt[i]])
                    MM(pss[:, 0:n], ones_f[:, :], sqt[i][:, 0:n], kc == 0, kc == 15, [b_const, b_sqt[i]], [bpss])
                CP(rs[:, 0:n], pss[:, 0:n], [bpss], [b_rs])
                ACT(rs[:, 0:n], rs[:, 0:n], AF.Sqrt, [b_rs, b_const], [b_rs], scale=1.0 / D, bias=epsc[:, 0:1])
                RCP(rs[:, 0:n], rs[:, 0:n], [b_rs], [b_rs])

            with ExitStack() as s4b:
                alloc_w(s4b, 'f', 4096, nb=6)
                h2T = sb(s4b, "t_h2T", [128, 16, 1024 + SW], BF16)
                hfc = [sb(s4b, f"t_hfc{i}", [128, 2, 1024 + SW], BF16) for i in range(2)]
                rl = [sb(s4b, f"t_rl{i}", [128, 512], F32) for i in range(2)]
                b_h2T = P.bufs('h2T', 3)
                b_hfc = P.bufs('hfc', 2)
                b_rl = P.bufs('rl', 2)
                rl_rr = [0]
                for ri, (t0, n) in enumerate(own3):
                    fm_rstd(ri, t0, n)
                    for kc in range(16):
                        i = kc % 2
                        TT(tmpn[i][:, 0:n], x1T[:, kc, t0:t0 + n], rs[:, 0:n], ALU.mult, [b_x1T[ri], b_rs], [b_tmpn[i]])
                        ACT(h2T[:, kc, t0:t0 + n], tmpn[i][:, 0:n], AF.Copy, [b_tmpn[i], b_const], [b_h2T[ri]],
                            scale=g2c[:, kc:kc + 1])
                nchunk = 32 if k.npass is None else k.npass
                for c in range(nchunk):
                    wu, bwu = load_w(w_up, c * 256, 256)
                    wd, bwd = load_w(w_down, 0, 2048, nkc=2, row0=c * 256)
                    hi = c % 2
                    for c2 in range(2):
                        for ri, (t0, n) in enumerate(own3):
                            pp, bpp = nps(0, 8)
                            for kc in range(16):
                                MM(pp[:, 0:n], wu[:, kc, c2 * 128:(c2 + 1) * 128], h2T[:, kc, t0:t0 + n], kc == 0, kc == 15,
                                   [bwu, b_h2T[ri]], [bpp])
                            i = rl_rr[0] % 2
                            rl_rr[0] += 1
                            ACT(rl[i][:, 0:n], pp[:, 0:n], AF.Relu, [bpp], [b_rl[i]])
                            TT(hfc[hi][:, c2, t0:t0 + n], rl[i][:, 0:n], rl[i][:, 0:n], ALU.mult, [b_rl[i]], [b_hfc[hi]])
                    for oc in range(16):
                        for ri, (t0, n) in enumerate(own3):
                            pp, bpp = nps(0, 8)
                            for kc2 in range(2):
                                MM(pp[:, 0:n], wd[:, kc2, oc * 128:(oc + 1) * 128], hfc[hi][:, kc2, t0:t0 + n], kc2 == 0, kc2 == 1,
                                   [bwd, b_hfc[hi]], [bpp])
                            TT(x1T[:, oc, t0:t0 + n], x1T[:, oc, t0:t0 + n], pp[:, 0:n], ALU.add, [bpp, b_x1T[ri]], [b_x1T[ri]])
            P.barrier()
            with ExitStack() as s4c:
                ytok = [sb(s4c, f"t_ytok{i}", [128, D], F32) for i in range(4)]
                b_ytok = P.bufs('ytok', 4)
                yk = [sb(s4c, f"t_yk{i}", [128, 512], F32) for i in range(2)]
                b_yk = P.bufs('yk', 2)
                for ri, (t0, n) in enumerate(own3):
                    fm_rstd(ri, t0, n)
                    nsub = (n + 127) // 128
                    for q4 in range(4):
                        pts = [(ps[4 + sbi], b_ps[4 + sbi]) for sbi in range(nsub)]
                        for j in range(4):
                            kc = q4 * 4 + j
                            i = kc % 2
                            TT(tmpn[i][:, 0:n], x1T[:, kc, t0:t0 + n], rs[:, 0:n], ALU.mult, [b_x1T[ri], b_rs], [b_tmpn[i]])
                            ACT(yk[i][:, 0:n], tmpn[i][:, 0:n], AF.Copy, [b_tmpn[i], b_const], [b_yk[i]], scale=gfc[:, kc:kc + 1])
                            for sbi in range(nsub):
                                m = min(128, n - sbi * 128)
                                MM(pts[sbi][0][0:m, j * 128:(j + 1) * 128], yk[i][:, sbi * 128:sbi * 128 + m], identf[:, :], True, True,
                                   [b_yk[i], b_const], [pts[sbi][1]])
                        for sbi in range(nsub):
                            m = min(128, n - sbi * 128)
                            EV(ytok[sbi][0:m, q4 * 512:(q4 + 1) * 512], pts[sbi][0][0:m, :], [pts[sbi][1]], [b_ytok[sbi]])
                    for sbi in range(nsub):
                        m = min(128, n - sbi * 128)
                        if ri < 2:
                            P.dma('sp', o_y[t0 + sbi * 128:t0 + sbi * 128 + m, :], ytok[sbi][0:m, :], reads=[b_ytok[sbi]], writes=[b_yout])
                        else:
                            P.dma('sp', o_ys[0:m, :], ytok[sbi][0:m, :], reads=[b_ytok[sbi]], writes=[b_yout])
        if dbg != 'p0':
            P.op('sp', lambda e: e.nop(), reads=[b_out_gla, b_obTd, b_oaTd, b_rows, b_mTd, b_yout])
        k.stats = P.finish(top)
        k.n_waits = P.n_waits
    return nc, k


def _prep_core(c, inp):
    b, hf = c // 2, c % 2
    m = {}
    xp = inp['x_prompt']
    own = xp[b, hf * 1024:(hf + 1) * 1024]
    pre = xp[b, 0:1024] if hf == 1 else np.zeros((1024, D), np.float32)
    m['xa'] = np.ascontiguousarray(np.concatenate([pre, own], 0))
    m['xs'] = np.ascontiguousarray(inp['x_sample'][c])
    m['sg'] = np.ascontiguousarray(inp['state_gla'][0, c])
    m['pt_rep'] = np.ascontiguousarray(np.tile(inp['page_table'][c][None, :], (128, 1)).astype(np.int32))
    m['cwk'] = np.ascontiguousarray(inp['cache_win_k'][0, c].reshape(512, 256))
    m['cwv'] = np.ascontiguousarray(inp['cache_win_v'][0, c].reshape(512, 256))
    m.update(_consts(hf))
    return m


def _prep_shared(inp):
    s = {}
    w_in = inp['w_in'][0]
    o = 1024 + 1536 + 48
    q_l = w_in[:, o:o + 512]
    k_l = w_in[:, o + 512:o + 1024]
    v_l = w_in[:, o + 1024:o + 2048]
    r_l = w_in[:, o + 2048:o + 3072]
    lr = w_in[:, o + 3072:o + 3088]
    s['w_gla_fm'] = np.ascontiguousarray(np.concatenate([q_l, k_l, lr], 1))
    s['w_gla_tm'] = np.ascontiguousarray(np.concatenate([k_l, v_l, r_l], 1))
    s['g1col'] = np.ascontiguousarray(inp['norm1_g'][0].reshape(16, 128).T)
    s['w_a2'] = np.ascontiguousarray(inp['gla_w_a2'][0])
    s['b_a'] = np.ascontiguousarray(inp['gla_b_a'][0][None, :])
    s['gnorm'] = np.ascontiguousarray(np.tile(inp['gla_norm_g'][0][None, :], (128, 1)))
    qcols = []
    for g in range(4):
        for r in range(2):
            for hd in (4 * g + r, 4 * g + r + 2):
                qcols.append(w_in[:, hd * 64:(hd + 1) * 64])
    s['w_q'] = np.ascontiguousarray(np.concatenate(qcols, 1))
    kv0 = 1024
    kcols = []
    for typ in (0, 1, 2, 4):
        for g in range(4):
            blk = w_in[:, kv0 + typ * 256 + g * 64: kv0 + typ * 256 + (g + 1) * 64]
            kcols += [blk, blk]
    s['w_kT'] = np.ascontiguousarray(np.concatenate(kcols, 1))
    s['w_kv'] = np.ascontiguousarray(w_in[:, kv0:kv0 + 1536])
    s['w_g'] = np.ascontiguousarray(w_in[:, 2560:2608])
    s['ccK'] = inp['cache_cmp_k'][0].reshape(163840, 256)
    s['ccV'] = inp['cache_cmp_v'][0].reshape(163840, 256)
    s['csK'] = inp['cache_slc_k'][0].reshape(163840, 256)
    s['csV'] = inp['cache_slc_v'][0].reshape(163840, 256)
    s['w_br'] = np.ascontiguousarray(w_in[:, 5696:9792])
    s['w_bra'] = np.ascontiguousarray(inp['w_br_a'][0])
    s['w_brb'] = np.ascontiguousarray(inp['w_br_b'][0])
    s['w_o'] = np.ascontiguousarray(inp['w_o'][0])
    s['w_up'] = np.ascontiguousarray(inp['w_up'][0])
    s['w_down'] = np.ascontiguousarray(inp['w_down'][0])
    s['g2col'] = np.ascontiguousarray(inp['norm2_g'][0].reshape(16, 128).T)
    s['gfcol'] = np.ascontiguousarray(inp['norm_f'].reshape(16, 128).T)
    s['bgate'] = np.ascontiguousarray(inp['b_nsa_gate'][0][None, :])
    for nm, key in (('w1k', 'cmp_k_w1'), ('w1v', 'cmp_v_w1')):
        w1 = inp[key][0].reshape(32, 64, 128).transpose(1, 0, 2)
        s[nm] = np.ascontiguousarray(np.concatenate([w1, w1], 0).reshape(128, 4096))
    w2k = inp['cmp_k_w2'][0]
    s['w2kd'] = np.ascontiguousarray(np.concatenate([w2k, w2k], 1))
    s['w2v'] = np.ascontiguousarray(inp['cmp_v_w2'][0])
    for nm, key in (('pek', 'cmp_pe_k'), ('pev', 'cmp_pe_v')):
        pe = inp[key][0].T
        pe = np.concatenate([pe, pe], 0)
        s[nm] = np.ascontiguousarray(np.repeat(pe[:, :, None], 32, axis=2).reshape(128, 1024))
    return s


_CACHE = {}


def kernel(**inp):
    inp = {k_: np.asarray(v) for k_, v in inp.items()}
    if 'nc' not in _CACHE:
        _CACHE['nc'] = build()
    nc, kk = _CACHE['nc']
    shared = _prep_shared(inp)
    in_maps = []
    for c in range(8):
        m = dict(shared)
        m.update(_prep_core(c, inp))
        in_maps.append(m)
    res = run_bass_kernel_spmd(nc, in_maps, core_ids=list(range(8)))
    R = res.results
    p_gla = np.stack([R[2 * b + 1]['o_pgla'] for b in range(4)])[None].astype(np.float32)
    s_gla = np.stack([R[c]['o_sgla'] for c in range(8)])[None].astype(np.float32)
    y_prompt = np.zeros((4, 2048, D), np.float32)
    y_sample = np.zeros((8, 1, D), np.float32)
    prow = [np.zeros((1, 4, 2048, 4, 64), np.float32) for _ in range(4)]
    pwin = [np.zeros((1, 4, 512, 4, 64), np.float32) for _ in range(2)]
    srow = [np.zeros((1, 8, 1, 4, 64), np.float32) for _ in range(4)]
    swin = [np.zeros((1, 8, 512, 4, 64), np.float32) for _ in range(2)]
    for c in range(8):
        b, hf = c // 2, c % 2
        y_prompt[b, hf * 1024:(hf + 1) * 1024] = R[c]['o_y']
        y_sample[c, 0] = R[c]['o_ys'][0]
        rows = R[c]['o_rows']
        for L in range(4):
            prow[L][0, b, hf * 1024:(hf + 1) * 1024] = rows[:, L * 256:(L + 1) * 256].reshape(1024, 4, 64)
            srow[L][0, c, 0] = R[c]['o_rows_s'][0, L * 256:(L + 1) * 256].reshape(4, 64)
        if hf == 1:
            for L in range(2):
                pwin[L][0, b] = rows[512:1024, (4 + L) * 256:(5 + L) * 256].reshape(512, 4, 64)
        swin[0][0, c] = R[c]['o_swk'].reshape(512, 4, 64)
        swin[1][0, c] = R[c]['o_swv'].reshape(512, 4, 64)
    return (y_prompt, y_sample, prow[0], prow[1], prow[2], prow[3], pwin[0], pwin[1], p_gla,
            srow[0], srow[1], srow[2], srow[3], swin[0], swin[1], s_gla)
```
